# Optimizing a Trainium2 kernel written in Bass

```python
import math
import jax, jax.numpy as jnp
from jax import lax
import numpy as np

D_MODEL = 1024
BATCH = 8
SEQ = 2048
DEPTH = 2

N_BRANCH = 3
BRANCH_WIDTH = 512
ATTN_Q_HEADS = 8
ATTN_KV_HEADS = 2
ATTN_Q_PER_KV = ATTN_Q_HEADS // ATTN_KV_HEADS
ATTN_HEAD_DIM = 64
ATTN_WINDOW = 128
ATTN_BLOCK = 128
ROPE_DIM = ATTN_HEAD_DIM // 4
ROPE_THETA = 500000.0
DN_HEADS = 8
DN_KEY_DIM = 64
DN_VALUE_DIM = 64
DN_CONV = 4
DN_CHUNK = 64
DN_QKV_WIDTH = DN_HEADS * (2 * DN_KEY_DIM + DN_VALUE_DIM)
RET_HEADS = 4
RET_KEY_DIM = 64
RET_VALUE_DIM = 128
RET_CHUNK = 64
RET_THETA = 10000.0
D_FF = 2816
N_MOD = 9
EPS = 1e-6
NEG_INF = -1e30

IN_SIZES = (
    ATTN_Q_HEADS * ATTN_HEAD_DIM,
    ATTN_KV_HEADS * ATTN_HEAD_DIM,
    ATTN_KV_HEADS * ATTN_HEAD_DIM,
    DN_QKV_WIDTH,
    DN_HEADS,
    DN_HEADS,
    DN_HEADS * DN_VALUE_DIM,
    RET_HEADS * RET_KEY_DIM,
    RET_HEADS * RET_KEY_DIM,
    RET_HEADS * RET_VALUE_DIM,
    RET_HEADS * RET_VALUE_DIM,
    N_BRANCH * D_MODEL,
)
N_IN = 768 + 2064 + 1536 + 3 * D_MODEL

kernel_name = 'hybrid_gated_parallel_mixer'


def _split_columns(t, sizes):
    out, start = [], 0
    for n in sizes:
        out.append(t[..., start:start + n])
        start += n
    return out


def rms_norm(x):
    xf = x.astype(jnp.float32)
    y = xf * lax.rsqrt(jnp.mean(xf * xf, axis=-1, keepdims=True) + EPS)
    return y.astype(x.dtype)


def l2_norm(x):
    xf = x.astype(jnp.float32)
    return (xf * lax.rsqrt(jnp.sum(xf * xf, axis=-1, keepdims=True) + EPS)).astype(x.dtype)


def modulate(x, gain, shift, scale):
    return rms_norm(x) * gain * (1.0 + scale) + shift


def swiglu_ffn(u, w13, w2):
    gate, up = jnp.split(u @ w13, 2, axis=-1)
    return (jax.nn.silu(gate) * up) @ w2


def partial_rope(x, pos):
    half = ROPE_DIM // 2
    inv_freq = ROPE_THETA ** (-jnp.arange(0, ROPE_DIM, 2, dtype=jnp.float32) / ROPE_DIM)
    phase = pos[:, None] * inv_freq[None, :]
    cos = jnp.cos(phase)[None, :, None, :].astype(x.dtype)
    sin = jnp.sin(phase)[None, :, None, :].astype(x.dtype)
    x1, x2, rest = x[..., :half], x[..., half:ROPE_DIM], x[..., ROPE_DIM:]
    return jnp.concatenate([x1 * cos - x2 * sin, x2 * cos + x1 * sin, rest], axis=-1)


def retnet_rotate(x, pos):
    dk = x.shape[-1]
    angle = 1.0 / (RET_THETA ** jnp.linspace(0.0, 1.0, dk // 2, dtype=jnp.float32))
    angle = jnp.repeat(angle, 2)
    phase = pos[:, None] * angle[None, :]
    cos = jnp.cos(phase)[None, :, None, :].astype(x.dtype)
    sin = jnp.sin(phase)[None, :, None, :].astype(x.dtype)
    rot = jnp.stack([-x[..., 1::2], x[..., 0::2]], axis=-1).reshape(x.shape)
    return x * cos + rot * sin


def causal_depthwise_conv(x, w):
    k = w.shape[0]
    return lax.conv_general_dilated(
        x, w[:, None, :].astype(x.dtype), window_strides=(1,), padding=[(k - 1, 0)],
        dimension_numbers=('NWC', 'WIO', 'NWC'), feature_group_count=x.shape[-1])


def sliding_window_attention(q, k, v, sinks):
    b, s = q.shape[0], q.shape[1]
    nb = s // ATTN_BLOCK
    qb = q.reshape(b, nb, ATTN_BLOCK, ATTN_KV_HEADS, ATTN_Q_PER_KV, ATTN_HEAD_DIM)

    def band(t):
        tb = t.reshape(b, nb, ATTN_BLOCK, ATTN_KV_HEADS, ATTN_HEAD_DIM)
        prev = jnp.pad(tb, ((0, 0), (1, 0), (0, 0), (0, 0), (0, 0)))[:, :-1]
        return jnp.concatenate([prev, tb], axis=2)

    kb, vb = band(k), band(v)
    logits = jnp.einsum('bnqgrd,bnkgd->bngrqk', qb, kb).astype(jnp.float32) * (ATTN_HEAD_DIM ** -0.5)
    qi = jnp.arange(ATTN_BLOCK)[:, None] + ATTN_BLOCK
    kj = jnp.arange(2 * ATTN_BLOCK)[None, :]
    in_window = (kj <= qi) & (kj > qi - ATTN_WINDOW)
    key_abs = jnp.arange(nb)[:, None, None] * ATTN_BLOCK + kj[None] - ATTN_BLOCK
    valid = in_window[None] & (key_abs >= 0)
    logits = jnp.where(valid[None, :, None, None], logits, NEG_INF)
    sink = sinks.astype(jnp.float32).reshape(ATTN_KV_HEADS, ATTN_Q_PER_KV)[None, None, :, :, None, None]
    sink = jnp.broadcast_to(sink, logits.shape[:-1] + (1,))
    probs = jax.nn.softmax(jnp.concatenate([logits, sink], axis=-1), axis=-1)[..., :-1]
    out = jnp.einsum('bngrqk,bnkgd->bnqgrd', probs.astype(v.dtype), vb)
    return out.reshape(b, s, ATTN_Q_HEADS * ATTN_HEAD_DIM)


def gated_delta_rule_chunked(q, k, v, log_decay, beta):
    b, s, h, dk = q.shape
    dv = v.shape[-1]
    c = DN_CHUNK
    nc = s // c
    f32 = jnp.float32

    def to_chunks(t):
        return t.astype(f32).reshape(b, nc, c, h, -1).transpose(0, 3, 1, 2, 4)

    q = to_chunks(q) * (dk ** -0.5)
    k = to_chunks(k)
    v = to_chunks(v)
    g = jnp.cumsum(to_chunks(log_decay[..., None])[..., 0], axis=-1)
    beta = to_chunks(beta[..., None])
    causal = jnp.tril(jnp.ones((c, c), dtype=bool))
    strict = jnp.tril(jnp.ones((c, c), dtype=bool), k=-1)
    diff = g[..., :, None] - g[..., None, :]
    decay = jnp.where(causal, jnp.exp(jnp.where(causal, diff, 0.0)), 0.0)
    k_beta = k * beta
    v_beta = v * beta
    lower = jnp.where(strict, jnp.einsum('bhnid,bhnjd->bhnij', k_beta, k) * decay, 0.0)
    unit_lower = lower + jnp.eye(c, dtype=f32)
    rhs = jnp.concatenate([v_beta, k_beta * jnp.exp(g)[..., None]], axis=-1)
    sol = lax.linalg.triangular_solve(unit_lower, rhs, left_side=True, lower=True, unit_diagonal=True)
    u_c, w_c = sol[..., :dv], sol[..., dv:]
    intra = jnp.where(causal, jnp.einsum('bhnid,bhnjd->bhnij', q, k) * decay, 0.0)
    q_decayed = q * jnp.exp(g)[..., None]
    k_to_end = k * jnp.exp(g[..., -1:] - g)[..., None]
    chunk_decay = jnp.exp(g[..., -1])

    def step(state, xs):
        u_i, w_i, intra_i, qd_i, kt_i, dec_i = xs
        v_new = u_i - jnp.einsum('bhik,bhkv->bhiv', w_i, state)
        out = jnp.einsum('bhik,bhkv->bhiv', qd_i, state) + jnp.einsum('bhij,bhjv->bhiv', intra_i, v_new)
        state = state * dec_i[..., None, None] + jnp.einsum('bhik,bhiv->bhkv', kt_i, v_new)
        return state, out

    xs = tuple(jnp.moveaxis(t, 2, 0) for t in (u_c, w_c, intra, q_decayed, k_to_end, chunk_decay))
    _, out = lax.scan(step, jnp.zeros((b, h, dk, dv), f32), xs)
    return out.transpose(1, 0, 3, 2, 4).reshape(b, s, h, dv)


def retention_chunked(q, k, v):
    b, s, h, dk = q.shape
    dv = v.shape[-1]
    c = RET_CHUNK
    nc = s // c
    f32 = jnp.float32

    def to_chunks(t):
        return t.astype(f32).reshape(b, nc, c, h, -1).transpose(0, 3, 1, 2, 4)

    q, k, v = to_chunks(q), to_chunks(k), to_chunks(v)
    log_gamma = jnp.log1p(-jnp.exp2(-5.0 - jnp.arange(h, dtype=f32)))
    pos = jnp.arange(c, dtype=f32)
    causal = jnp.tril(jnp.ones((c, c), dtype=bool))
    diff = jnp.where(causal, pos[:, None] - pos[None, :], 0.0)
    decay = jnp.where(causal, jnp.exp(log_gamma[:, None, None] * diff), 0.0)
    scores = jnp.einsum('bhnid,bhnjd->bhnij', q, k) * decay[None, :, None]
    intra_out = jnp.einsum('bhnij,bhnjv->bhniv', scores, v)
    q_decayed = q * jnp.exp(log_gamma[:, None] * (pos + 1.0))[None, :, None, :, None]
    k_to_end = k * jnp.exp(log_gamma[:, None] * (c - 1.0 - pos))[None, :, None, :, None]
    chunk_kv = jnp.einsum('bhnjd,bhnjv->bhndv', k_to_end, v)
    chunk_decay = jnp.exp(log_gamma * c)[None, :, None, None]

    def step(state, xs):
        qd_i, kv_i = xs
        out = jnp.einsum('bhik,bhkv->bhiv', qd_i, state)
        return state * chunk_decay + kv_i, out

    _, inter = lax.scan(step, jnp.zeros((b, h, dk, dv), f32),
                        (jnp.moveaxis(q_decayed, 2, 0), jnp.moveaxis(chunk_kv, 2, 0)))
    out = intra_out + jnp.moveaxis(inter, 0, 2)
    return out.transpose(0, 2, 3, 1, 4).reshape(b, s, h, dv)


def hybrid_mixer(u, w_in, attn_q_norm, attn_k_norm, attn_sinks, dn_conv, dn_a_log, dn_dt_bias,
                 dn_out_norm, w_branch, w_out):
    b, s, _ = u.shape
    f32 = jnp.float32
    proj = u @ w_in
    (a_q, a_k, a_v, d_qkv, d_a, d_b, d_z, r_q, r_k, r_v, r_g, gate_logits) = _split_columns(proj, IN_SIZES)
    pos = jnp.arange(s, dtype=f32)

    q = a_q.reshape(b, s, ATTN_Q_HEADS, ATTN_HEAD_DIM)
    k = a_k.reshape(b, s, ATTN_KV_HEADS, ATTN_HEAD_DIM)
    v = a_v.reshape(b, s, ATTN_KV_HEADS, ATTN_HEAD_DIM)
    q = partial_rope(rms_norm(q) * attn_q_norm, pos)
    k = partial_rope(rms_norm(k) * attn_k_norm, pos)
    out_a = sliding_window_attention(q, k, v, attn_sinks)

    qkv = jax.nn.silu(causal_depthwise_conv(d_qkv, dn_conv))
    dq, dk_, dv_ = _split_columns(qkv, (DN_HEADS * DN_KEY_DIM, DN_HEADS * DN_KEY_DIM, DN_HEADS * DN_VALUE_DIM))
    dq = l2_norm(dq.reshape(b, s, DN_HEADS, DN_KEY_DIM))
    dk_ = l2_norm(dk_.reshape(b, s, DN_HEADS, DN_KEY_DIM))
    dv_ = dv_.reshape(b, s, DN_HEADS, DN_VALUE_DIM)
    log_decay = -jnp.exp(dn_a_log.astype(f32)) * jax.nn.softplus(d_a.astype(f32) + dn_dt_bias.astype(f32))
    beta = jax.nn.sigmoid(d_b.astype(f32))
    o_b = gated_delta_rule_chunked(dq, dk_, dv_, log_decay, beta).astype(u.dtype)
    o_b = rms_norm(o_b) * dn_out_norm * jax.nn.silu(d_z.reshape(b, s, DN_HEADS, DN_VALUE_DIM))
    out_b = o_b.reshape(b, s, DN_HEADS * DN_VALUE_DIM)

    rq = retnet_rotate(r_q.reshape(b, s, RET_HEADS, RET_KEY_DIM), pos)
    rk = retnet_rotate(r_k.reshape(b, s, RET_HEADS, RET_KEY_DIM), pos) * (RET_KEY_DIM ** -0.5)
    rv = r_v.reshape(b, s, RET_HEADS, RET_VALUE_DIM)
    o_c = rms_norm(retention_chunked(rq, rk, rv).astype(u.dtype))
    out_c = o_c.reshape(b, s, RET_HEADS * RET_VALUE_DIM) * jax.nn.silu(r_g)

    branches = jnp.stack([out_a, out_b, out_c], axis=2)
    per_branch = jnp.einsum('bsgi,gid->bsgd', branches, w_branch)
    gates = jax.nn.sigmoid(gate_logits.reshape(b, s, N_BRANCH, D_MODEL))
    merged = jnp.sum(gates * per_branch, axis=2)
    return merged @ w_out


def setup_inputs(seed: int = 0) -> dict:
    key = jax.random.key(seed)
    ks = jax.random.split(key, 24)
    f32 = jnp.float32
    L, D = DEPTH, D_MODEL

    def normal(k, shape, scale):
        return jax.random.normal(k, shape, f32) * scale

    def gain(k, shape):
        return 1.0 + 0.1 * jax.random.normal(k, shape, f32)

    dt = jnp.exp(jax.random.uniform(ks[13], (L, DN_HEADS), f32, minval=math.log(1e-3), maxval=math.log(1e-1)))
    return {
        'x': normal(ks[0], (BATCH, SEQ, D), 1.0),
        'c': normal(ks[1], (BATCH, D), 1.0),
        'w_mod': normal(ks[2], (L, D, N_MOD * D), 0.5 * D ** -0.5),
        'b_mod': normal(ks[3], (L, N_MOD * D), 0.02),
        'ffn1_norm': gain(ks[4], (L, D)),
        'ffn1_w13': normal(ks[5], (L, D, 2 * D_FF), D ** -0.5),
        'ffn1_w2': normal(ks[6], (L, D_FF, D), D_FF ** -0.5),
        'mix_norm': gain(ks[7], (L, D)),
        'w_in': normal(ks[8], (L, D, N_IN), D ** -0.5),
        'attn_q_norm': gain(ks[9], (L, ATTN_HEAD_DIM)),
        'attn_k_norm': gain(ks[10], (L, ATTN_HEAD_DIM)),
        'attn_sinks': normal(ks[11], (L, ATTN_Q_HEADS), 1.0),
        'dn_conv': normal(ks[12], (L, DN_CONV, DN_QKV_WIDTH), DN_CONV ** -0.5),
        'dn_a_log': jnp.log(jax.random.uniform(ks[14], (L, DN_HEADS), f32, minval=1.0, maxval=16.0)),
        'dn_dt_bias': jnp.log(jnp.expm1(dt)),
        'dn_out_norm': gain(ks[15], (L, DN_VALUE_DIM)),
        'w_branch': normal(ks[16], (L, N_BRANCH, BRANCH_WIDTH, D), BRANCH_WIDTH ** -0.5),
        'w_out': normal(ks[17], (L, D, D), D ** -0.5),
        'ffn2_norm': gain(ks[18], (L, D)),
        'ffn2_w13': normal(ks[19], (L, D, 2 * D_FF), D ** -0.5),
        'ffn2_w2': normal(ks[20], (L, D_FF, D), D_FF ** -0.5),
    }


def reference(x, c, w_mod, b_mod, ffn1_norm, ffn1_w13, ffn1_w2, mix_norm, w_in, attn_q_norm,
              attn_k_norm, attn_sinks, dn_conv, dn_a_log, dn_dt_bias, dn_out_norm, w_branch, w_out,
              ffn2_norm, ffn2_w13, ffn2_w2):
    b = x.shape[0]
    cond = jax.nn.silu(c)
    for layer in range(DEPTH):
        mod = (cond @ w_mod[layer] + b_mod[layer]).reshape(b, N_MOD, 1, D_MODEL)
        u1 = modulate(x, ffn1_norm[layer], mod[:, 0], mod[:, 1])
        x = x + 0.5 * mod[:, 2] * swiglu_ffn(u1, ffn1_w13[layer], ffn1_w2[layer])
        u2 = modulate(x, mix_norm[layer], mod[:, 3], mod[:, 4])
        x = x + mod[:, 5] * hybrid_mixer(u2, w_in[layer], attn_q_norm[layer], attn_k_norm[layer],
                                         attn_sinks[layer], dn_conv[layer], dn_a_log[layer],
                                         dn_dt_bias[layer], dn_out_norm[layer], w_branch[layer],
                                         w_out[layer])
        u3 = modulate(x, ffn2_norm[layer], mod[:, 6], mod[:, 7])
        x = x + 0.5 * mod[:, 8] * swiglu_ffn(u3, ffn2_w13[layer], ffn2_w2[layer])
    return x
```

```python
import numpy as np
from contextlib import ExitStack
import concourse.bass as bass
import concourse.mybir as mybir
from concourse.bass_utils import run_bass_kernel_spmd

F32 = mybir.dt.float32
BF16 = mybir.dt.bfloat16
AF = mybir.ActivationFunctionType
ALU = mybir.AluOpType
AX = mybir.AxisListType

S = 2048
D = 1024
L = 2
DFF = 2816
NFC = 22
KC = 8
SLAB = 2048
NSLOT = 12
CUT = 9
TR_FULL = True
TKEY_RELAX = True
HORD = [0, 2, 4, 6, 1, 3, 5, 7]
EPS = 1e-6
FGROUPS = [(0, 8), (8, 16), (16, 22)]

C_AQ, C_AK, C_AV, C_DQKV, C_DA, C_DB, C_DZ, C_RQ, C_RK, C_RV, C_RG, C_GL = (
    0, 512, 640, 768, 2304, 2312, 2320, 2832, 3088, 3344, 3856, 4368)


class Tok:
    __slots__ = ("w", "r", "name", "pf")

    def __init__(self, name=""):
        self.w = None
        self.r = {}
        self.name = name
        self.pf = False


class DmaSem:
    def __init__(self, sem, key):
        self.sem = sem
        self.key = key
        self.count = 0


class Sched:
    ENG = ("pe", "act", "dve", "pool", "sp")

    def __init__(self, nc, es):
        self.nc = nc
        self.eng = {"pe": nc.tensor, "act": nc.scalar, "dve": nc.vector, "pool": nc.gpsimd, "sp": nc.sync}
        self.sem = {e: es.enter_context(nc.semaphore("sem_" + e)) for e in self.ENG}
        self.prog = {e: [] for e in self.ENG}
        self.cnt = {e: 0 for e in self.ENG}
        self.seen = {e: {} for e in self.ENG}
        self.needed = {e: set() for e in self.ENG}
        self.dsems = {}
        self.es = es
        self.pe_cls = None

    def dma_sem(self, name):
        s = DmaSem(self.es.enter_context(self.nc.semaphore("dsem_" + name)), "dma_" + name)
        self.dsems[s.key] = s
        return s

    def _collect(self, eng, reads, writes, pe_full=False):
        waits = {}

        def need(k, s):
            if waits.get(k, -1) < s:
                waits[k] = s
        for t in reads:
            if t.w is not None:
                need(*t.w)
        for t in writes:
            if t.w is not None and not (eng == "pe" and t.w[0] == "pe" and pe_full and t.pf == pe_full):
                need(*t.w)
            for k, s in t.r.items():
                need(k, s)
        final = []
        for k, s in waits.items():
            if self.seen[eng].get(k, -1) < s:
                self.seen[eng][k] = s
                final.append((k, s))
                if k in self.needed:
                    self.needed[k].add(s)
        return final

    def op(self, eng, fn, reads=(), writes=(), pe_full=False, pe_cls=None):
        final = self._collect(eng, reads, writes, pe_full)
        if eng == "pe":
            if self.pe_cls is not None and pe_cls != self.pe_cls and self.cnt["pe"] > 0:
                s_ = self.cnt["pe"] - 1
                if self.seen["pe"].get("pe", -1) < s_:
                    self.seen["pe"]["pe"] = s_
                    final.append(("pe", s_))
                    self.needed["pe"].add(s_)
            self.pe_cls = pe_cls
        seq = self.cnt[eng]
        self.cnt[eng] += 1
        self.prog[eng].append((final, fn, seq))
        for t in reads:
            t.r[eng] = seq
        for t in writes:
            t.w = (eng, seq)
            t.r = {}
            t.pf = pe_full
        return seq

    def dma(self, q, dsem, out, in_, reads=(), writes=(), **kw):
        final = self._collect(q, reads, writes)
        dsem.count += 1
        val = dsem.count

        def fn(e, out=out, in_=in_, dsem=dsem, kw=kw):
            return e.dma_start(out=out, in_=in_, **kw).then_inc(dsem.sem, 16)
        self.prog[q].append((final, fn, None))
        for t in reads:
            t.r[dsem.key] = val
        for t in writes:
            t.w = (dsem.key, val)
            t.r = {}

    def fence(self, eng, toks):
        self.prog[eng].append((self._fence_filter(eng, toks), None, None))

    def _fence_filter(self, eng, toks):
        waits = {}
        for t in toks:
            if t.w is not None:
                waits[t.w[0]] = max(waits.get(t.w[0], -1), t.w[1])
            for k, s_ in t.r.items():
                waits[k] = max(waits.get(k, -1), s_)
        final = []
        for k, s_ in waits.items():
            if self.seen[eng].get(k, -1) < s_:
                self.seen[eng][k] = s_
                final.append((k, s_))
                if k in self.needed:
                    self.needed[k].add(s_)
        return final

    def barrier(self):
        for e in self.ENG:
            final = []
            for k in self.ENG:
                if self.cnt[k] > 0:
                    s_ = self.cnt[k] - 1
                    if self.seen[e].get(k, -1) < s_:
                        self.seen[e][k] = s_
                        final.append((k, s_))
                        self.needed[k].add(s_)
            for k, d in self.dsems.items():
                if d.count > 0 and self.seen[e].get(k, -1) < d.count:
                    self.seen[e][k] = d.count
                    final.append((k, d.count))
            self.prog[e].append((final, None, None))

    def wait_all(self, eng, toks):
        final = self._collect(eng, toks, ())
        self.prog[eng].append((final, None, None))

    def replay(self, block):
        rank = {e: {s: i + 1 for i, s in enumerate(sorted(self.needed[e]))} for e in self.ENG}

        def semval(k, s):
            if k in rank:
                return self.sem[k], rank[k][s]
            return self.dsems[k].sem, 16 * s

        def make(e_name):
            def body(e):
                for waits, fn, seq in self.prog[e_name]:
                    for k, s in waits:
                        sm, v = semval(k, s)
                        e.wait_ge(sm, v)
                    if fn is None:
                        continue
                    ins = fn(e)
                    if seq is not None and seq in rank[e_name]:
                        ins.then_inc(self.sem[e_name], 1)
            return body
        block.tensor(make("pe"))
        block.scalar(make("act"))
        block.vector(make("dve"))
        block.gpsimd(make("pool"))
        block.sync(make("sp"))


def _kslab(Wcols):
    n = Wcols.shape[1]
    out = np.zeros((128, KC, 256), np.float32)
    out[:, :, :n] = Wcols.reshape(KC, 128, n).transpose(1, 0, 2)
    return out.reshape(128, SLAB)


def _ffn_slabs(w13, w2):
    slabs = []
    for (f0, f1) in FGROUPS:
        for c in range(f0, f1):
            cols = np.concatenate([w13[:, c * 128:(c + 1) * 128], w13[:, DFF + c * 128:DFF + (c + 1) * 128]], axis=1)
            slabs.append(_kslab(cols))
        for c in range(f0, f1, 2):
            slabs.append(w2[c * 128:(c + 2) * 128, :].reshape(2, 128, D).transpose(1, 0, 2).reshape(128, SLAB))
    return slabs


def pack_layer(inp, l, stages):
    slabs = []
    wm = inp["w_mod"][l]
    for j in range(36):
        slabs.append(_kslab(wm[:, j * 256:(j + 1) * 256]))
    slabs += _ffn_slabs(inp["ffn1_w13"][l], inp["ffn1_w2"][l])
    if stages >= 2:
        win = inp["w_in"][l]
        qperm = np.concatenate([np.arange(h * 64, (h + 1) * 64) for h in (0, 4, 1, 5, 2, 6, 3, 7)])
        winA = np.concatenate([win[:, qperm], win[:, 512:768]], axis=1)
        for j in range(3):
            slabs.append(_kslab(winA[:, j * 256:(j + 1) * 256]))
        for j in range(6):
            slabs.append(_kslab(win[:, C_RQ + j * 256:C_RQ + (j + 1) * 256]))
        for j in range(6):
            slabs.append(_kslab(win[:, C_DQKV + j * 256:C_DQKV + (j + 1) * 256]))
        for j in range(2):
            slabs.append(_kslab(win[:, C_DZ + j * 256:C_DZ + (j + 1) * 256]))
        slabs.append(_kslab(win[:, C_DA:C_DA + 16]))
        wb = inp["w_branch"][l]
        for m in range(8):
            for g in range(3):
                sl = np.zeros((128, SLAB), np.float32)
                gc = win[:, C_GL + g * D + m * 128:C_GL + g * D + (m + 1) * 128]
                sl[:, 0:1024] = gc.reshape(KC, 128, 128).transpose(1, 0, 2).reshape(128, 1024)
                bc = wb[g][:, m * 128:(m + 1) * 128]
                sl[:, 1024:1536] = bc.reshape(4, 128, 128).transpose(1, 0, 2).reshape(128, 512)
                slabs.append(sl)
        wo = inp["w_out"][l]
        for m in range(0, 8, 2):
            sl = np.zeros((128, SLAB), np.float32)
            for i in range(2):
                oc = wo[:, (m + i) * 128:(m + i + 1) * 128]
                sl[:, i * 1024:(i + 1) * 1024] = oc.reshape(KC, 128, 128).transpose(1, 0, 2).reshape(128, 1024)
            slabs.append(sl)
    if stages >= 3:
        slabs += _ffn_slabs(inp["ffn2_w13"][l], inp["ffn2_w2"][l])
    return slabs


class Prog:
    def __init__(self, nc, es, n_slabs, stages, nlayers, dbg=False):
        self.nc = nc
        self.es = es
        self.stages = stages
        self.nlayers = nlayers
        sc = self.sc = Sched(nc, es)
        dt = nc.dram_tensor
        self.x_d = dt("x", [S, D], F32, kind="ExternalInput").ap()
        self.w_d = dt("wst", [n_slabs, 128, SLAB], F32, kind="ExternalInput").ap()
        self.cT_d = dt("cT", [128, KC], F32, kind="ExternalInput").ap()
        self.bmod_d = dt("bmod", [128, L * 72], F32, kind="ExternalInput").ap()
        self.gains_d = dt("gains", [128, L * 3 * KC], F32, kind="ExternalInput").ap()
        self.cst_d = dt("cst", [128, CST_N], F32, kind="ExternalInput").ap()
        self.y_d = dt("y", [S, D], F32, kind="ExternalOutput").ap()
        self.mcf_d = dt("mcf", [128, MCF_N], F32, kind="ExternalInput").ap()
        self.mcb_d = dt("mcb", [128, MCB_N], F32, kind="ExternalInput").ap()
        self.vec_d = dt("vec", [128, L * VEC_N], F32, kind="ExternalInput").ap()
        self.cw_d = dt("cw", [128, L * 4 * 1536], F32, kind="ExternalInput").ap()
        self.xsp_d = dt("xsp", [128, KC * S], F32).ap()
        self.osp_d = dt("osp", [12, 128, S], BF16, kind=("ExternalOutput" if dbg else "Internal")).ap()
        self.dbg = dbg

        def sb(name, shape, dtype):
            return es.enter_context(nc.sbuf_tensor(name, shape, dtype))

        def ps(name):
            return es.enter_context(nc.psum_tensor(name, [128, 512], F32))
        self.R0 = sb("R0", [128, KC * S], F32)
        self.R1 = sb("R1", [128, KC * S], BF16)
        self.R2 = sb("R2", [128, KC * S], BF16)
        self.ring = sb("ring", [128, NSLOT * SLAB], BF16)
        self.R3 = sb("R3", [128, 2048], F32)
        self.cst = sb("cstsb", [128, CST_N], F32)
        self.cstb = sb("cstb", [128, CSTB_N], BF16)
        self.condT = sb("condT", [128, KC], BF16)
        self.condf = sb("condf", [128, KC], F32)
        self.modrow = sb("modrow", [1, 2 * 512], F32)
        self.bmodT = sb("bmodT", [128, L * 72], F32)
        self.modT = sb("modT", [128, L * 72], F32)
        self.gains = sb("gains_sb", [128, L * 3 * KC], F32)
        self.coef = sb("coef", [128, L * 3 * 3 * KC], F32)
        self.rstd = sb("rstd", [128, 2 * 512], F32)
        self.tmpA = sb("tmpA", [128, 4 * 512], F32)
        self.sqb = sb("sqb", [128, 4 * 512], BF16)
        self.P = [ps("ps%d" % i) for i in range(8)]
        self.Pt = [Tok("ps%d" % i) for i in range(8)]
        self.xT = self.R0[:].rearrange("p (c t) -> p c t", c=KC)
        self.uT = self.R1[:].rearrange("p (c t) -> p c t", c=KC)
        self.actT = self.R2[:].rearrange("p (c t) -> p c t", c=KC)
        self.xtok = [[Tok() for _ in range(16)] for _ in range(KC)]
        self.utok = [[Tok() for _ in range(4)] for _ in range(KC)]
        self.atok = [[Tok() for _ in range(4)] for _ in range(KC)]
        self.t_cst = Tok("cst")
        self.t_cstb = Tok("cstb")
        self.t_mod = Tok("mod")
        self.t_modrow = Tok("modrow")
        self.t_mr = [Tok(), Tok()]
        self.t_coef = Tok("coef")
        self.t_rstd = [Tok(), Tok()]
        self.t_tmpA = [Tok() for _ in range(4)]
        self.t_sqb = [Tok() for _ in range(4)]
        self.t_misc = Tok("misc")
        self.csem = sc.dma_sem("c")
        self.xsem = [sc.dma_sem("x0"), sc.dma_sem("x1")]
        self.osem = [sc.dma_sem("o%d" % i) for i in range(4)]
        self.spsem = sc.dma_sem("sp")
        self.mcsem = sc.dma_sem("mc")
        self.mcbsem = sc.dma_sem("mcb")
        self.cwsem = sc.dma_sem("cw")
        self.olsem = sc.dma_sem("ol")
        self.otsem = [sc.dma_sem("ot0"), sc.dma_sem("ot1")]
        self.rlsem = sc.dma_sem("rl")
        self.n_slabs = n_slabs
        self.w_issue = 0
        self.w_use = 0
        self.w_released = 0
        self.slot_tok = [Tok("slot%d" % i) for i in range(NSLOT)]
        self.slot_sem = [sc.dma_sem("w%d" % i) for i in range(NSLOT)]
        self.rr = 0

    def slot_ap(self, i):
        s = i % NSLOT
        return self.ring[:, s * SLAB:(s + 1) * SLAB]

    def w_pump(self):
        while self.w_issue < self.n_slabs and self.w_issue < self.w_released + NSLOT:
            i = self.w_issue
            s = i % NSLOT
            self.sc.dma("pool", self.slot_sem[s], self.slot_ap(i), self.w_d[i], writes=[self.slot_tok[s]])
            self.w_issue += 1

    def w_take(self, n):
        out = []
        for _ in range(n):
            i = self.w_use
            assert i < self.w_issue, "weight stream underflow (ring too small for resident set)"
            out.append((self.slot_ap(i), self.slot_tok[i % NSLOT]))
            self.w_use += 1
        return out

    def w_release(self, n):
        self.w_released += n
        self.w_pump()

    def mm(self, out, lhsT, rhs, start, stop, reads, writes, full=False, tkey=None):
        self.sc.op("pe", lambda e: e.matmul(out, lhsT=lhsT, rhs=rhs, start=start, stop=stop), reads, writes,
                   pe_full=(True if full else (tkey if (tkey is not None and TKEY_RELAX) else False)),
                   pe_cls=("f" if lhsT.dtype == F32 else "b"))

    def tr(self, out, in_, ident, reads, writes):
        self.sc.op("pe", lambda e: e.transpose(out, in_, ident), reads, writes, pe_full=TR_FULL,
                   pe_cls=("f" if in_.dtype == F32 else "b"))

    def act(self, out, in_, func, reads, writes, bias=None, scale=None, accum_out=None):
        kw = {}
        if bias is not None:
            kw["bias"] = bias
        if scale is not None:
            kw["scale"] = scale
        if accum_out is not None:
            kw["accum_out"] = accum_out
        self.sc.op("act", lambda e: e.activation(out=out, in_=in_, func=func, **kw), reads, writes)

    def tt(self, out, in0, in1, op, reads, writes, eng="dve"):
        self.sc.op(eng, lambda e: e.tensor_tensor(out=out, in0=in0, in1=in1, op=op), reads, writes)

    def ts(self, out, in0, s1, s2, op0, op1, reads, writes, eng="dve"):
        if op1 is None:
            self.sc.op(eng, lambda e: e.tensor_scalar(out=out, in0=in0, scalar1=s1, scalar2=None, op0=op0), reads, writes)
        else:
            self.sc.op(eng, lambda e: e.tensor_scalar(out=out, in0=in0, scalar1=s1, scalar2=s2, op0=op0, op1=op1), reads, writes)

    def stt(self, out, in0, scalar, in1, op0, op1, reads, writes):
        self.sc.op("dve", lambda e: e.scalar_tensor_tensor(out=out, in0=in0, scalar=scalar, in1=in1, op0=op0, op1=op1), reads, writes)

    def copy(self, out, in_, reads, writes, eng="dve"):
        if eng == "act":
            self.sc.op("act", lambda e: e.copy(out=out, in_=in_), reads, writes)
        else:
            self.sc.op(eng, lambda e: e.tensor_copy(out=out, in_=in_), reads, writes)

    def recip(self, out, in_, reads, writes):
        self.sc.op("dve", lambda e: e.reciprocal(out=out, in_=in_), reads, writes)

    def memset(self, ap, val, writes, eng="dve"):
        self.sc.op(eng, lambda e: e.memset(ap, val), (), writes)

    def load_consts(self):
        sc = self.sc
        sc.dma("sp", self.csem, self.cst[:], self.cst_d, writes=[self.t_cst])
        sc.dma("sp", self.csem, self.condf[:], self.cT_d, writes=[self.t_misc])
        sc.dma("sp", self.csem, self.bmodT[:], self.bmod_d, writes=[self.t_modrow])
        sc.dma("sp", self.csem, self.gains[:], self.gains_d, writes=[self.t_coef])
        for t in (self.t_cst, self.t_misc, self.t_modrow, self.t_coef):
            t.w = (self.csem.key, self.csem.count)
        self.copy(self.cstb[:], self.cst[:, 0:CSTB_N], [self.t_cst], [self.t_cstb])
        self.act(self.condT[:], self.condf[:], AF.Silu, [self.t_misc], [self.t_misc])

    def ident_f(self):
        return self.cst[:, CO_IDENT:CO_IDENT + 128]

    def ident_b(self):
        return self.cstb[:, CO_IDENT:CO_IDENT + 128]

    def onesdiv_b(self):
        return self.cstb[:, CO_ONESDIV:CO_ONESDIV + 128]

    def load_x(self):
        stg = self.R2[:].bitcast(F32)
        stok = [Tok(), Tok()]
        for tt in range(16):
            b = tt % 2
            st = stg[:, b * D:(b + 1) * D]
            self.sc.dma("sp", self.xsem[b], st, self.x_d[tt * 128:(tt + 1) * 128, :], writes=[stok[b]])
            for h in range(2):
                pi = 6 + h
                for q in range(4):
                    c = h * 4 + q
                    self.tr(self.P[pi][:, q * 128:(q + 1) * 128], st[:, c * 128:(c + 1) * 128], self.ident_f(),
                            [stok[b], self.t_cst], [self.Pt[pi]])
                dst = self.xT[:, h * 4:(h + 1) * 4, tt * 128:(tt + 1) * 128]
                src = self.P[pi][:].rearrange("p (q t) -> p q t", q=4)
                self.copy(dst, src, [self.Pt[pi]], [self.xtok[h * 4 + q][tt] for q in range(4)],
                          eng=("act" if h == 0 else "dve"))

    def store_x(self):
        stg = self.R2[:].bitcast(F32)
        stok = [Tok() for _ in range(4)]
        for tt in range(16):
            b = tt % 4
            st = stg[:, b * D:(b + 1) * D]
            for h in range(2):
                pi = 6 + h
                for q in range(4):
                    c = h * 4 + q
                    self.tr(self.P[pi][:, q * 128:(q + 1) * 128], self.xT[:, c, tt * 128:(tt + 1) * 128], self.ident_f(),
                            [self.xtok[c][tt], self.t_cst], [self.Pt[pi]])
                self.copy(st[:, h * 512:(h + 1) * 512], self.P[pi][:], [self.Pt[pi]], [stok[b]],
                          eng=("act" if h == 0 else "dve"))
            self.sc.dma("sp", self.osem[b], self.y_d[tt * 128:(tt + 1) * 128, :], st, reads=[stok[b]])
        self.sc.wait_all("sp", stok)

    def compute_mod(self, l):
        for j2 in range(18):
            slabs = self.w_take(2)
            pi = 6 + (j2 % 2)
            one = self.cst[0:1, CO_IDENT:CO_IDENT + 1]
            for h in range(2):
                ap, tk = slabs[h]
                w = ap.rearrange("p (k n) -> p k n", k=KC)
                for kc in range(KC):
                    self.mm(self.P[pi][0:1, h * 256:(h + 1) * 256], self.condT[:, kc:kc + 1], w[:, kc, :],
                            kc == 0, kc == KC - 1, [tk, self.t_misc], [self.Pt[pi]])
            self.w_release(2)
            mr = self.modrow[0:1, (j2 % 2) * 512:(j2 % 2 + 1) * 512]
            self.copy(mr, self.P[pi][0:1, :], [self.Pt[pi]], [self.t_mr[j2 % 2]])
            for q4 in range(4):
                q = j2 * 4 + q4
                self.mm(self.P[4][:, q:q + 1], mr[0:1, q4 * 128:(q4 + 1) * 128], one, True, True,
                        [self.t_mr[j2 % 2], self.t_cst], [self.Pt[4]])
        self.tt(self.modT[:, l * 72:(l + 1) * 72], self.P[4][:, 0:72], self.bmodT[:, l * 72:(l + 1) * 72], ALU.add,
                [self.Pt[4], self.t_modrow], [self.t_mod])
        for i in range(3):
            base = (l * 3 + i) * 3 * KC
            m0 = l * 72 + (3 * i) * KC
            self.stt(self.coef[:, base:base + KC], self.modT[:, m0 + KC:m0 + 2 * KC], 1.0,
                     self.gains[:, (l * 3 + i) * KC:(l * 3 + i + 1) * KC], ALU.add, ALU.mult,
                     [self.t_mod, self.t_coef], [self.t_coef])
            self.copy(self.coef[:, base + KC:base + 2 * KC], self.modT[:, m0:m0 + KC], [self.t_mod], [self.t_coef])
            self.ts(self.coef[:, base + 2 * KC:base + 3 * KC], self.modT[:, m0 + 2 * KC:m0 + 3 * KC],
                    0.5 if i != 1 else 1.0, None, ALU.mult, None, [self.t_mod], [self.t_coef])

    def cf(self, l, i, k, c):
        o = (l * 3 + i) * 3 * KC + k * KC + c
        return self.coef[:, o:o + 1]

    def modulate(self, l, i):
        for t in range(4):
            tsl = slice(t * 512, (t + 1) * 512)
            xt = [self.xtok[0][0]]
            for c in range(KC):
                b = c % 4
                rd = [self.xtok[c][4 * t + q] for q in range(4)]
                self.act(self.sqb[:, b * 512:(b + 1) * 512], self.xT[:, c, tsl], AF.Square, rd, [self.t_sqb[b]])
                self.mm(self.P[5][:], self.onesdiv_b(), self.sqb[:, b * 512:(b + 1) * 512], c == 0, c == KC - 1,
                        [self.t_sqb[b], self.t_cstb], [self.Pt[5]], full=True)
            r = t % 2
            rs = self.rstd[:, r * 512:(r + 1) * 512]
            self.act(rs, self.P[5][:], AF.Sqrt, [self.Pt[5]], [self.t_rstd[r]], bias=EPS)
            self.recip(rs, rs, [self.t_rstd[r]], [self.t_rstd[r]])
            for c in range(KC):
                b = c % 4
                rd = [self.xtok[c][4 * t + q] for q in range(4)]
                tmp = self.tmpA[:, b * 512:(b + 1) * 512]
                self.tt(tmp, self.xT[:, c, tsl], rs, ALU.mult, rd + [self.t_rstd[r]], [self.t_tmpA[b]])
                self.act(self.uT[:, c, tsl], tmp, AF.Identity, [self.t_tmpA[b], self.t_coef], [self.utok[c][t]],
                         bias=self.cf(l, i, 1, c), scale=self.cf(l, i, 0, c))

    def ffn(self, l, i):
        sg = self.tmpA
        for (f0, f1) in FGROUPS:
            nf = f1 - f0
            w13 = self.w_take(nf)
            for cl in range(nf):
                ap, tk = w13[cl]
                w = ap.rearrange("p (k n) -> p k n", k=KC)
                for t in range(4):
                    tsl = slice(t * 512, (t + 1) * 512)
                    pg = (cl * 4 + t) % 2
                    pu = 2 + pg
                    for kc in range(KC):
                        self.mm(self.P[pg][:], w[:, kc, 0:128], self.uT[:, kc, tsl], kc == 0, kc == KC - 1,
                                [tk, self.utok[kc][t]], [self.Pt[pg]], full=True)
                    for kc in range(KC):
                        self.mm(self.P[pu][:], w[:, kc, 128:256], self.uT[:, kc, tsl], kc == 0, kc == KC - 1,
                                [tk, self.utok[kc][t]], [self.Pt[pu]], full=True)
                    b = (cl * 4 + t) % 4
                    sgt = sg[:, b * 512:(b + 1) * 512]
                    self.act(sgt, self.P[pg][:], AF.Silu, [self.Pt[pg]], [self.t_tmpA[b]])
                    self.tt(self.actT[:, cl, tsl], sgt, self.P[pu][:], ALU.mult, [self.t_tmpA[b], self.Pt[pu]],
                            [self.atok[cl][t]])
            self.w_release(nf)
            w2 = self.w_take(nf // 2)
            for m in range(KC):
                for t in range(4):
                    tsl = slice(t * 512, (t + 1) * 512)
                    py = 4 + (m * 4 + t) % 2
                    for cl in range(nf):
                        ap, tk = w2[cl // 2]
                        w = ap.rearrange("p (f n) -> p f n", f=2)
                        self.mm(self.P[py][:], w[:, cl % 2, m * 128:(m + 1) * 128], self.actT[:, cl, tsl],
                                cl == 0, cl == nf - 1, [tk, self.atok[cl][t]], [self.Pt[py]], full=True)
                    xt = [self.xtok[m][4 * t + q] for q in range(4)]
                    self.stt(self.xT[:, m, tsl], self.P[py][:], self.cf(l, i, 2, m), self.xT[:, m, tsl],
                             ALU.mult, ALU.add, [self.Pt[py], self.t_coef] + xt, xt)
            self.w_release(nf // 2)


CO_IDENT = 0
CO_ONESDIV = 128
CSTB_N = 256
CST_N = 256


def make_consts():
    c = np.zeros((128, CST_N), np.float32)
    c[:, CO_IDENT:CO_IDENT + 128] = np.eye(128, dtype=np.float32)
    c[:, CO_ONESDIV:CO_ONESDIV + 128] = 1.0 / D
    return c


def build_program(n_slabs, stages=3, nlayers=L, dbg=None):
    nc = bass.Bass("TRN2", target_bir_lowering=False)
    with ExitStack() as es:
        pg = Prog(nc, es, n_slabs, stages, nlayers, dbg=dbg)
        blk = es.enter_context(nc.Block())
        pg.w_pump()
        pg.load_consts()
        pg.load_x()
        for l in range(nlayers):
            pg.compute_mod(l)
            pg.modulate(l, 0)
            pg.ffn(l, 0)
            if stages >= 2:
                pg.mixer(l)
            if stages >= 3:
                pg.modulate(l, 2)
                pg.ffn(l, 2)
        pg.store_x()
        assert pg.w_use == n_slabs, (pg.w_use, n_slabs)
        pg.sc.replay(blk)
    return nc


def prepare_inputs(inp, stages=3, nlayers=L):
    slabs = []
    for l in range(nlayers):
        slabs += pack_layer(inp, l, stages)
    wst = np.ascontiguousarray(np.stack(slabs, axis=0))
    gains = np.stack([inp["ffn1_norm"], inp["mix_norm"], inp["ffn2_norm"]], axis=1)
    gains = np.ascontiguousarray(gains.reshape(L, 3, KC, 128).transpose(3, 0, 1, 2).reshape(128, L * 3 * KC))
    bmod = np.ascontiguousarray(inp["b_mod"].reshape(L, 72, 128).transpose(2, 0, 1).reshape(128, L * 72))
    cst = make_consts()
    mcf, mcb = make_mixer_consts()
    vec, cw = make_vecs(inp)
    maps = []
    for b in range(8):
        maps.append({
            "x": np.ascontiguousarray(inp["x"][b]),
            "wst": wst,
            "cT": np.ascontiguousarray(inp["c"][b].reshape(KC, 128).T),
            "bmod": bmod,
            "gains": gains,
            "cst": cst,
            "mcf": mcf, "mcb": mcb, "vec": vec, "cw": cw,
        })
    return maps, wst.shape[0]


def kernel(**inputs):
    inp = {k: np.asarray(v, dtype=np.float32) for k, v in inputs.items()}
    maps, n_slabs = prepare_inputs(inp)
    nc = build_program(n_slabs)
    res = run_bass_kernel_spmd(nc, maps, core_ids=list(range(8)))
    return np.stack([res.results[b]["y"] for b in range(8)], axis=0).astype(np.float32)


def _layout(items):
    off, o = {}, 0
    for k, n in items:
        off[k] = o
        o += n
    return off, o


MCF, MCF_N = _layout([("cosA", 128), ("sinA", 128), ("cosC", 512), ("sinC", 512), ("decC", 512), ("gqC", 256),
                      ("kendC", 4), ("gSC", 2), ("ublk", 128), ("blk", 128), ("ind", 256), ("mbias", 128), ("ones", 128)])
MCB, MCB_N = _layout([("mcur", 128), ("mprev", 128), ("shc", 384), ("shp", 384), ("nm", 768), ("nmT", 768)])
VEC, VEC_N = _layout([("gq", 64), ("gk", 64), ("sink", 8), ("alog", 8), ("dtb", 8), ("gon", 64)])
LEVELS = [1, 2, 4, 8, 16, 32]


def make_mixer_consts():
    f = np.zeros((128, MCF_N), np.float32)
    b = np.zeros((128, MCB_N), np.float32)
    pos = np.arange(S, dtype=np.float32)
    inv = (np.float32(500000.0) ** (-np.arange(0, 16, 2, dtype=np.float32) / np.float32(16))).astype(np.float32)
    ph = (pos[:, None] * inv[None, :]).astype(np.float32)
    f[:, MCF["cosA"]:MCF["cosA"] + 128] = np.cos(ph).astype(np.float32).reshape(16, 128, 8).transpose(1, 0, 2).reshape(128, 128)
    f[:, MCF["sinA"]:MCF["sinA"] + 128] = np.sin(ph).astype(np.float32).reshape(16, 128, 8).transpose(1, 0, 2).reshape(128, 128)
    ang = (1.0 / (np.float32(10000.0) ** np.linspace(0.0, 1.0, 32, dtype=np.float32))).astype(np.float32)
    phc = (pos[:, None] * ang[None, :]).astype(np.float32)
    f[:, MCF["cosC"]:MCF["cosC"] + 512] = np.cos(phc).astype(np.float32).reshape(16, 128, 32).transpose(1, 0, 2).reshape(128, 512)
    f[:, MCF["sinC"]:MCF["sinC"] + 512] = np.sin(phc).astype(np.float32).reshape(16, 128, 32).transpose(1, 0, 2).reshape(128, 512)
    lg = np.log1p(-np.exp2(-5.0 - np.arange(4, dtype=np.float64)))
    j = np.arange(128)[:, None].astype(np.float64)
    i = np.arange(128)[None, :].astype(np.float64)
    dec = np.zeros((128, 4, 128), np.float64)
    for h in range(4):
        dec[:, h, :] = np.where(i >= j, np.exp(lg[h] * (i - j)), 0.0) * 0.125
    f[:, MCF["decC"]:MCF["decC"] + 512] = dec.reshape(128, 512)
    gq = np.zeros((128, 2, 128), np.float64)
    for p in range(128):
        for pr in range(2):
            h = 2 * pr + p // 64
            gq[p, pr, :] = np.exp(lg[h] * (np.arange(128) + 1.0))
    f[:, MCF["gqC"]:MCF["gqC"] + 256] = gq.reshape(128, 256)
    for h in range(4):
        f[:, MCF["kendC"] + h] = 0.125 * np.exp(lg[h] * (127.0 - np.arange(128)))
    for pr in range(2):
        for p in range(128):
            f[p, MCF["gSC"] + pr] = np.exp(lg[2 * pr + p // 64] * 128.0)
    t = np.arange(128)[:, None]
    m = np.arange(128)[None, :]
    same = (t // 64) == (m // 64)
    f[:, MCF["ublk"]:MCF["ublk"] + 128] = ((t <= m) & same)
    f[:, MCF["blk"]:MCF["blk"] + 128] = same
    f[:, MCF["ind"]:MCF["ind"] + 128] = (t < 64) * np.ones((1, 128))
    f[:, MCF["ind"] + 128:MCF["ind"] + 256] = (t >= 64) * np.ones((1, 128))
    f[:, MCF["mbias"]:MCF["mbias"] + 128] = np.where((m >= t) & same, 0.0, -30000.0)
    f[:, MCF["ones"]:MCF["ones"] + 128] = 1.0
    k = np.arange(128)[:, None]
    q = np.arange(128)[None, :]
    b[:, MCB["mcur"]:MCB["mcur"] + 128] = (k <= q)
    b[:, MCB["mprev"]:MCB["mprev"] + 128] = (k > q)
    for s_ in (1, 2, 3):
        b[:, MCB["shc"] + (s_ - 1) * 128:MCB["shc"] + s_ * 128] = (t == m - s_)
        b[:, MCB["shp"] + (s_ - 1) * 128:MCB["shp"] + s_ * 128] = (t == 128 + m - s_)
    for li, s_ in enumerate(LEVELS):
        msk = ((t // (2 * s_)) == (m // (2 * s_))) & ((t % (2 * s_)) < s_) & ((m % (2 * s_)) >= s_)
        b[:, MCB["nm"] + li * 128:MCB["nm"] + (li + 1) * 128] = -1.0 * msk
        b[:, MCB["nmT"] + li * 128:MCB["nmT"] + (li + 1) * 128] = -1.0 * msk.T
    return f, b


def make_vecs(inp):
    v = np.zeros((L, 128, VEC_N), np.float32)
    for l in range(L):
        for name, key in (("gq", "attn_q_norm"), ("gk", "attn_k_norm"), ("sink", "attn_sinks"), ("alog", "dn_a_log"),
                          ("dtb", "dn_dt_bias"), ("gon", "dn_out_norm")):
            a = inp[key][l]
            v[l, :, VEC[name]:VEC[name] + a.shape[0]] = a[None, :]
    cw = np.ascontiguousarray(np.broadcast_to(inp["dn_conv"].reshape(L, 1, 4 * 1536), (L, 128, 4 * 1536)))
    return np.ascontiguousarray(v.transpose(1, 0, 2).reshape(128, L * VEC_N)), np.ascontiguousarray(
        cw.transpose(1, 0, 2).reshape(128, L * 4 * 1536))


class Arena:
    def __init__(self, f32view, b16view, nbytes):
        self.f = f32view
        self.b = b16view
        self.n = nbytes
        self.o = 0

    def f32(self, n):
        self.o = (self.o + 3) // 4 * 4
        o = self.o
        self.o += 4 * n
        assert self.o <= self.n, ("arena overflow", self.o, self.n)
        return self.f[:, o // 4:o // 4 + n]

    def b16(self, n):
        self.o = (self.o + 3) // 4 * 4
        o = self.o
        self.o += 2 * n
        assert self.o <= self.n, ("arena overflow", self.o, self.n)
        return self.b[:, o // 2:o // 2 + n]


def _mixer_setup(self, l):
    sc = self.sc
    for c in range(KC):
        sc.dma("sp", self.spsem, self.xsp_d[:, c * S:(c + 1) * S], self.xT[:, c, :],
               reads=[self.xtok[c][tt] for tt in range(16)])
    sc.barrier()
    A0 = Arena(self.R0[:], self.R0[:].bitcast(BF16), 4 * KC * S)
    A2 = Arena(self.R2[:].bitcast(F32), self.R2[:], 2 * KC * S)
    self.A0, self.A2 = A0, A2
    m = self.m = {}
    tk = self.mt = {}
    m["mcf"] = A0.f32(MCF_N)
    m["mcb"] = A0.b16(MCB_N)
    m["vec"] = A0.f32(VEC_N)
    m["esink"] = A0.f32(8)
    m["nA"] = A0.f32(8)
    tk["mc"] = Tok("mc")
    tk["mcb"] = Tok("mcb")
    sc.dma("sp", self.mcsem, m["mcf"], self.mcf_d, writes=[tk["mc"]])
    sc.dma("sp", self.mcsem, m["vec"], self.vec_d[:, l * VEC_N:(l + 1) * VEC_N], writes=[tk["mc"]])
    for h_ in range(2):
        hs = slice(h_ * (MCB_N // 2), (h_ + 1) * (MCB_N // 2))
        sc.dma("pool", self.mcbsem, m["mcb"][:, hs], self.mcb_d[:, hs], writes=[tk["mcb"]])
    m["oT"] = [A0.b16(512), A0.b16(512)]
    tk["oT"] = [Tok(), Tok()]
    tk["mc"].w = (self.mcsem.key, self.mcsem.count)
    tk["der"] = Tok("der")
    self.act(m["esink"], m["vec"][:, VEC["sink"]:VEC["sink"] + 8], AF.Exp, [tk["mc"]], [tk["der"]])
    self.act(m["nA"], m["vec"][:, VEC["alog"]:VEC["alog"] + 8], AF.Exp, [tk["mc"]], [tk["der"]])
    self.ts(m["nA"], m["nA"], -1.0, None, ALU.mult, None, [tk["der"]], [tk["der"]])
    self.Pb = [p[:].bitcast(BF16) for p in self.P]


def _mcf(self, name, n):
    return self.m["mcf"][:, MCF[name]:MCF[name] + n]


def _mcb(self, name, n, off=0):
    return self.m["mcb"][:, MCB[name] + off:MCB[name] + off + n]


def _vec(self, name, n):
    return self.m["vec"][:, VEC[name]:VEC[name] + n]


def _emit_out(self, g, tt, ob, t_ob, bank=2):
    par = tt % 2
    for k in range(4):
        self.tr(self.Pb[bank][:, k * 128:(k + 1) * 128], ob[:, k * 128:(k + 1) * 128], self.ident_b(),
                [t_ob, self.t_cstb], [self.Pt[bank]])
    oT = self.m["oT"][par]
    self.copy(oT, self.Pb[bank][:, 0:512], [self.Pt[bank]], [self.mt["oT"][par]], eng="act")
    dst = self.osp_d[g * 4:(g + 1) * 4, :, tt * 128:(tt + 1) * 128].rearrange("k p t -> p k t")
    self.sc.dma("sp", self.otsem[par], dst, oT.rearrange("p (k t) -> p k t", k=4), reads=[self.mt["oT"][par]])


def _branch_A(self, l):
    m, tk, A2 = self.m, self.mt, self.A2
    P, Pt, Pb = self.P, self.Pt, self.Pb
    mark = A2.o
    sq = A2.f32(640)
    qn = A2.f32(640)
    rt = A2.f32(4 * 80)
    qb = A2.b16(640)
    ss = A2.f32(10)
    rst = A2.f32(10)
    qT = A2.b16(512)
    kT = [A2.b16(128), A2.b16(128)]
    vaug = [A2.b16(130), A2.b16(130)]
    E = [[A2.b16(512) for _ in range(2)] for _ in range(2)]
    den = A2.f32(4)
    oa = A2.b16(512)
    t_sq, t_qn, t_rt, t_qb, t_ss, t_qT, t_den, t_oa = (Tok() for _ in range(8))
    t_kT = [Tok(), Tok()]
    t_v = [Tok(), Tok()]
    t_E = [[Tok(), Tok()], [Tok(), Tok()]]
    for par in range(2):
        self.memset(vaug[par], 1.0, [t_v[par]])
    slabs = self.w_take(3)
    qn3 = qn.rearrange("p (h d) -> p h d", d=64)
    qb3 = qb.rearrange("p (h d) -> p h d", d=64)
    rt4 = rt.rearrange("p (a h d) -> p a h d", a=4, h=10)
    for tt in range(16):
        tsl = slice(tt * 128, (tt + 1) * 128)
        par = tt % 2
        for j in range(3):
            ap, wtk = slabs[j]
            w = ap.rearrange("p (k n) -> p k n", k=KC)
            dst = P[0][:, j * 256:(j + 1) * 256] if j < 2 else P[1][:, 0:256]
            for kc in range(KC):
                self.mm(dst, self.uT[:, kc, tsl], w[:, kc, :], kc == 0, kc == KC - 1,
                        [wtk, self.utok[kc][tt // 4]], [Pt[0] if j < 2 else Pt[1]], full=True)
        self.act(sq[:, 0:512], P[0][:], AF.Square, [Pt[0]], [t_sq])
        self.act(sq[:, 512:640], P[1][:, 0:128], AF.Square, [Pt[1]], [t_sq])
        self.sc.op("dve", lambda e: e.tensor_reduce(out=ss, in_=sq.rearrange("p (h d) -> p h d", d=64), axis=AX.X, op=ALU.add),
                   [t_sq], [t_ss])
        self.act(rst, ss, AF.Sqrt, [t_ss], [t_ss], bias=EPS, scale=1.0 / 64)
        self.recip(rst, rst, [t_ss], [t_ss])
        self.tt(qn3[:, 0:8, :], P[0][:].rearrange("p (h d) -> p h d", d=64), rst[:, 0:8].unsqueeze(2).to_broadcast([128, 8, 64]),
                ALU.mult, [Pt[0], t_ss], [t_qn])
        self.tt(qn3[:, 8:10, :], P[1][:, 0:128].rearrange("p (h d) -> p h d", d=64),
                rst[:, 8:10].unsqueeze(2).to_broadcast([128, 2, 64]), ALU.mult, [Pt[1], t_ss], [t_qn])
        self.tt(qn3[:, 0:8, :], qn3[:, 0:8, :], _vec(self, "gq", 64).unsqueeze(1).to_broadcast([128, 8, 64]), ALU.mult,
                [t_qn, tk["mc"]], [t_qn])
        self.tt(qn3[:, 8:10, :], qn3[:, 8:10, :], _vec(self, "gk", 64).unsqueeze(1).to_broadcast([128, 2, 64]), ALU.mult,
                [t_qn, tk["mc"]], [t_qn])
        self.copy(vaug[par].rearrange("p (g d) -> p g d", g=2)[:, :, 0:64], P[1][:, 128:256].rearrange("p (g d) -> p g d", g=2),
                  [Pt[1]], [t_v[par]], eng="act")
        cos = _mcf(self, "cosA", 128)[:, tt * 8:(tt + 1) * 8].unsqueeze(1).to_broadcast([128, 10, 8])
        sin = _mcf(self, "sinA", 128)[:, tt * 8:(tt + 1) * 8].unsqueeze(1).to_broadcast([128, 10, 8])
        x1, x2 = qn3[:, :, 0:8], qn3[:, :, 8:16]
        self.tt(rt4[:, 0], x1, cos, ALU.mult, [t_qn, tk["mc"]], [t_rt], eng="pool")
        self.tt(rt4[:, 1], x2, sin, ALU.mult, [t_qn, tk["mc"]], [t_rt], eng="pool")
        self.tt(rt4[:, 2], x2, cos, ALU.mult, [t_qn, tk["mc"]], [t_rt], eng="pool")
        self.tt(rt4[:, 3], x1, sin, ALU.mult, [t_qn, tk["mc"]], [t_rt], eng="pool")
        self.copy(qb, qn, [t_qn], [t_qb], eng="act")
        self.tt(qb3[:, :, 0:8], rt4[:, 0], rt4[:, 1], ALU.subtract, [t_rt, t_qb], [t_qb], eng="pool")
        self.tt(qb3[:, :, 8:16], rt4[:, 2], rt4[:, 3], ALU.add, [t_rt, t_qb], [t_qb], eng="pool")
        for i in range(4):
            self.tr(Pb[2][:, i * 128:(i + 1) * 128], qb[:, i * 128:(i + 1) * 128], self.ident_b(), [t_qb, self.t_cstb], [Pt[2]])
        self.tr(Pb[1][:, 0:128], qb[:, 512:640], self.ident_b(), [t_qb, self.t_cstb], [Pt[1]])
        self.copy(qT, Pb[2][:, 0:512], [Pt[2]], [t_qT], eng="act")
        self.copy(kT[par], Pb[1][:, 0:128], [Pt[1]], [t_kT[par]], eng="dve")
        blocks = [(1, par)] if tt == 0 else [(0, 1 - par), (1, par)]
        for g in range(2):
            for (jj, kp) in blocks:
                bank = 3 + g * 2 + jj
                self.mm(P[bank][:], kT[kp][g * 64:(g + 1) * 64, :], qT[g * 64:(g + 1) * 64, :], True, True,
                        [t_kT[kp], t_qT], [Pt[bank]])
                self.act(E[g][jj], P[bank][:], AF.Exp, [Pt[bank]], [t_E[g][jj]], scale=0.125)
                msk = _mcb(self, "mcur" if jj == 1 else "mprev", 128).unsqueeze(1).to_broadcast([128, 4, 128])
                e3 = E[g][jj].rearrange("p (i q) -> p i q", i=4)
                self.tt(e3, e3, msk, ALU.mult, [t_E[g][jj], tk["mcb"]], [t_E[g][jj]], eng="pool")
            for i in range(4):
                for n_, (jj, kp) in enumerate(blocks):
                    self.mm(P[7][:, i * 65:(i + 1) * 65], E[g][jj][:, i * 128:(i + 1) * 128],
                            vaug[kp][:, g * 65:(g + 1) * 65], n_ == 0, n_ == len(blocks) - 1,
                            [t_E[g][jj], t_v[kp]], [Pt[7]], full=True)
            p3 = P[7][:, 0:260].rearrange("p (i d) -> p i d", i=4)
            self.tt(den, p3[:, :, 64], m["esink"][:, g * 4:(g + 1) * 4], ALU.add, [Pt[7], tk["der"]], [t_den])
            self.recip(den, den, [t_den], [t_den])
            self.tt(oa[:, g * 256:(g + 1) * 256].rearrange("p (i d) -> p i d", i=4), p3[:, :, 0:64],
                    den.unsqueeze(2).to_broadcast([128, 4, 64]), ALU.mult, [Pt[7], t_den], [t_oa])
        _emit_out(self, 0, tt, oa, t_oa, bank=7)
        yield
    self.w_release(3)
    A2.o = mark


Prog.mixer_setup = _mixer_setup
Prog.branch_A = _branch_A


def _run_AC(self, l, doA=True, doC=True):
    mark = self.A2.o
    mark0 = self.A0.o
    ga = self.branch_A(l) if doA else None
    if not doA:
        self.w_take(3)
    gc = self.branch_C(l) if doC else None
    if not doC:
        self.w_take(6)
    for tt in range(16):
        if ga is not None:
            next(ga)
        if gc is not None:
            next(gc)
    for g_ in (ga, gc):
        if g_ is not None:
            for _ in g_:
                pass
    if not doA:
        self.w_release(3)
    if not doC:
        self.w_release(6)
    self.A2.o = mark
    self.A0.o = mark0


def _mixer(self, l):
    sc = self.sc
    self.modulate(l, 1)
    self.mixer_setup(l)
    dbg = self.dbg
    _run_AC(self, l, (not dbg or "A" in dbg), (not dbg or "C" in dbg))
    if not dbg or "B" in dbg:
        self.branch_B(l)
    else:
        self.w_take(9); self.w_release(9)
    if dbg:
        for _ in range(28):
            self.w_take(1); self.w_release(1)
        self.reload_x()
        return
    self.merge(l)


def _reload_x(self):
    sc = self.sc
    sc.barrier()
    for c in range(KC):
        sc.dma("sp", self.rlsem, self.xT[:, c, :], self.xsp_d[:, c * S:(c + 1) * S],
               writes=[self.xtok[c][tt] for tt in range(16)])
    for row in self.xtok:
        for t in row:
            t.w = (self.rlsem.key, self.rlsem.count)


Prog.mixer = _mixer
Prog.reload_x = _reload_x


def _branch_C(self, l):
    m, tk, A2 = self.m, self.mt, self.A2
    P, Pt, Pb = self.P, self.Pt, self.Pb
    mark = A2.o
    rt = self.A0.f32(4 * 256)
    qkr = A2.b16(512)
    qkT = A2.b16(512)
    vt = A2.b16(512)
    ST = A2.b16(512)
    qdT = A2.b16(256)
    kend = A2.b16(256)
    Sf = A2.f32(256)
    Stmp = A2.f32(256)
    Sb = A2.b16(256)
    osq = A2.f32(512)
    sg = A2.f32(512)
    ot = A2.f32(512)
    oc = A2.b16(512)
    ssC = A2.f32(4)
    rstC = A2.f32(4)
    t_rt, t_qkr, t_qkT, t_vt, t_ST, t_qdT, t_kend, t_S, t_Stmp, t_Sb, t_osq, t_sg, t_ot, t_oc, t_ss = (Tok() for _ in range(15))
    slabs = self.w_take(6)
    rt4 = rt.rearrange("p (a h m) -> p a h m", a=4, h=8)
    qkr4 = qkr.rearrange("p (h m two) -> p h m two", h=8, two=2)
    for tt in range(16):
        tsl = slice(tt * 128, (tt + 1) * 128)
        for j in range(6):
            ap, wtk = slabs[j]
            w = ap.rearrange("p (k n) -> p k n", k=KC)
            bank = (0, 0, 1, 1, 6, 6)[j]
            dst = P[bank][:, (j % 2) * 256:(j % 2 + 1) * 256]
            for kc in range(KC):
                self.mm(dst, self.uT[:, kc, tsl], w[:, kc, :], kc == 0, kc == KC - 1, [wtk, self.utok[kc][tt // 4]], [Pt[bank]], full=True)
        v4 = P[0][:].rearrange("p (h m two) -> p h m two", h=8, two=2)
        xe, xo = v4[:, :, :, 0], v4[:, :, :, 1]
        cos = _mcf(self, "cosC", 512)[:, tt * 32:(tt + 1) * 32].unsqueeze(1).to_broadcast([128, 8, 32])
        sin = _mcf(self, "sinC", 512)[:, tt * 32:(tt + 1) * 32].unsqueeze(1).to_broadcast([128, 8, 32])
        self.tt(rt4[:, 0], xe, cos, ALU.mult, [Pt[0], tk["mc"]], [t_rt])
        self.tt(rt4[:, 1], xo, sin, ALU.mult, [Pt[0], tk["mc"]], [t_rt])
        self.tt(rt4[:, 2], xo, cos, ALU.mult, [Pt[0], tk["mc"]], [t_rt])
        self.tt(rt4[:, 3], xe, sin, ALU.mult, [Pt[0], tk["mc"]], [t_rt])
        self.tt(qkr4[:, :, :, 0], rt4[:, 0], rt4[:, 1], ALU.subtract, [t_rt], [t_qkr], eng="pool")
        self.tt(qkr4[:, :, :, 1], rt4[:, 2], rt4[:, 3], ALU.add, [t_rt], [t_qkr], eng="pool")
        self.copy(vt, P[1][:], [Pt[1]], [t_vt], eng="act")
        for i in range(4):
            self.tr(Pb[2][:, i * 128:(i + 1) * 128], qkr[:, i * 128:(i + 1) * 128], self.ident_b(), [t_qkr, self.t_cstb], [Pt[2]])
        self.copy(qkT, Pb[2][:, 0:512], [Pt[2]], [t_qkT], eng="act")
        qkT3 = qkT.rearrange("p (i t) -> p i t", i=4)
        for h in (0, 2, 1, 3):
            r0 = (h % 2) * 64
            self.mm(P[3][:, h * 128:(h + 1) * 128], qkT3[r0:r0 + 64, 2 + h // 2, :], qkT3[r0:r0 + 64, h // 2, :], True, True,
                    [t_qkT], [Pt[3]], tkey=("T", r0, 0))
        self.tt(ST, P[3][:], _mcf(self, "decC", 512), ALU.mult, [Pt[3], tk["mc"]], [t_ST])
        if tt > 0:
            self.tt(qdT, qkT[:, 0:256], _mcf(self, "gqC", 256), ALU.mult, [t_qkT, tk["mc"]], [t_qdT], eng="pool")
        qdT3 = qdT.rearrange("p (i t) -> p i t", i=2)
        Sb3 = Sb.rearrange("p (i v) -> p i v", i=2)
        for h in range(4):
            r0 = (h % 2) * 64
            self.mm(P[4][:, h * 128:(h + 1) * 128], ST[:, h * 128:(h + 1) * 128], vt[:, h * 128:(h + 1) * 128], True, tt == 0,
                    [t_ST, t_vt], [Pt[4]])
            if tt > 0:
                self.mm(P[4][:, h * 128:(h + 1) * 128], qdT3[r0:r0 + 64, h // 2, :], Sb3[r0:r0 + 64, h // 2, :], False, True,
                        [t_qdT, t_Sb], [Pt[4]])
        if tt < 15:
            self.tt(kend.rearrange("p (h d) -> p h d", h=4), qkr[:, 256:512].rearrange("p (h d) -> p h d", h=4),
                    _mcf(self, "kendC", 4).unsqueeze(2).to_broadcast([128, 4, 64]), ALU.mult, [t_qkr, tk["mc"]], [t_kend], eng="pool")
            for h in (0, 2, 1, 3):
                r0 = (h % 2) * 64
                self.mm(P[5][r0:r0 + 64, (h // 2) * 128:(h // 2 + 1) * 128], kend[:, h * 64:(h + 1) * 64],
                        vt[:, h * 128:(h + 1) * 128], True, True, [t_kend, t_vt], [Pt[5]], tkey=("T", 0, r0))
            if tt == 0:
                self.copy(Sf, P[5][:, 0:256], [Pt[5]], [t_S], eng="dve")
            else:
                self.tt(Stmp.rearrange("p (i v) -> p i v", i=2), Sf.rearrange("p (i v) -> p i v", i=2),
                        _mcf(self, "gSC", 2).unsqueeze(2).to_broadcast([128, 2, 128]), ALU.mult, [t_S, tk["mc"]], [t_Stmp], eng="pool")
                self.tt(Sf, Stmp, P[5][:, 0:256], ALU.add, [t_Stmp, Pt[5]], [t_S])
            self.copy(Sb, Sf, [t_S], [t_Sb], eng="act")
        self.act(osq, P[4][:], AF.Square, [Pt[4]], [t_osq])
        self.sc.op("dve", lambda e: e.tensor_reduce(out=ssC, in_=osq.rearrange("p (h d) -> p h d", h=4), axis=AX.X, op=ALU.add),
                   [t_osq], [t_ss])
        self.act(rstC, ssC, AF.Sqrt, [t_ss], [t_ss], bias=EPS, scale=1.0 / 128)
        self.recip(rstC, rstC, [t_ss], [t_ss])
        self.act(sg, P[6][:], AF.Silu, [Pt[6]], [t_sg])
        self.tt(ot.rearrange("p (h d) -> p h d", h=4), P[4][:].rearrange("p (h d) -> p h d", h=4),
                rstC.unsqueeze(2).to_broadcast([128, 4, 128]), ALU.mult, [Pt[4], t_ss], [t_ot])
        self.tt(oc, ot, sg, ALU.mult, [t_ot, t_sg], [t_oc], eng="pool")
        _emit_out(self, 2, tt, oc, t_oc, bank=2)
        yield
    self.w_release(6)


Prog.branch_C = _branch_C


def _branch_B(self, l):
    m, tk = self.m, self.mt
    P, Pt, Pb = self.P, self.Pt, self.Pb
    A0, A2 = self.A0, self.A2
    mark0, mark2 = A0.o, A2.o
    sc = self.sc

    def bc_h(ap8, n):
        return ap8.unsqueeze(2).to_broadcast([128, 8, n])

    def bc_m(ap, h):
        return ap.unsqueeze(1).to_broadcast([128, h, ap.shape[1]])

    cw = A2.b16(4 * 1536)
    xc = [A2.b16(1536), A2.b16(1536)]
    acc = A2.f32(1536)
    ctmp = [A2.f32(512), A2.f32(512)]
    zs = A2.f32(512)
    Vb = A2.f32(512)
    A3 = Arena(self.R3[:], self.R3[:].bitcast(BF16), 8192)
    qkf = A0.f32(1024)
    Nm = qkf
    sqf = A0.b16(1024)
    IT = sqf
    Kn, Qn, Kb, Kbe, Kend, Qd = (A0.b16(512) for _ in range(6))
    featT = A0.b16(2560)
    decT = A0.f32(1024)
    tmpW = decT
    X, XT = A0.f32(1024), A0.f32(1024)
    Mm, Ym = A3.f32(1024), A3.f32(1024)
    zz = A0.f32(512)
    vnew = A0.b16(512)
    Sf = A0.f32(256)
    Sb = A0.b16(256)
    ob = A0.b16(512)
    sm = A0.f32(16 * 12)
    xa, ax, ee, lp, alog, beta, gcum, eg, kdec, tmp8 = (sm[:, i * 8:(i + 1) * 8] for i in range(10))
    decS = sm[:, 80:96]
    ss16 = sm[:, 96:112]
    rst16 = sm[:, 112:128]
    ss8 = sm[:, 128:136]
    rst8 = sm[:, 136:144]
    rhsU = acc[:, 0:1024]
    tmpWT = acc[:, 0:1024]
    osq, ot = ctmp[0], ctmp[1]
    T = {k: Tok(k) for k in ("cw", "zs", "qkf", "sqf", "Kn", "Qn", "Kb", "Kbe", "Kend", "Vb", "Qd", "featT", "decT", "M",
                             "X", "XT", "Ym", "zz", "vnew", "S", "Sb", "ob", "gate", "g2", "nrm", "nrm8")}
    T["N"] = T["qkf"]
    T["IT"] = T["sqf"]
    t_xc = [Tok(), Tok()]
    t_acc = [Tok(), Tok(), Tok()]
    t_ct = [Tok(), Tok()]
    for j in range(4):
        sc.dma("pool", self.cwsem, cw[:, j * 1536:(j + 1) * 1536], self.cw_d[:, l * 6144 + j * 1536:l * 6144 + (j + 1) * 1536],
               writes=[T["cw"]])
    T["cw"].w = (self.cwsem.key, self.cwsem.count)
    slabs = self.w_take(9)
    ident8 = bc_m(self.ident_f(), 8)
    Kn3, Qn3, Kb3, Kbe3, Kend3, Vb3, Qd3 = (a.rearrange("p (h d) -> p h d", h=8) for a in (Kn, Qn, Kb, Kbe, Kend, Vb, Qd))
    F4 = featT.rearrange("p (k i t) -> p k i t", k=5, i=4)
    N3, M3, X3, XT3, Y3, IT3 = (a.rearrange("p (h t) -> p h t", h=8) for a in (Nm, Mm, X, XT, Ym, IT))
    dec3 = decT.rearrange("p (h t) -> p h t", h=8)
    first = True
    self.bdbg = dict(cw=cw, acc=acc, qkf=qkf, Kn=Kn, Qn=Qn, Kb=Kb, Kbe=Kbe, Kend=Kend, Vb=Vb, Qd=Qd, featT=featT, decT=decT, N=Nm, M=Mm,
                     X=X, XT=XT, Ym=Ym, IT=IT, zz=zz, vnew=vnew, Sf=Sf, Sb=Sb, sm=sm, zs=zs, xc0=xc[0])
    for tt in range(16):
        tsl = slice(tt * 128, (tt + 1) * 128)
        par = tt % 2
        for j in range(9):
            ap, wtk = slabs[j]
            w = ap.rearrange("p (k n) -> p k n", k=KC)
            bank = (0, 0, 1, 1, 2, 2, 3, 3, 4)[j]
            dst = P[bank][:, (j % 2) * 256:(j % 2 + 1) * 256]
            for kc in range(KC):
                self.mm(dst, self.uT[:, kc, tsl], w[:, kc, :], kc == 0, kc == KC - 1, [wtk, self.utok[kc][tt // 4]], [Pt[bank]], full=True)
        self.act(zs, P[3][:], AF.Silu, [Pt[3]], [T["zs"]])
        self.tt(xa, P[4][:, 0:8], _vec(self, "dtb", 8), ALU.add, [Pt[4], tk["mc"]], [T["gate"]])
        self.act(ax, xa, AF.Abs, [T["gate"]], [T["gate"]])
        self.act(ee, ax, AF.Exp, [T["gate"]], [T["gate"]], scale=-1.0)
        self.act(lp, ee, AF.Ln, [T["gate"]], [T["gate"]], bias=1.0)
        self.ts(xa, xa, 0.0, None, ALU.max, None, [T["gate"]], [T["gate"]])
        self.tt(xa, xa, lp, ALU.add, [T["gate"]], [T["gate"]])
        self.tt(alog, xa, m["nA"], ALU.mult, [T["gate"], tk["der"]], [T["gate"]])
        self.act(beta, P[4][:, 8:16], AF.Sigmoid, [Pt[4]], [T["gate"]])
        for ct in range(3):
            self.copy(xc[par][:, ct * 512:(ct + 1) * 512], P[ct][:], [Pt[ct]], [t_xc[par]], eng="act")
        for ct in range(3):
            csl = slice(ct * 512, (ct + 1) * 512)
            for s_ in (1, 2, 3):
                bank = 4 + s_
                self.mm(P[bank][:], _mcb(self, "shc", 128, (s_ - 1) * 128), xc[par][:, csl], True, tt == 0,
                        [tk["mcb"], t_xc[par]], [Pt[bank]], full=True)
                if tt > 0:
                    self.mm(P[bank][:], _mcb(self, "shp", 128, (s_ - 1) * 128), xc[1 - par][:, csl], False, True,
                            [tk["mcb"], t_xc[1 - par]], [Pt[bank]], full=True)
            self.tt(acc[:, csl], P[ct][:], cw[:, 3 * 1536 + ct * 512:3 * 1536 + (ct + 1) * 512], ALU.mult,
                    [Pt[ct], T["cw"]], [t_acc[ct]])
            for s_ in (1, 2, 3):
                b_ = s_ % 2
                self.tt(ctmp[b_], P[4 + s_][:], cw[:, (3 - s_) * 1536 + ct * 512:(3 - s_) * 1536 + (ct + 1) * 512], ALU.mult,
                        [Pt[4 + s_], T["cw"]], [t_ct[b_]])
                self.tt(acc[:, csl], acc[:, csl], ctmp[b_], ALU.add, [t_acc[ct], t_ct[b_]], [t_acc[ct]], eng="pool")
        self.act(qkf, acc[:, 0:1024], AF.Silu, [t_acc[0], t_acc[1]], [T["qkf"]])
        self.act(ctmp[0], acc[:, 1024:1536], AF.Silu, [t_acc[2]], [t_ct[0]])
        self.tt(Vb3, ctmp[0].rearrange("p (h d) -> p h d", h=8), bc_h(beta, 64), ALU.mult, [t_ct[0], T["gate"]], [T["Vb"]])
        self.act(sqf, qkf, AF.Square, [T["qkf"]], [T["sqf"]])
        sc.op("dve", lambda e: e.tensor_reduce(out=ss16, in_=sqf.rearrange("p (h d) -> p h d", h=16), axis=AX.X, op=ALU.add),
              [T["sqf"]], [T["nrm"]])
        self.act(rst16, ss16, AF.Sqrt, [T["nrm"]], [T["nrm"]], bias=EPS)
        self.recip(rst16, rst16, [T["nrm"]], [T["nrm"]])
        self.ts(rst16[:, 0:8], rst16[:, 0:8], 0.125, None, ALU.mult, None, [T["nrm"]], [T["nrm"]])
        self.tt(Qn3, qkf[:, 0:512].rearrange("p (h d) -> p h d", h=8), bc_h(rst16[:, 0:8], 64), ALU.mult, [T["qkf"], T["nrm"]], [T["Qn"]])
        self.tt(Kn3, qkf[:, 512:1024].rearrange("p (h d) -> p h d", h=8), bc_h(rst16[:, 8:16], 64), ALU.mult,
                [T["qkf"], T["nrm"]], [T["Kn"]])
        self.mm(P[5][:, 0:8], _mcf(self, "ublk", 128), alog, True, True, [tk["mc"], T["gate"]], [Pt[5]])
        self.mm(P[5][:, 8:16], _mcf(self, "blk", 128), alog, True, True, [tk["mc"], T["gate"]], [Pt[5]])
        self.mm(P[5][:, 16:24], _mcf(self, "ind", 128), alog, True, True, [tk["mc"], T["gate"]], [Pt[5]])
        self.mm(P[5][:, 24:32], _mcf(self, "ind", 256)[:, 128:256], alog, True, True, [tk["mc"], T["gate"]], [Pt[5]])
        self.copy(gcum, P[5][:, 0:8], [Pt[5]], [T["g2"]])
        self.act(eg, P[5][:, 0:8], AF.Exp, [Pt[5]], [T["g2"]])
        self.tt(tmp8, P[5][:, 8:16], gcum, ALU.subtract, [Pt[5], T["g2"]], [T["g2"]])
        self.act(kdec, tmp8, AF.Exp, [T["g2"]], [T["g2"]])
        self.act(decS, P[5][:, 16:32], AF.Exp, [Pt[5]], [T["g2"]])
        self.tt(rhsU.rearrange("p (h t) -> p h t", h=8), bc_m(_mcf(self, "ublk", 128), 8), bc_h(alog, 128), ALU.mult,
                [tk["mc"], T["gate"], t_acc[0], t_acc[1]], [t_acc[0], t_acc[1]])
        for hh in range(2):
            self.mm(P[6 + hh][:], _mcf(self, "ones", 128), rhsU[:, hh * 512:(hh + 1) * 512], True, True,
                    [tk["mc"], t_acc[0], t_acc[1]], [Pt[6 + hh]], full=True)
        for h in range(8):
            self.stt(dec3[:, h, :], P[6 + h // 4][:, (h % 4) * 128:(h % 4 + 1) * 128], gcum[:, h:h + 1], _mcf(self, "mbias", 128),
                     ALU.subtract, ALU.add, [Pt[6 + h // 4], T["g2"], tk["mc"]], [T["decT"]])
        self.act(decT, decT, AF.Exp, [T["decT"]], [T["decT"]])
        self.tt(Kb3, Kn3, bc_h(beta, 64), ALU.mult, [T["Kn"], T["gate"]], [T["Kb"]], eng="pool")
        self.tt(Kbe3, Kb3, bc_h(eg, 64), ALU.mult, [T["Kb"], T["g2"]], [T["Kbe"]], eng="pool")
        self.tt(Kend3, Kn3, bc_h(kdec, 64), ALU.mult, [T["Kn"], T["g2"]], [T["Kend"]], eng="pool")
        self.tt(Qd3, Qn3, bc_h(eg, 64), ALU.mult, [T["Qn"], T["g2"]], [T["Qd"]], eng="pool")
        for ki, (arr, tkn) in enumerate(((Kn, "Kn"), (Kb, "Kb"), (Qn, "Qn"), (Qd, "Qd"), (Kbe, "Kbe"))):
            for pr in range(4):
                self.tr(Pb[ki][:, pr * 128:(pr + 1) * 128], arr[:, pr * 128:(pr + 1) * 128], self.ident_b(),
                        [T[tkn], self.t_cstb], [Pt[ki]])
        for ki in range(5):
            self.copy(featT[:, ki * 512:(ki + 1) * 512], Pb[ki][:, 0:512], [Pt[ki]], [T["featT"]], eng=("act" if ki % 2 == 0 else "dve"))
        for h in HORD:
            r0, pr = (h % 2) * 64, h // 2
            self.mm(P[2 + h // 4][:, (h % 4) * 128:(h % 4 + 1) * 128], F4[r0:r0 + 64, 0, pr, :], F4[r0:r0 + 64, 1, pr, :], True, True,
                    [T["featT"]], [Pt[2 + h // 4]], tkey=("T", r0, 0))
        for h in HORD:
            r0, pr = (h % 2) * 64, h // 2
            self.mm(P[4 + h // 4][:, (h % 4) * 128:(h % 4 + 1) * 128], F4[r0:r0 + 64, 0, pr, :], F4[r0:r0 + 64, 2, pr, :], True, True,
                    [T["featT"]], [Pt[4 + h // 4]], tkey=("T", r0, 0))
        for hh in range(2):
            hs = slice(hh * 512, (hh + 1) * 512)
            self.tt(Nm[:, hs], P[2 + hh][:], decT[:, hs], ALU.mult, [Pt[2 + hh], T["decT"]], [T["N"]])
            self.tt(IT[:, hs], P[4 + hh][:], decT[:, hs], ALU.mult, [Pt[4 + hh], T["decT"]], [T["IT"]])
        for h in range(8):
            self.tr(P[6 + h // 4][:, (h % 4) * 128:(h % 4 + 1) * 128], N3[:, h, :], self.ident_f(), [T["N"], self.t_cst], [Pt[6 + h // 4]])
        self.copy(Mm[:, 0:512], P[6][:], [Pt[6]], [T["M"]], eng="act")
        self.copy(Mm[:, 512:1024], P[7][:], [Pt[7]], [T["M"]], eng="act")
        self.tt(X3, N3, bc_m(_mcb(self, "nm", 128, 0), 8), ALU.mult, [T["N"], tk["mcb"]], [T["X"]], eng="pool")
        self.tt(X3, X3, ident8, ALU.add, [T["X"], self.t_cst], [T["X"]], eng="pool")
        self.tt(XT3, M3, bc_m(_mcb(self, "nmT", 128, 0), 8), ALU.mult, [T["M"], tk["mcb"]], [T["XT"]], eng="pool")
        self.tt(XT3, XT3, ident8, ALU.add, [T["XT"], self.t_cst], [T["XT"]], eng="pool")
        for li in range(1, 6):
            last = li == 5
            for h in range(8):
                self.mm(P[h // 4][:, (h % 4) * 128:(h % 4 + 1) * 128], M3[:, h, :], X3[:, h, :], True, True, [T["M"], T["X"]], [Pt[h // 4]], full=True)
            self.copy(Ym[:, 0:512], P[0][:], [Pt[0]], [T["Ym"]], eng="act")
            self.copy(Ym[:, 512:1024], P[1][:], [Pt[1]], [T["Ym"]], eng="act")
            for h in range(8):
                self.mm(P[2 + h // 4][:, (h % 4) * 128:(h % 4 + 1) * 128], XT3[:, h, :], Y3[:, h, :], True, True,
                        [T["XT"], T["Ym"]], [Pt[2 + h // 4]], full=True)
            if not last:
                for h in range(8):
                    self.mm(P[4 + h // 4][:, (h % 4) * 128:(h % 4 + 1) * 128], Y3[:, h, :], XT3[:, h, :], True, True,
                            [T["XT"], T["Ym"]], [Pt[4 + h // 4]], full=True)
            for hh in range(2):
                hs = slice(hh * 512, (hh + 1) * 512)
                self.tt(tmpW[:, hs].rearrange("p (h t) -> p h t", h=4), P[2 + hh][:].rearrange("p (h t) -> p h t", h=4),
                        bc_m(_mcb(self, "nm", 128, li * 128), 4), ALU.mult, [Pt[2 + hh], tk["mcb"], T["decT"]], [T["decT"]])
            self.tt(X, X, tmpW, ALU.add, [T["X"], T["decT"]], [T["X"]], eng="pool")
            if not last:
                for hh in range(2):
                    hs = slice(hh * 512, (hh + 1) * 512)
                    self.tt(tmpWT[:, hs].rearrange("p (h t) -> p h t", h=4), P[4 + hh][:].rearrange("p (h t) -> p h t", h=4),
                            bc_m(_mcb(self, "nmT", 128, li * 128), 4), ALU.mult, [Pt[4 + hh], tk["mcb"], t_acc[0], t_acc[1]],
                            [t_acc[0], t_acc[1]])
                self.tt(XT, XT, tmpWT, ALU.add, [T["XT"], t_acc[0], t_acc[1]], [T["XT"]], eng="pool")
        for c in range(2):
            cs = slice(c * 64, (c + 1) * 64)
            lastc = (tt == 15 and c == 1)
            if not first:
                for h in HORD:
                    r0, pr = (h % 2) * 64, h // 2
                    self.mm(P[0][cs, h * 64:(h + 1) * 64], F4[r0:r0 + 64, 4, pr, cs], Sb[r0:r0 + 64, pr * 64:(pr + 1) * 64], True, True,
                            [T["featT"], T["Sb"]], [Pt[0]], tkey=("T", r0, c * 64))
                self.tt(zz[cs, :], Vb[cs, :], P[0][cs, :], ALU.subtract, [T["Vb"], Pt[0]], [T["zz"]])
            else:
                self.copy(zz[cs, :], Vb[cs, :], [T["Vb"]], [T["zz"]])
            for h in range(8):
                self.mm(P[6][cs, h * 64:(h + 1) * 64], X3[cs, h, cs], zz[cs, h * 64:(h + 1) * 64], True, True, [T["X"], T["zz"]], [Pt[6]],
                        tkey=("T", c * 64, c * 64))
            self.copy(vnew[cs, :], P[6][cs, :], [Pt[6]], [T["vnew"]], eng="act")
            for h in range(8):
                r0, pr = (h % 2) * 64, h // 2
                if not first:
                    self.mm(P[1][cs, h * 64:(h + 1) * 64], F4[r0:r0 + 64, 3, pr, cs], Sb[r0:r0 + 64, pr * 64:(pr + 1) * 64], True, False,
                            [T["featT"], T["Sb"]], [Pt[1]])
                self.mm(P[1][cs, h * 64:(h + 1) * 64], IT3[cs, h, cs], vnew[cs, h * 64:(h + 1) * 64], first, True,
                        [T["IT"], T["vnew"]], [Pt[1]])
            if not lastc:
                for h in HORD:
                    r0, pr = (h % 2) * 64, h // 2
                    self.mm(P[2][r0:r0 + 64, pr * 64:(pr + 1) * 64], Kend[cs, h * 64:(h + 1) * 64], vnew[cs, h * 64:(h + 1) * 64],
                            True, True, [T["Kend"], T["vnew"]], [Pt[2]], tkey=("T", c * 64, r0))
                if first:
                    self.copy(Sf, P[2][:, 0:256], [Pt[2]], [T["S"]])
                else:
                    S3 = Sf.rearrange("p (i v) -> p i v", i=4)
                    dS = decS[:, c * 8:(c + 1) * 8]
                    for half in range(2):
                        ps_ = slice(half * 64, (half + 1) * 64)
                        self.tt(S3[ps_], S3[ps_], dS[ps_, half::2].unsqueeze(2).to_broadcast([64, 4, 64]), ALU.mult,
                                [T["S"], T["g2"]], [T["S"]], eng="pool")
                    self.tt(Sf, Sf, P[2][:, 0:256], ALU.add, [T["S"], Pt[2]], [T["S"]])
                self.copy(Sb, Sf, [T["S"]], [T["Sb"]], eng="act")
            first = False
        self.act(osq, P[1][:], AF.Square, [Pt[1]], [t_ct[0]])
        sc.op("dve", lambda e: e.tensor_reduce(out=ss8, in_=osq.rearrange("p (h d) -> p h d", h=8), axis=AX.X, op=ALU.add),
              [t_ct[0]], [T["nrm8"]])
        self.act(rst8, ss8, AF.Sqrt, [T["nrm8"]], [T["nrm8"]], bias=EPS, scale=1.0 / 64)
        self.recip(rst8, rst8, [T["nrm8"]], [T["nrm8"]])
        ot3 = ot.rearrange("p (h d) -> p h d", h=8)
        self.tt(ot3, P[1][:].rearrange("p (h d) -> p h d", h=8), bc_h(rst8, 64), ALU.mult, [Pt[1], T["nrm8"]], [t_ct[1]])
        self.tt(ot3, ot3, bc_m(_vec(self, "gon", 64), 8), ALU.mult, [t_ct[1], tk["mc"]], [t_ct[1]])
        self.tt(ob, ot, zs, ALU.mult, [t_ct[1], T["zs"]], [T["ob"]], eng="pool")
        _emit_out(self, 1, tt, ob, T["ob"])
    self.w_release(9)
    A0.o, A2.o = mark0, mark2


Prog.branch_B = _branch_B


def _merge(self, l):
    sc = self.sc
    P, Pt = self.P, self.Pt
    sc.barrier()
    R0b = self.R0[:].bitcast(BF16)
    outT = R0b[:, 0:12 * S].rearrange("p (j t) -> p j t", j=12)
    otok = [Tok() for _ in range(12)]
    for j in range(12):
        sc.dma("sp", self.olsem, outT[:, j, :], self.osp_d[j], writes=[otok[j]])
    for t in otok:
        t.w = (self.olsem.key, self.olsem.count)
    macc = self.R0[:, 12 * S // 2:12 * S // 2 + 2048].rearrange("p (t n) -> p t n", t=4)
    sgm = self.R0[:, 12 * S // 2 + 2048:12 * S // 2 + 3072].rearrange("p (t n) -> p t n", t=2)
    tmpm = self.R0[:, 12 * S // 2 + 3072:12 * S // 2 + 4096].rearrange("p (t n) -> p t n", t=2)
    t_macc = [Tok() for _ in range(4)]
    t_sgm = [Tok(), Tok()]
    t_tmpm = [Tok(), Tok()]
    mergedT = self.actT
    mtok = self.atok
    n = 0
    for m in range(KC):
        for g in range(3):
            (ap, wtk), = self.w_take(1)
            wg = ap[:, 0:1024].rearrange("p (k n) -> p k n", k=KC)
            wbm = ap[:, 1024:1536].rearrange("p (k n) -> p k n", k=4)
            for t in range(4):
                tsl = slice(t * 512, (t + 1) * 512)
                pg, pb, b = n % 2, 2 + n % 2, n % 2
                n += 1
                for kc in range(KC):
                    self.mm(P[pg][:], wg[:, kc, :], self.uT[:, kc, tsl], kc == 0, kc == KC - 1, [wtk, self.utok[kc][t]], [Pt[pg]], full=True)
                for k in range(4):
                    self.mm(P[pb][:], wbm[:, k, :], outT[:, g * 4 + k, tsl], k == 0, k == 3, [wtk, otok[g * 4 + k]], [Pt[pb]], full=True)
                self.act(sgm[:, b, :], P[pg][:], AF.Sigmoid, [Pt[pg]], [t_sgm[b]])
                if g == 0:
                    self.tt(macc[:, t, :], sgm[:, b, :], P[pb][:], ALU.mult, [t_sgm[b], Pt[pb]], [t_macc[t]])
                else:
                    self.tt(tmpm[:, b, :], sgm[:, b, :], P[pb][:], ALU.mult, [t_sgm[b], Pt[pb]], [t_tmpm[b]])
                    if g == 1:
                        self.tt(macc[:, t, :], macc[:, t, :], tmpm[:, b, :], ALU.add, [t_macc[t], t_tmpm[b]], [t_macc[t]], eng="pool")
                    else:
                        self.tt(mergedT[:, m, tsl], macc[:, t, :], tmpm[:, b, :], ALU.add, [t_macc[t], t_tmpm[b]], [mtok[m][t]],
                                eng="pool")
            self.w_release(1)
    self.reload_x()
    for mp in range(0, KC, 2):
        (ap, wtk), = self.w_take(1)
        wo = ap.rearrange("p (i k n) -> p i k n", i=2, k=KC)
        for i in range(2):
            mo = mp + i
            for t in range(4):
                tsl = slice(t * 512, (t + 1) * 512)
                py = 4 + (mo * 4 + t) % 2
                for kc in range(KC):
                    self.mm(P[py][:], wo[:, i, kc, :], mergedT[:, kc, tsl], kc == 0, kc == KC - 1, [wtk, mtok[kc][t]], [Pt[py]], full=True)
                xt = [self.xtok[mo][4 * t + q] for q in range(4)]
                self.stt(self.xT[:, mo, tsl], P[py][:], self.cf(l, 1, 2, mo), self.xT[:, mo, tsl], ALU.mult, ALU.add,
                         [Pt[py], self.t_coef] + xt, xt)
        self.w_release(1)


Prog.merge = _merge
```

```python
import numpy as np
from contextlib import ExitStack
import concourse.bass as bass
import concourse.mybir as mybir
from concourse.bass_utils import run_bass_kernel_spmd

F32 = mybir.dt.float32
BF16 = mybir.dt.bfloat16
AF = mybir.ActivationFunctionType
ALU = mybir.AluOpType
AX = mybir.AxisListType

S = 2048
D = 1024
L = 2
DFF = 2816
NFC = 22
KC = 8
SLAB = 2048
NSLOT = 12
CUT = 9
TR_FULL = True
TKEY_RELAX = True
HORD = [0, 2, 4, 6, 1, 3, 5, 7]
EPS = 1e-6
FGROUPS = [(0, 8), (8, 16), (16, 22)]

C_AQ, C_AK, C_AV, C_DQKV, C_DA, C_DB, C_DZ, C_RQ, C_RK, C_RV, C_RG, C_GL = (
    0, 512, 640, 768, 2304, 2312, 2320, 2832, 3088, 3344, 3856, 4368)


class Tok:
    __slots__ = ("w", "r", "name", "pf")

    def __init__(self, name=""):
        self.w = None
        self.r = {}
        self.name = name
        self.pf = False


class DmaSem:
    def __init__(self, sem, key):
        self.sem = sem
        self.key = key
        self.count = 0


class Sched:
    ENG = ("pe", "act", "dve", "pool", "sp")

    def __init__(self, nc, es):
        self.nc = nc
        self.eng = {"pe": nc.tensor, "act": nc.scalar, "dve": nc.vector, "pool": nc.gpsimd, "sp": nc.sync}
        self.sem = {e: es.enter_context(nc.semaphore("sem_" + e)) for e in self.ENG}
        self.prog = {e: [] for e in self.ENG}
        self.cnt = {e: 0 for e in self.ENG}
        self.seen = {e: {} for e in self.ENG}
        self.needed = {e: set() for e in self.ENG}
        self.dsems = {}
        self.es = es
        self.pe_cls = None

    def dma_sem(self, name):
        s = DmaSem(self.es.enter_context(self.nc.semaphore("dsem_" + name)), "dma_" + name)
        self.dsems[s.key] = s
        return s

    def _collect(self, eng, reads, writes, pe_full=False):
        waits = {}

        def need(k, s):
            if waits.get(k, -1) < s:
                waits[k] = s
        for t in reads:
            if t.w is not None:
                need(*t.w)
        for t in writes:
            if t.w is not None and not (eng == "pe" and t.w[0] == "pe" and pe_full and t.pf == pe_full):
                need(*t.w)
            for k, s in t.r.items():
                need(k, s)
        final = []
        for k, s in waits.items():
            if self.seen[eng].get(k, -1) < s:
                self.seen[eng][k] = s
                final.append((k, s))
                if k in self.needed:
                    self.needed[k].add(s)
        return final

    def op(self, eng, fn, reads=(), writes=(), pe_full=False, pe_cls=None):
        final = self._collect(eng, reads, writes, pe_full)
        if eng == "pe":
            if self.pe_cls is not None and pe_cls != self.pe_cls and self.cnt["pe"] > 0:
                s_ = self.cnt["pe"] - 1
                if self.seen["pe"].get("pe", -1) < s_:
                    self.seen["pe"]["pe"] = s_
                    final.append(("pe", s_))
                    self.needed["pe"].add(s_)
            self.pe_cls = pe_cls
        seq = self.cnt[eng]
        self.cnt[eng] += 1
        self.prog[eng].append((final, fn, seq))
        for t in reads:
            t.r[eng] = seq
        for t in writes:
            t.w = (eng, seq)
            t.r = {}
            t.pf = pe_full
        return seq

    def dma(self, q, dsem, out, in_, reads=(), writes=(), **kw):
        final = self._collect(q, reads, writes)
        dsem.count += 1
        val = dsem.count

        def fn(e, out=out, in_=in_, dsem=dsem, kw=kw):
            return e.dma_start(out=out, in_=in_, **kw).then_inc(dsem.sem, 16)
        self.prog[q].append((final, fn, None))
        for t in reads:
            t.r[dsem.key] = val
        for t in writes:
            t.w = (dsem.key, val)
            t.r = {}

    def fence(self, eng, toks):
        self.prog[eng].append((self._fence_filter(eng, toks), None, None))

    def _fence_filter(self, eng, toks):
        waits = {}
        for t in toks:
            if t.w is not None:
                waits[t.w[0]] = max(waits.get(t.w[0], -1), t.w[1])
            for k, s_ in t.r.items():
                waits[k] = max(waits.get(k, -1), s_)
        final = []
        for k, s_ in waits.items():
            if self.seen[eng].get(k, -1) < s_:
                self.seen[eng][k] = s_
                final.append((k, s_))
                if k in self.needed:
                    self.needed[k].add(s_)
        return final

    def barrier(self):
        for e in self.ENG:
            final = []
            for k in self.ENG:
                if self.cnt[k] > 0:
                    s_ = self.cnt[k] - 1
                    if self.seen[e].get(k, -1) < s_:
                        self.seen[e][k] = s_
                        final.append((k, s_))
                        self.needed[k].add(s_)
            for k, d in self.dsems.items():
                if d.count > 0 and self.seen[e].get(k, -1) < d.count:
                    self.seen[e][k] = d.count
                    final.append((k, d.count))
            self.prog[e].append((final, None, None))

    def wait_all(self, eng, toks):
        final = self._collect(eng, toks, ())
        self.prog[eng].append((final, None, None))

    def replay(self, block):
        rank = {e: {s: i + 1 for i, s in enumerate(sorted(self.needed[e]))} for e in self.ENG}

        def semval(k, s):
            if k in rank:
                return self.sem[k], rank[k][s]
            return self.dsems[k].sem, 16 * s

        def make(e_name):
            def body(e):
                for waits, fn, seq in self.prog[e_name]:
                    for k, s in waits:
                        sm, v = semval(k, s)
                        e.wait_ge(sm, v)
                    if fn is None:
                        continue
                    ins = fn(e)
                    if seq is not None and seq in rank[e_name]:
                        ins.then_inc(self.sem[e_name], 1)
            return body
        block.tensor(make("pe"))
        block.scalar(make("act"))
        block.vector(make("dve"))
        block.gpsimd(make("pool"))
        block.sync(make("sp"))


def _kslab(Wcols):
    n = Wcols.shape[1]
    out = np.zeros((128, KC, 256), np.float32)
    out[:, :, :n] = Wcols.reshape(KC, 128, n).transpose(1, 0, 2)
    return out.reshape(128, SLAB)


def _ffn_slabs(w13, w2):
    slabs = []
    for (f0, f1) in FGROUPS:
        for c in range(f0, f1):
            cols = np.concatenate([w13[:, c * 128:(c + 1) * 128], w13[:, DFF + c * 128:DFF + (c + 1) * 128]], axis=1)
            slabs.append(_kslab(cols))
        for c in range(f0, f1, 2):
            slabs.append(w2[c * 128:(c + 2) * 128, :].reshape(2, 128, D).transpose(1, 0, 2).reshape(128, SLAB))
    return slabs


def pack_layer(inp, l, stages):
    slabs = []
    wm = inp["w_mod"][l]
    for j in range(36):
        slabs.append(_kslab(wm[:, j * 256:(j + 1) * 256]))
    slabs += _ffn_slabs(inp["ffn1_w13"][l], inp["ffn1_w2"][l])
    if stages >= 2:
        win = inp["w_in"][l]
        qperm = np.concatenate([np.arange(h * 64, (h + 1) * 64) for h in (0, 4, 1, 5, 2, 6, 3, 7)])
        winA = np.concatenate([win[:, qperm], win[:, 512:768]], axis=1)
        for j in range(3):
            slabs.append(_kslab(winA[:, j * 256:(j + 1) * 256]))
        for j in range(6):
            slabs.append(_kslab(win[:, C_RQ + j * 256:C_RQ + (j + 1) * 256]))
        for j in range(6):
            slabs.append(_kslab(win[:, C_DQKV + j * 256:C_DQKV + (j + 1) * 256]))
        for j in range(2):
            slabs.append(_kslab(win[:, C_DZ + j * 256:C_DZ + (j + 1) * 256]))
        slabs.append(_kslab(win[:, C_DA:C_DA + 16]))
        wb = inp["w_branch"][l]
        for m in range(8):
            for g in range(3):
                sl = np.zeros((128, SLAB), np.float32)
                gc = win[:, C_GL + g * D + m * 128:C_GL + g * D + (m + 1) * 128]
                sl[:, 0:1024] = gc.reshape(KC, 128, 128).transpose(1, 0, 2).reshape(128, 1024)
                bc = wb[g][:, m * 128:(m + 1) * 128]
                sl[:, 1024:1536] = bc.reshape(4, 128, 128).transpose(1, 0, 2).reshape(128, 512)
                slabs.append(sl)
        wo = inp["w_out"][l]
        for m in range(0, 8, 2):
            sl = np.zeros((128, SLAB), np.float32)
            for i in range(2):
                oc = wo[:, (m + i) * 128:(m + i + 1) * 128]
                sl[:, i * 1024:(i + 1) * 1024] = oc.reshape(KC, 128, 128).transpose(1, 0, 2).reshape(128, 1024)
            slabs.append(sl)
    if stages >= 3:
        slabs += _ffn_slabs(inp["ffn2_w13"][l], inp["ffn2_w2"][l])
    return slabs


class Prog:
    def __init__(self, nc, es, n_slabs, stages, nlayers, dbg=False):
        self.nc = nc
        self.es = es
        self.stages = stages
        self.nlayers = nlayers
        sc = self.sc = Sched(nc, es)
        dt = nc.dram_tensor
        self.x_d = dt("x", [S, D], F32, kind="ExternalInput").ap()
        self.w_d = dt("wst", [n_slabs, 128, SLAB], F32, kind="ExternalInput").ap()
        self.cT_d = dt("cT", [128, KC], F32, kind="ExternalInput").ap()
        self.bmod_d = dt("bmod", [128, L * 72], F32, kind="ExternalInput").ap()
        self.gains_d = dt("gains", [128, L * 3 * KC], F32, kind="ExternalInput").ap()
        self.cst_d = dt("cst", [128, CST_N], F32, kind="ExternalInput").ap()
        self.y_d = dt("y", [S, D], F32, kind="ExternalOutput").ap()
        self.mcf_d = dt("mcf", [128, MCF_N], F32, kind="ExternalInput").ap()
        self.mcb_d = dt("mcb", [128, MCB_N], F32, kind="ExternalInput").ap()
        self.vec_d = dt("vec", [128, L * VEC_N], F32, kind="ExternalInput").ap()
        self.cw_d = dt("cw", [128, L * 4 * 1536], F32, kind="ExternalInput").ap()
        self.xsp_d = dt("xsp", [128, KC * S], F32).ap()
        self.osp_d = dt("osp", [12, 128, S], BF16, kind=("ExternalOutput" if dbg else "Internal")).ap()
        self.dbg = dbg

        def sb(name, shape, dtype):
            return es.enter_context(nc.sbuf_tensor(name, shape, dtype))

        def ps(name):
            return es.enter_context(nc.psum_tensor(name, [128, 512], F32))
        self.R0 = sb("R0", [128, KC * S], F32)
        self.R1 = sb("R1", [128, KC * S], BF16)
        self.R2 = sb("R2", [128, KC * S], BF16)
        self.ring = sb("ring", [128, NSLOT * SLAB], BF16)
        self.R3 = sb("R3", [128, 2048], F32)
        self.cst = sb("cstsb", [128, CST_N], F32)
        self.cstb = sb("cstb", [128, CSTB_N], BF16)
        self.condT = sb("condT", [128, KC], BF16)
        self.condf = sb("condf", [128, KC], F32)
        self.modrow = sb("modrow", [1, 2 * 512], F32)
        self.bmodT = sb("bmodT", [128, L * 72], F32)
        self.modT = sb("modT", [128, L * 72], F32)
        self.gains = sb("gains_sb", [128, L * 3 * KC], F32)
        self.coef = sb("coef", [128, L * 3 * 3 * KC], F32)
        self.rstd = sb("rstd", [128, 2 * 512], F32)
        self.tmpA = sb("tmpA", [128, 4 * 512], F32)
        self.sqb = sb("sqb", [128, 4 * 512], BF16)
        self.P = [ps("ps%d" % i) for i in range(8)]
        self.Pt = [Tok("ps%d" % i) for i in range(8)]
        self.xT = self.R0[:].rearrange("p (c t) -> p c t", c=KC)
        self.uT = self.R1[:].rearrange("p (c t) -> p c t", c=KC)
        self.actT = self.R2[:].rearrange("p (c t) -> p c t", c=KC)
        self.xtok = [[Tok() for _ in range(16)] for _ in range(KC)]
        self.utok = [[Tok() for _ in range(4)] for _ in range(KC)]
        self.atok = [[Tok() for _ in range(4)] for _ in range(KC)]
        self.t_cst = Tok("cst")
        self.t_cstb = Tok("cstb")
        self.t_mod = Tok("mod")
        self.t_modrow = Tok("modrow")
        self.t_mr = [Tok(), Tok()]
        self.t_coef = Tok("coef")
        self.t_rstd = [Tok(), Tok()]
        self.t_tmpA = [Tok() for _ in range(4)]
        self.t_sqb = [Tok() for _ in range(4)]
        self.t_misc = Tok("misc")
        self.csem = sc.dma_sem("c")
        self.xsem = [sc.dma_sem("x0"), sc.dma_sem("x1")]
        self.osem = [sc.dma_sem("o%d" % i) for i in range(4)]
        self.spsem = sc.dma_sem("sp")
        self.mcsem = sc.dma_sem("mc")
        self.mcbsem = sc.dma_sem("mcb")
        self.cwsem = sc.dma_sem("cw")
        self.olsem = sc.dma_sem("ol")
        self.otsem = [sc.dma_sem("ot0"), sc.dma_sem("ot1")]
        self.rlsem = sc.dma_sem("rl")
        self.n_slabs = n_slabs
        self.w_issue = 0
        self.w_use = 0
        self.w_released = 0
        self.slot_tok = [Tok("slot%d" % i) for i in range(NSLOT)]
        self.slot_sem = [sc.dma_sem("w%d" % i) for i in range(NSLOT)]
        self.rr = 0

    def slot_ap(self, i):
        s = i % NSLOT
        return self.ring[:, s * SLAB:(s + 1) * SLAB]

    def w_pump(self):
        while self.w_issue < self.n_slabs and self.w_issue < self.w_released + NSLOT:
            i = self.w_issue
            s = i % NSLOT
            self.sc.dma("pool", self.slot_sem[s], self.slot_ap(i), self.w_d[i], writes=[self.slot_tok[s]])
            self.w_issue += 1

    def w_take(self, n):
        out = []
        for _ in range(n):
            i = self.w_use
            assert i < self.w_issue, "weight stream underflow (ring too small for resident set)"
            out.append((self.slot_ap(i), self.slot_tok[i % NSLOT]))
            self.w_use += 1
        return out

    def w_release(self, n):
        self.w_released += n
        self.w_pump()

    def mm(self, out, lhsT, rhs, start, stop, reads, writes, full=False, tkey=None):
        self.sc.op("pe", lambda e: e.matmul(out, lhsT=lhsT, rhs=rhs, start=start, stop=stop), reads, writes,
                   pe_full=(True if full else (tkey if (tkey is not None and TKEY_RELAX) else False)),
                   pe_cls=("f" if lhsT.dtype == F32 else "b"))

    def tr(self, out, in_, ident, reads, writes):
        self.sc.op("pe", lambda e: e.transpose(out, in_, ident), reads, writes, pe_full=TR_FULL,
                   pe_cls=("f" if in_.dtype == F32 else "b"))

    def act(self, out, in_, func, reads, writes, bias=None, scale=None, accum_out=None):
        kw = {}
        if bias is not None:
            kw["bias"] = bias
        if scale is not None:
            kw["scale"] = scale
        if accum_out is not None:
            kw["accum_out"] = accum_out
        self.sc.op("act", lambda e: e.activation(out=out, in_=in_, func=func, **kw), reads, writes)

    def tt(self, out, in0, in1, op, reads, writes, eng="dve"):
        self.sc.op(eng, lambda e: e.tensor_tensor(out=out, in0=in0, in1=in1, op=op), reads, writes)

    def ts(self, out, in0, s1, s2, op0, op1, reads, writes, eng="dve"):
        if op1 is None:
            self.sc.op(eng, lambda e: e.tensor_scalar(out=out, in0=in0, scalar1=s1, scalar2=None, op0=op0), reads, writes)
        else:
            self.sc.op(eng, lambda e: e.tensor_scalar(out=out, in0=in0, scalar1=s1, scalar2=s2, op0=op0, op1=op1), reads, writes)

    def stt(self, out, in0, scalar, in1, op0, op1, reads, writes):
        self.sc.op("dve", lambda e: e.scalar_tensor_tensor(out=out, in0=in0, scalar=scalar, in1=in1, op0=op0, op1=op1), reads, writes)

    def copy(self, out, in_, reads, writes, eng="dve"):
        if eng == "act":
            self.sc.op("act", lambda e: e.copy(out=out, in_=in_), reads, writes)
        else:
            self.sc.op(eng, lambda e: e.tensor_copy(out=out, in_=in_), reads, writes)

    def recip(self, out, in_, reads, writes):
        self.sc.op("dve", lambda e: e.reciprocal(out=out, in_=in_), reads, writes)

    def memset(self, ap, val, writes, eng="dve"):
        self.sc.op(eng, lambda e: e.memset(ap, val), (), writes)

    def load_consts(self):
        sc = self.sc
        sc.dma("sp", self.csem, self.cst[:], self.cst_d, writes=[self.t_cst])
        sc.dma("sp", self.csem, self.condf[:], self.cT_d, writes=[self.t_misc])
        sc.dma("sp", self.csem, self.bmodT[:], self.bmod_d, writes=[self.t_modrow])
        sc.dma("sp", self.csem, self.gains[:], self.gains_d, writes=[self.t_coef])
        for t in (self.t_cst, self.t_misc, self.t_modrow, self.t_coef):
            t.w = (self.csem.key, self.csem.count)
        self.copy(self.cstb[:], self.cst[:, 0:CSTB_N], [self.t_cst], [self.t_cstb])
        self.act(self.condT[:], self.condf[:], AF.Silu, [self.t_misc], [self.t_misc])

    def ident_f(self):
        return self.cst[:, CO_IDENT:CO_IDENT + 128]

    def ident_b(self):
        return self.cstb[:, CO_IDENT:CO_IDENT + 128]

    def onesdiv_b(self):
        return self.cstb[:, CO_ONESDIV:CO_ONESDIV + 128]

    def load_x(self):
        stg = self.R2[:].bitcast(F32)
        stok = [Tok(), Tok()]
        for tt in range(16):
            b = tt % 2
            st = stg[:, b * D:(b + 1) * D]
            self.sc.dma("sp", self.xsem[b], st, self.x_d[tt * 128:(tt + 1) * 128, :], writes=[stok[b]])
            for h in range(2):
                pi = 6 + h
                for q in range(4):
                    c = h * 4 + q
                    self.tr(self.P[pi][:, q * 128:(q + 1) * 128], st[:, c * 128:(c + 1) * 128], self.ident_f(),
                            [stok[b], self.t_cst], [self.Pt[pi]])
                dst = self.xT[:, h * 4:(h + 1) * 4, tt * 128:(tt + 1) * 128]
                src = self.P[pi][:].rearrange("p (q t) -> p q t", q=4)
                self.copy(dst, src, [self.Pt[pi]], [self.xtok[h * 4 + q][tt] for q in range(4)],
                          eng=("act" if h == 0 else "dve"))

    def store_x(self):
        stg = self.R2[:].bitcast(F32)
        stok = [Tok() for _ in range(4)]
        for tt in range(16):
            b = tt % 4
            st = stg[:, b * D:(b + 1) * D]
            for h in range(2):
                pi = 6 + h
                for q in range(4):
                    c = h * 4 + q
                    self.tr(self.P[pi][:, q * 128:(q + 1) * 128], self.xT[:, c, tt * 128:(tt + 1) * 128], self.ident_f(),
                            [self.xtok[c][tt], self.t_cst], [self.Pt[pi]])
                self.copy(st[:, h * 512:(h + 1) * 512], self.P[pi][:], [self.Pt[pi]], [stok[b]],
                          eng=("act" if h == 0 else "dve"))
            self.sc.dma("sp", self.osem[b], self.y_d[tt * 128:(tt + 1) * 128, :], st, reads=[stok[b]])
        self.sc.wait_all("sp", stok)

    def compute_mod(self, l):
        for j2 in range(18):
            slabs = self.w_take(2)
            pi = 6 + (j2 % 2)
            one = self.cst[0:1, CO_IDENT:CO_IDENT + 1]
            for h in range(2):
                ap, tk = slabs[h]
                w = ap.rearrange("p (k n) -> p k n", k=KC)
                for kc in range(KC):
                    self.mm(self.P[pi][0:1, h * 256:(h + 1) * 256], self.condT[:, kc:kc + 1], w[:, kc, :],
                            kc == 0, kc == KC - 1, [tk, self.t_misc], [self.Pt[pi]])
            self.w_release(2)
            mr = self.modrow[0:1, (j2 % 2) * 512:(j2 % 2 + 1) * 512]
            self.copy(mr, self.P[pi][0:1, :], [self.Pt[pi]], [self.t_mr[j2 % 2]])
            for q4 in range(4):
                q = j2 * 4 + q4
                self.mm(self.P[4][:, q:q + 1], mr[0:1, q4 * 128:(q4 + 1) * 128], one, True, True,
                        [self.t_mr[j2 % 2], self.t_cst], [self.Pt[4]])
        self.tt(self.modT[:, l * 72:(l + 1) * 72], self.P[4][:, 0:72], self.bmodT[:, l * 72:(l + 1) * 72], ALU.add,
                [self.Pt[4], self.t_modrow], [self.t_mod])
        for i in range(3):
            base = (l * 3 + i) * 3 * KC
            m0 = l * 72 + (3 * i) * KC
            self.stt(self.coef[:, base:base + KC], self.modT[:, m0 + KC:m0 + 2 * KC], 1.0,
                     self.gains[:, (l * 3 + i) * KC:(l * 3 + i + 1) * KC], ALU.add, ALU.mult,
                     [self.t_mod, self.t_coef], [self.t_coef])
            self.copy(self.coef[:, base + KC:base + 2 * KC], self.modT[:, m0:m0 + KC], [self.t_mod], [self.t_coef])
            self.ts(self.coef[:, base + 2 * KC:base + 3 * KC], self.modT[:, m0 + 2 * KC:m0 + 3 * KC],
                    0.5 if i != 1 else 1.0, None, ALU.mult, None, [self.t_mod], [self.t_coef])

    def cf(self, l, i, k, c):
        o = (l * 3 + i) * 3 * KC + k * KC + c
        return self.coef[:, o:o + 1]

    def modulate(self, l, i):
        for t in range(4):
            tsl = slice(t * 512, (t + 1) * 512)
            xt = [self.xtok[0][0]]
            for c in range(KC):
                b = c % 4
                rd = [self.xtok[c][4 * t + q] for q in range(4)]
                self.act(self.sqb[:, b * 512:(b + 1) * 512], self.xT[:, c, tsl], AF.Square, rd, [self.t_sqb[b]])
                self.mm(self.P[5][:], self.onesdiv_b(), self.sqb[:, b * 512:(b + 1) * 512], c == 0, c == KC - 1,
                        [self.t_sqb[b], self.t_cstb], [self.Pt[5]], full=True)
            r = t % 2
            rs = self.rstd[:, r * 512:(r + 1) * 512]
            self.act(rs, self.P[5][:], AF.Sqrt, [self.Pt[5]], [self.t_rstd[r]], bias=EPS)
            self.recip(rs, rs, [self.t_rstd[r]], [self.t_rstd[r]])
            for c in range(KC):
                b = c % 4
                rd = [self.xtok[c][4 * t + q] for q in range(4)]
                tmp = self.tmpA[:, b * 512:(b + 1) * 512]
                self.tt(tmp, self.xT[:, c, tsl], rs, ALU.mult, rd + [self.t_rstd[r]], [self.t_tmpA[b]])
                self.act(self.uT[:, c, tsl], tmp, AF.Identity, [self.t_tmpA[b], self.t_coef], [self.utok[c][t]],
                         bias=self.cf(l, i, 1, c), scale=self.cf(l, i, 0, c))

    def ffn(self, l, i):
        sg = self.tmpA
        for (f0, f1) in FGROUPS:
            nf = f1 - f0
            w13 = self.w_take(nf)
            for cl in range(nf):
                ap, tk = w13[cl]
                w = ap.rearrange("p (k n) -> p k n", k=KC)
                for t in range(4):
                    tsl = slice(t * 512, (t + 1) * 512)
                    pg = (cl * 4 + t) % 2
                    pu = 2 + pg
                    for kc in range(KC):
                        self.mm(self.P[pg][:], w[:, kc, 0:128], self.uT[:, kc, tsl], kc == 0, kc == KC - 1,
                                [tk, self.utok[kc][t]], [self.Pt[pg]], full=True)
                    for kc in range(KC):
                        self.mm(self.P[pu][:], w[:, kc, 128:256], self.uT[:, kc, tsl], kc == 0, kc == KC - 1,
                                [tk, self.utok[kc][t]], [self.Pt[pu]], full=True)
                    b = (cl * 4 + t) % 4
                    sgt = sg[:, b * 512:(b + 1) * 512]
                    self.act(sgt, self.P[pg][:], AF.Silu, [self.Pt[pg]], [self.t_tmpA[b]])
                    self.tt(self.actT[:, cl, tsl], sgt, self.P[pu][:], ALU.mult, [self.t_tmpA[b], self.Pt[pu]],
                            [self.atok[cl][t]])
            self.w_release(nf)
            w2 = self.w_take(nf // 2)
            for m in range(KC):
                for t in range(4):
                    tsl = slice(t * 512, (t + 1) * 512)
                    py = 4 + (m * 4 + t) % 2
                    for cl in range(nf):
                        ap, tk = w2[cl // 2]
                        w = ap.rearrange("p (f n) -> p f n", f=2)
                        self.mm(self.P[py][:], w[:, cl % 2, m * 128:(m + 1) * 128], self.actT[:, cl, tsl],
                                cl == 0, cl == nf - 1, [tk, self.atok[cl][t]], [self.Pt[py]], full=True)
                    xt = [self.xtok[m][4 * t + q] for q in range(4)]
                    self.stt(self.xT[:, m, tsl], self.P[py][:], self.cf(l, i, 2, m), self.xT[:, m, tsl],
                             ALU.mult, ALU.add, [self.Pt[py], self.t_coef] + xt, xt)
            self.w_release(nf // 2)


CO_IDENT = 0
CO_ONESDIV = 128
CSTB_N = 256
CST_N = 256


def make_consts():
    c = np.zeros((128, CST_N), np.float32)
    c[:, CO_IDENT:CO_IDENT + 128] = np.eye(128, dtype=np.float32)
    c[:, CO_ONESDIV:CO_ONESDIV + 128] = 1.0 / D
    return c


def build_program(n_slabs, stages=3, nlayers=L, dbg=None):
    nc = bass.Bass("TRN2", target_bir_lowering=False)
    with ExitStack() as es:
        pg = Prog(nc, es, n_slabs, stages, nlayers, dbg=dbg)
        blk = es.enter_context(nc.Block())
        pg.w_pump()
        pg.load_consts()
        pg.load_x()
        for l in range(nlayers):
            pg.compute_mod(l)
            pg.modulate(l, 0)
            pg.ffn(l, 0)
            if stages >= 2:
                pg.mixer(l)
            if stages >= 3:
                pg.modulate(l, 2)
                pg.ffn(l, 2)
        pg.store_x()
        assert pg.w_use == n_slabs, (pg.w_use, n_slabs)
        pg.sc.replay(blk)
    return nc


def prepare_inputs(inp, stages=3, nlayers=L):
    slabs = []
    for l in range(nlayers):
        slabs += pack_layer(inp, l, stages)
    wst = np.ascontiguousarray(np.stack(slabs, axis=0))
    gains = np.stack([inp["ffn1_norm"], inp["mix_norm"], inp["ffn2_norm"]], axis=1)
    gains = np.ascontiguousarray(gains.reshape(L, 3, KC, 128).transpose(3, 0, 1, 2).reshape(128, L * 3 * KC))
    bmod = np.ascontiguousarray(inp["b_mod"].reshape(L, 72, 128).transpose(2, 0, 1).reshape(128, L * 72))
    cst = make_consts()
    mcf, mcb = make_mixer_consts()
    vec, cw = make_vecs(inp)
    maps = []
    for b in range(8):
        maps.append({
            "x": np.ascontiguousarray(inp["x"][b]),
            "wst": wst,
            "cT": np.ascontiguousarray(inp["c"][b].reshape(KC, 128).T),
            "bmod": bmod,
            "gains": gains,
            "cst": cst,
            "mcf": mcf, "mcb": mcb, "vec": vec, "cw": cw,
        })
    return maps, wst.shape[0]


def kernel(**inputs):
    inp = {k: np.asarray(v, dtype=np.float32) for k, v in inputs.items()}
    maps, n_slabs = prepare_inputs(inp)
    nc = build_program(n_slabs)
    res = run_bass_kernel_spmd(nc, maps, core_ids=list(range(8)))
    return np.stack([res.results[b]["y"] for b in range(8)], axis=0).astype(np.float32)


def _layout(items):
    off, o = {}, 0
    for k, n in items:
        off[k] = o
        o += n
    return off, o


MCF, MCF_N = _layout([("cosA", 128), ("sinA", 128), ("cosC", 512), ("sinC", 512), ("decC", 512), ("gqC", 256),
                      ("kendC", 4), ("gSC", 2), ("ublk", 128), ("blk", 128), ("ind", 256), ("mbias", 128), ("ones", 128)])
MCB, MCB_N = _layout([("mcur", 128), ("mprev", 128), ("shc", 384), ("shp", 384), ("nm", 768), ("nmT", 768)])
VEC, VEC_N = _layout([("gq", 64), ("gk", 64), ("sink", 8), ("alog", 8), ("dtb", 8), ("gon", 64)])
LEVELS = [1, 2, 4, 8, 16, 32]


def make_mixer_consts():
    f = np.zeros((128, MCF_N), np.float32)
    b = np.zeros((128, MCB_N), np.float32)
    pos = np.arange(S, dtype=np.float32)
    inv = (np.float32(500000.0) ** (-np.arange(0, 16, 2, dtype=np.float32) / np.float32(16))).astype(np.float32)
    ph = (pos[:, None] * inv[None, :]).astype(np.float32)
    f[:, MCF["cosA"]:MCF["cosA"] + 128] = np.cos(ph).astype(np.float32).reshape(16, 128, 8).transpose(1, 0, 2).reshape(128, 128)
    f[:, MCF["sinA"]:MCF["sinA"] + 128] = np.sin(ph).astype(np.float32).reshape(16, 128, 8).transpose(1, 0, 2).reshape(128, 128)
    ang = (1.0 / (np.float32(10000.0) ** np.linspace(0.0, 1.0, 32, dtype=np.float32))).astype(np.float32)
    phc = (pos[:, None] * ang[None, :]).astype(np.float32)
    f[:, MCF["cosC"]:MCF["cosC"] + 512] = np.cos(phc).astype(np.float32).reshape(16, 128, 32).transpose(1, 0, 2).reshape(128, 512)
    f[:, MCF["sinC"]:MCF["sinC"] + 512] = np.sin(phc).astype(np.float32).reshape(16, 128, 32).transpose(1, 0, 2).reshape(128, 512)
    lg = np.log1p(-np.exp2(-5.0 - np.arange(4, dtype=np.float64)))
    j = np.arange(128)[:, None].astype(np.float64)
    i = np.arange(128)[None, :].astype(np.float64)
    dec = np.zeros((128, 4, 128), np.float64)
    for h in range(4):
        dec[:, h, :] = np.where(i >= j, np.exp(lg[h] * (i - j)), 0.0) * 0.125
    f[:, MCF["decC"]:MCF["decC"] + 512] = dec.reshape(128, 512)
    gq = np.zeros((128, 2, 128), np.float64)
    for p in range(128):
        for pr in range(2):
            h = 2 * pr + p // 64
            gq[p, pr, :] = np.exp(lg[h] * (np.arange(128) + 1.0))
    f[:, MCF["gqC"]:MCF["gqC"] + 256] = gq.reshape(128, 256)
    for h in range(4):
        f[:, MCF["kendC"] + h] = 0.125 * np.exp(lg[h] * (127.0 - np.arange(128)))
    for pr in range(2):
        for p in range(128):
            f[p, MCF["gSC"] + pr] = np.exp(lg[2 * pr + p // 64] * 128.0)
    t = np.arange(128)[:, None]
    m = np.arange(128)[None, :]
    same = (t // 64) == (m // 64)
    f[:, MCF["ublk"]:MCF["ublk"] + 128] = ((t <= m) & same)
    f[:, MCF["blk"]:MCF["blk"] + 128] = same
    f[:, MCF["ind"]:MCF["ind"] + 128] = (t < 64) * np.ones((1, 128))
    f[:, MCF["ind"] + 128:MCF["ind"] + 256] = (t >= 64) * np.ones((1, 128))
    f[:, MCF["mbias"]:MCF["mbias"] + 128] = np.where((m >= t) & same, 0.0, -30000.0)
    f[:, MCF["ones"]:MCF["ones"] + 128] = 1.0
    k = np.arange(128)[:, None]
    q = np.arange(128)[None, :]
    b[:, MCB["mcur"]:MCB["mcur"] + 128] = (k <= q)
    b[:, MCB["mprev"]:MCB["mprev"] + 128] = (k > q)
    for s_ in (1, 2, 3):
        b[:, MCB["shc"] + (s_ - 1) * 128:MCB["shc"] + s_ * 128] = (t == m - s_)
        b[:, MCB["shp"] + (s_ - 1) * 128:MCB["shp"] + s_ * 128] = (t == 128 + m - s_)
    for li, s_ in enumerate(LEVELS):
        msk = ((t // (2 * s_)) == (m // (2 * s_))) & ((t % (2 * s_)) < s_) & ((m % (2 * s_)) >= s_)
        b[:, MCB["nm"] + li * 128:MCB["nm"] + (li + 1) * 128] = -1.0 * msk
        b[:, MCB["nmT"] + li * 128:MCB["nmT"] + (li + 1) * 128] = -1.0 * msk.T
    return f, b


def make_vecs(inp):
    v = np.zeros((L, 128, VEC_N), np.float32)
    for l in range(L):
        for name, key in (("gq", "attn_q_norm"), ("gk", "attn_k_norm"), ("sink", "attn_sinks"), ("alog", "dn_a_log"),
                          ("dtb", "dn_dt_bias"), ("gon", "dn_out_norm")):
            a = inp[key][l]
            v[l, :, VEC[name]:VEC[name] + a.shape[0]] = a[None, :]
    cw = np.ascontiguousarray(np.broadcast_to(inp["dn_conv"].reshape(L, 1, 4 * 1536), (L, 128, 4 * 1536)))
    return np.ascontiguousarray(v.transpose(1, 0, 2).reshape(128, L * VEC_N)), np.ascontiguousarray(
        cw.transpose(1, 0, 2).reshape(128, L * 4 * 1536))


class Arena:
    def __init__(self, f32view, b16view, nbytes):
        self.f = f32view
        self.b = b16view
        self.n = nbytes
        self.o = 0

    def f32(self, n):
        self.o = (self.o + 3) // 4 * 4
        o = self.o
        self.o += 4 * n
        assert self.o <= self.n, ("arena overflow", self.o, self.n)
        return self.f[:, o // 4:o // 4 + n]

    def b16(self, n):
        self.o = (self.o + 3) // 4 * 4
        o = self.o
        self.o += 2 * n
        assert self.o <= self.n, ("arena overflow", self.o, self.n)
        return self.b[:, o // 2:o // 2 + n]


def _mixer_setup(self, l):
    sc = self.sc
    for c in range(KC):
        sc.dma("sp", self.spsem, self.xsp_d[:, c * S:(c + 1) * S], self.xT[:, c, :],
               reads=[self.xtok[c][tt] for tt in range(16)])
    sc.barrier()
    A0 = Arena(self.R0[:], self.R0[:].bitcast(BF16), 4 * KC * S)
    A2 = Arena(self.R2[:].bitcast(F32), self.R2[:], 2 * KC * S)
    self.A0, self.A2 = A0, A2
    m = self.m = {}
    tk = self.mt = {}
    m["mcf"] = A0.f32(MCF_N)
    m["mcb"] = A0.b16(MCB_N)
    m["vec"] = A0.f32(VEC_N)
    m["esink"] = A0.f32(8)
    m["nA"] = A0.f32(8)
    tk["mc"] = Tok("mc")
    tk["mcb"] = Tok("mcb")
    sc.dma("sp", self.mcsem, m["mcf"], self.mcf_d, writes=[tk["mc"]])
    sc.dma("sp", self.mcsem, m["vec"], self.vec_d[:, l * VEC_N:(l + 1) * VEC_N], writes=[tk["mc"]])
    for h_ in range(2):
        hs = slice(h_ * (MCB_N // 2), (h_ + 1) * (MCB_N // 2))
        sc.dma("pool", self.mcbsem, m["mcb"][:, hs], self.mcb_d[:, hs], writes=[tk["mcb"]])
    m["oT"] = [A0.b16(512), A0.b16(512)]
    tk["oT"] = [Tok(), Tok()]
    tk["mc"].w = (self.mcsem.key, self.mcsem.count)
    tk["der"] = Tok("der")
    self.act(m["esink"], m["vec"][:, VEC["sink"]:VEC["sink"] + 8], AF.Exp, [tk["mc"]], [tk["der"]])
    self.act(m["nA"], m["vec"][:, VEC["alog"]:VEC["alog"] + 8], AF.Exp, [tk["mc"]], [tk["der"]])
    self.ts(m["nA"], m["nA"], -1.0, None, ALU.mult, None, [tk["der"]], [tk["der"]])
    self.Pb = [p[:].bitcast(BF16) for p in self.P]


def _mcf(self, name, n):
    return self.m["mcf"][:, MCF[name]:MCF[name] + n]


def _mcb(self, name, n, off=0):
    return self.m["mcb"][:, MCB[name] + off:MCB[name] + off + n]


def _vec(self, name, n):
    return self.m["vec"][:, VEC[name]:VEC[name] + n]


def _emit_out(self, g, tt, ob, t_ob, bank=2):
    par = tt % 2
    for k in range(4):
        self.tr(self.Pb[bank][:, k * 128:(k + 1) * 128], ob[:, k * 128:(k + 1) * 128], self.ident_b(),
                [t_ob, self.t_cstb], [self.Pt[bank]])
    oT = self.m["oT"][par]
    self.copy(oT, self.Pb[bank][:, 0:512], [self.Pt[bank]], [self.mt["oT"][par]], eng="act")
    dst = self.osp_d[g * 4:(g + 1) * 4, :, tt * 128:(tt + 1) * 128].rearrange("k p t -> p k t")
    self.sc.dma("sp", self.otsem[par], dst, oT.rearrange("p (k t) -> p k t", k=4), reads=[self.mt["oT"][par]])


def _branch_A(self, l):
    m, tk, A2 = self.m, self.mt, self.A2
    P, Pt, Pb = self.P, self.Pt, self.Pb
    mark = A2.o
    sq = A2.f32(640)
    qn = A2.f32(640)
    rt = A2.f32(4 * 80)
    qb = A2.b16(640)
    ss = A2.f32(10)
    rst = A2.f32(10)
    qT = A2.b16(512)
    kT = [A2.b16(128), A2.b16(128)]
    vaug = [A2.b16(130), A2.b16(130)]
    E = [[A2.b16(512) for _ in range(2)] for _ in range(2)]
    den = A2.f32(4)
    oa = A2.b16(512)
    t_sq, t_qn, t_rt, t_qb, t_ss, t_qT, t_den, t_oa = (Tok() for _ in range(8))
    t_kT = [Tok(), Tok()]
    t_v = [Tok(), Tok()]
    t_E = [[Tok(), Tok()], [Tok(), Tok()]]
    for par in range(2):
        self.memset(vaug[par], 1.0, [t_v[par]])
    slabs = self.w_take(3)
    qn3 = qn.rearrange("p (h d) -> p h d", d=64)
    qb3 = qb.rearrange("p (h d) -> p h d", d=64)
    rt4 = rt.rearrange("p (a h d) -> p a h d", a=4, h=10)
    for tt in range(16):
        tsl = slice(tt * 128, (tt + 1) * 128)
        par = tt % 2
        for j in range(3):
            ap, wtk = slabs[j]
            w = ap.rearrange("p (k n) -> p k n", k=KC)
            dst = P[0][:, j * 256:(j + 1) * 256] if j < 2 else P[1][:, 0:256]
            for kc in range(KC):
                self.mm(dst, self.uT[:, kc, tsl], w[:, kc, :], kc == 0, kc == KC - 1,
                        [wtk, self.utok[kc][tt // 4]], [Pt[0] if j < 2 else Pt[1]], full=True)
        self.act(sq[:, 0:512], P[0][:], AF.Square, [Pt[0]], [t_sq])
        self.act(sq[:, 512:640], P[1][:, 0:128], AF.Square, [Pt[1]], [t_sq])
        self.sc.op("dve", lambda e: e.tensor_reduce(out=ss, in_=sq.rearrange("p (h d) -> p h d", d=64), axis=AX.X, op=ALU.add),
                   [t_sq], [t_ss])
        self.act(rst, ss, AF.Sqrt, [t_ss], [t_ss], bias=EPS, scale=1.0 / 64)
        self.recip(rst, rst, [t_ss], [t_ss])
        self.tt(qn3[:, 0:8, :], P[0][:].rearrange("p (h d) -> p h d", d=64), rst[:, 0:8].unsqueeze(2).to_broadcast([128, 8, 64]),
                ALU.mult, [Pt[0], t_ss], [t_qn])
        self.tt(qn3[:, 8:10, :], P[1][:, 0:128].rearrange("p (h d) -> p h d", d=64),
                rst[:, 8:10].unsqueeze(2).to_broadcast([128, 2, 64]), ALU.mult, [Pt[1], t_ss], [t_qn])
        self.tt(qn3[:, 0:8, :], qn3[:, 0:8, :], _vec(self, "gq", 64).unsqueeze(1).to_broadcast([128, 8, 64]), ALU.mult,
                [t_qn, tk["mc"]], [t_qn])
        self.tt(qn3[:, 8:10, :], qn3[:, 8:10, :], _vec(self, "gk", 64).unsqueeze(1).to_broadcast([128, 2, 64]), ALU.mult,
                [t_qn, tk["mc"]], [t_qn])
        self.copy(vaug[par].rearrange("p (g d) -> p g d", g=2)[:, :, 0:64], P[1][:, 128:256].rearrange("p (g d) -> p g d", g=2),
                  [Pt[1]], [t_v[par]], eng="act")
        cos = _mcf(self, "cosA", 128)[:, tt * 8:(tt + 1) * 8].unsqueeze(1).to_broadcast([128, 10, 8])
        sin = _mcf(self, "sinA", 128)[:, tt * 8:(tt + 1) * 8].unsqueeze(1).to_broadcast([128, 10, 8])
        x1, x2 = qn3[:, :, 0:8], qn3[:, :, 8:16]
        self.tt(rt4[:, 0], x1, cos, ALU.mult, [t_qn, tk["mc"]], [t_rt], eng="pool")
        self.tt(rt4[:, 1], x2, sin, ALU.mult, [t_qn, tk["mc"]], [t_rt], eng="pool")
        self.tt(rt4[:, 2], x2, cos, ALU.mult, [t_qn, tk["mc"]], [t_rt], eng="pool")
        self.tt(rt4[:, 3], x1, sin, ALU.mult, [t_qn, tk["mc"]], [t_rt], eng="pool")
        self.copy(qb, qn, [t_qn], [t_qb], eng="act")
        self.tt(qb3[:, :, 0:8], rt4[:, 0], rt4[:, 1], ALU.subtract, [t_rt, t_qb], [t_qb], eng="pool")
        self.tt(qb3[:, :, 8:16], rt4[:, 2], rt4[:, 3], ALU.add, [t_rt, t_qb], [t_qb], eng="pool")
        for i in range(4):
            self.tr(Pb[2][:, i * 128:(i + 1) * 128], qb[:, i * 128:(i + 1) * 128], self.ident_b(), [t_qb, self.t_cstb], [Pt[2]])
        self.tr(Pb[1][:, 0:128], qb[:, 512:640], self.ident_b(), [t_qb, self.t_cstb], [Pt[1]])
        self.copy(qT, Pb[2][:, 0:512], [Pt[2]], [t_qT], eng="act")
        self.copy(kT[par], Pb[1][:, 0:128], [Pt[1]], [t_kT[par]], eng="dve")
        blocks = [(1, par)] if tt == 0 else [(0, 1 - par), (1, par)]
        for g in range(2):
            for (jj, kp) in blocks:
                bank = 3 + g * 2 + jj
                self.mm(P[bank][:], kT[kp][g * 64:(g + 1) * 64, :], qT[g * 64:(g + 1) * 64, :], True, True,
                        [t_kT[kp], t_qT], [Pt[bank]])
                self.act(E[g][jj], P[bank][:], AF.Exp, [Pt[bank]], [t_E[g][jj]], scale=0.125)
                msk = _mcb(self, "mcur" if jj == 1 else "mprev", 128).unsqueeze(1).to_broadcast([128, 4, 128])
                e3 = E[g][jj].rearrange("p (i q) -> p i q", i=4)
                self.tt(e3, e3, msk, ALU.mult, [t_E[g][jj], tk["mcb"]], [t_E[g][jj]], eng="pool")
            for i in range(4):
                for n_, (jj, kp) in enumerate(blocks):
                    self.mm(P[7][:, i * 65:(i + 1) * 65], E[g][jj][:, i * 128:(i + 1) * 128],
                            vaug[kp][:, g * 65:(g + 1) * 65], n_ == 0, n_ == len(blocks) - 1,
                            [t_E[g][jj], t_v[kp]], [Pt[7]], full=True)
            p3 = P[7][:, 0:260].rearrange("p (i d) -> p i d", i=4)
            self.tt(den, p3[:, :, 64], m["esink"][:, g * 4:(g + 1) * 4], ALU.add, [Pt[7], tk["der"]], [t_den])
            self.recip(den, den, [t_den], [t_den])
            self.tt(oa[:, g * 256:(g + 1) * 256].rearrange("p (i d) -> p i d", i=4), p3[:, :, 0:64],
                    den.unsqueeze(2).to_broadcast([128, 4, 64]), ALU.mult, [Pt[7], t_den], [t_oa])
        _emit_out(self, 0, tt, oa, t_oa, bank=7)
        yield
    self.w_release(3)
    A2.o = mark


Prog.mixer_setup = _mixer_setup
Prog.branch_A = _branch_A


def _run_AC(self, l, doA=True, doC=True):
    mark = self.A2.o
    mark0 = self.A0.o
    ga = self.branch_A(l) if doA else None
    if not doA:
        self.w_take(3)
    gc = self.branch_C(l) if doC else None
    if not doC:
        self.w_take(6)
    for tt in range(16):
        if ga is not None:
            next(ga)
        if gc is not None:
            next(gc)
    for g_ in (ga, gc):
        if g_ is not None:
            for _ in g_:
                pass
    if not doA:
        self.w_release(3)
    if not doC:
        self.w_release(6)
    self.A2.o = mark
    self.A0.o = mark0


def _mixer(self, l):
    sc = self.sc
    self.modulate(l, 1)
    self.mixer_setup(l)
    dbg = self.dbg
    _run_AC(self, l, (not dbg or "A" in dbg), (not dbg or "C" in dbg))
    if not dbg or "B" in dbg:
        self.branch_B(l)
    else:
        self.w_take(9); self.w_release(9)
    if dbg:
        for _ in range(28):
            self.w_take(1); self.w_release(1)
        self.reload_x()
        return
    self.merge(l)


def _reload_x(self):
    sc = self.sc
    sc.barrier()
    for c in range(KC):
        sc.dma("sp", self.rlsem, self.xT[:, c, :], self.xsp_d[:, c * S:(c + 1) * S],
               writes=[self.xtok[c][tt] for tt in range(16)])
    for row in self.xtok:
        for t in row:
            t.w = (self.rlsem.key, self.rlsem.count)


Prog.mixer = _mixer
Prog.reload_x = _reload_x


def _branch_C(self, l):
    m, tk, A2 = self.m, self.mt, self.A2
    P, Pt, Pb = self.P, self.Pt, self.Pb
    mark = A2.o
    rt = self.A0.f32(4 * 256)
    qkr = A2.b16(512)
    qkT = A2.b16(512)
    vt = A2.b16(512)
    ST = A2.b16(512)
    qdT = A2.b16(256)
    kend = A2.b16(256)
    Sf = A2.f32(256)
    Stmp = A2.f32(256)
    Sb = A2.b16(256)
    osq = A2.f32(512)
    sg = A2.f32(512)
    ot = A2.f32(512)
    oc = A2.b16(512)
    ssC = A2.f32(4)
    rstC = A2.f32(4)
    t_rt, t_qkr, t_qkT, t_vt, t_ST, t_qdT, t_kend, t_S, t_Stmp, t_Sb, t_osq, t_sg, t_ot, t_oc, t_ss = (Tok() for _ in range(15))
    slabs = self.w_take(6)
    rt4 = rt.rearrange("p (a h m) -> p a h m", a=4, h=8)
    qkr4 = qkr.rearrange("p (h m two) -> p h m two", h=8, two=2)
    for tt in range(16):
        tsl = slice(tt * 128, (tt + 1) * 128)
        for j in range(6):
            ap, wtk = slabs[j]
            w = ap.rearrange("p (k n) -> p k n", k=KC)
            bank = (0, 0, 1, 1, 6, 6)[j]
            dst = P[bank][:, (j % 2) * 256:(j % 2 + 1) * 256]
            for kc in range(KC):
                self.mm(dst, self.uT[:, kc, tsl], w[:, kc, :], kc == 0, kc == KC - 1, [wtk, self.utok[kc][tt // 4]], [Pt[bank]], full=True)
        v4 = P[0][:].rearrange("p (h m two) -> p h m two", h=8, two=2)
        xe, xo = v4[:, :, :, 0], v4[:, :, :, 1]
        cos = _mcf(self, "cosC", 512)[:, tt * 32:(tt + 1) * 32].unsqueeze(1).to_broadcast([128, 8, 32])
        sin = _mcf(self, "sinC", 512)[:, tt * 32:(tt + 1) * 32].unsqueeze(1).to_broadcast([128, 8, 32])
        self.tt(rt4[:, 0], xe, cos, ALU.mult, [Pt[0], tk["mc"]], [t_rt])
        self.tt(rt4[:, 1], xo, sin, ALU.mult, [Pt[0], tk["mc"]], [t_rt])
        self.tt(rt4[:, 2], xo, cos, ALU.mult, [Pt[0], tk["mc"]], [t_rt])
        self.tt(rt4[:, 3], xe, sin, ALU.mult, [Pt[0], tk["mc"]], [t_rt])
        self.tt(qkr4[:, :, :, 0], rt4[:, 0], rt4[:, 1], ALU.subtract, [t_rt], [t_qkr], eng="pool")
        self.tt(qkr4[:, :, :, 1], rt4[:, 2], rt4[:, 3], ALU.add, [t_rt], [t_qkr], eng="pool")
        self.copy(vt, P[1][:], [Pt[1]], [t_vt], eng="act")
        for i in range(4):
            self.tr(Pb[2][:, i * 128:(i + 1) * 128], qkr[:, i * 128:(i + 1) * 128], self.ident_b(), [t_qkr, self.t_cstb], [Pt[2]])
        self.copy(qkT, Pb[2][:, 0:512], [Pt[2]], [t_qkT], eng="act")
        qkT3 = qkT.rearrange("p (i t) -> p i t", i=4)
        for h in (0, 2, 1, 3):
            r0 = (h % 2) * 64
            self.mm(P[3][:, h * 128:(h + 1) * 128], qkT3[r0:r0 + 64, 2 + h // 2, :], qkT3[r0:r0 + 64, h // 2, :], True, True,
                    [t_qkT], [Pt[3]], tkey=("T", r0, 0))
        self.tt(ST, P[3][:], _mcf(self, "decC", 512), ALU.mult, [Pt[3], tk["mc"]], [t_ST])
        if tt > 0:
            self.tt(qdT, qkT[:, 0:256], _mcf(self, "gqC", 256), ALU.mult, [t_qkT, tk["mc"]], [t_qdT], eng="pool")
        qdT3 = qdT.rearrange("p (i t) -> p i t", i=2)
        Sb3 = Sb.rearrange("p (i v) -> p i v", i=2)
        for h in range(4):
            r0 = (h % 2) * 64
            self.mm(P[4][:, h * 128:(h + 1) * 128], ST[:, h * 128:(h + 1) * 128], vt[:, h * 128:(h + 1) * 128], True, tt == 0,
                    [t_ST, t_vt], [Pt[4]])
            if tt > 0:
                self.mm(P[4][:, h * 128:(h + 1) * 128], qdT3[r0:r0 + 64, h // 2, :], Sb3[r0:r0 + 64, h // 2, :], False, True,
                        [t_qdT, t_Sb], [Pt[4]])
        if tt < 15:
            self.tt(kend.rearrange("p (h d) -> p h d", h=4), qkr[:, 256:512].rearrange("p (h d) -> p h d", h=4),
                    _mcf(self, "kendC", 4).unsqueeze(2).to_broadcast([128, 4, 64]), ALU.mult, [t_qkr, tk["mc"]], [t_kend], eng="pool")
            for h in (0, 2, 1, 3):
                r0 = (h % 2) * 64
                self.mm(P[5][r0:r0 + 64, (h // 2) * 128:(h // 2 + 1) * 128], kend[:, h * 64:(h + 1) * 64],
                        vt[:, h * 128:(h + 1) * 128], True, True, [t_kend, t_vt], [Pt[5]], tkey=("T", 0, r0))
            if tt == 0:
                self.copy(Sf, P[5][:, 0:256], [Pt[5]], [t_S], eng="dve")
            else:
                self.tt(Stmp.rearrange("p (i v) -> p i v", i=2), Sf.rearrange("p (i v) -> p i v", i=2),
                        _mcf(self, "gSC", 2).unsqueeze(2).to_broadcast([128, 2, 128]), ALU.mult, [t_S, tk["mc"]], [t_Stmp], eng="pool")
                self.tt(Sf, Stmp, P[5][:, 0:256], ALU.add, [t_Stmp, Pt[5]], [t_S])
            self.copy(Sb, Sf, [t_S], [t_Sb], eng="act")
        self.act(osq, P[4][:], AF.Square, [Pt[4]], [t_osq])
        self.sc.op("dve", lambda e: e.tensor_reduce(out=ssC, in_=osq.rearrange("p (h d) -> p h d", h=4), axis=AX.X, op=ALU.add),
                   [t_osq], [t_ss])
        self.act(rstC, ssC, AF.Sqrt, [t_ss], [t_ss], bias=EPS, scale=1.0 / 128)
        self.recip(rstC, rstC, [t_ss], [t_ss])
        self.act(sg, P[6][:], AF.Silu, [Pt[6]], [t_sg])
        self.tt(ot.rearrange("p (h d) -> p h d", h=4), P[4][:].rearrange("p (h d) -> p h d", h=4),
                rstC.unsqueeze(2).to_broadcast([128, 4, 128]), ALU.mult, [Pt[4], t_ss], [t_ot])
        self.tt(oc, ot, sg, ALU.mult, [t_ot, t_sg], [t_oc], eng="pool")
        _emit_out(self, 2, tt, oc, t_oc, bank=2)
        yield
    self.w_release(6)


Prog.branch_C = _branch_C


def _branch_B(self, l):
    m, tk = self.m, self.mt
    P, Pt, Pb = self.P, self.Pt, self.Pb
    A0, A2 = self.A0, self.A2
    mark0, mark2 = A0.o, A2.o
    sc = self.sc

    def bc_h(ap8, n):
        return ap8.unsqueeze(2).to_broadcast([128, 8, n])

    def bc_m(ap, h):
        return ap.unsqueeze(1).to_broadcast([128, h, ap.shape[1]])

    cw = A2.b16(4 * 1536)
    xc = [A2.b16(1536), A2.b16(1536)]
    acc = A2.f32(1536)
    ctmp = [A2.f32(512), A2.f32(512)]
    zs = A2.f32(512)
    Vb = A2.f32(512)
    A3 = Arena(self.R3[:], self.R3[:].bitcast(BF16), 8192)
    qkf = A0.f32(1024)
    Nm = qkf
    sqf = A0.b16(1024)
    IT = sqf
    Kn, Qn, Kb, Kbe, Kend, Qd = (A0.b16(512) for _ in range(6))
    featT = A0.b16(2560)
    decT = A0.f32(1024)
    tmpW = A0.f32(1024)
    tW = [Tok(), Tok()]
    X, XT = A0.f32(1024), A0.f32(1024)
    Mm, Ym = A3.f32(1024), A3.f32(1024)
    zz = A0.f32(512)
    vnew = A0.b16(512)
    Sf = A0.f32(256)
    Sb = A0.b16(256)
    ob = A0.b16(512)
    sm = A0.f32(16 * 12)
    xa, ax, ee, lp, alog, beta, gcum, eg, kdec, tmp8 = (sm[:, i * 8:(i + 1) * 8] for i in range(10))
    decS = sm[:, 80:96]
    ss16 = sm[:, 96:112]
    rst16 = sm[:, 112:128]
    ss8 = sm[:, 128:136]
    rst8 = sm[:, 136:144]
    rhsU = acc[:, 0:1024]
    tmpWT = acc[:, 0:1024]
    osq, ot = ctmp[0], ctmp[1]
    T = {k: Tok(k) for k in ("cw", "zs", "qkf", "sqf", "Kn", "Qn", "Kb", "Kbe", "Kend", "Vb", "Qd", "featT", "decT", "M",
                             "X", "XT", "Ym", "zz", "vnew", "S", "Sb", "ob", "gate", "g2", "nrm", "nrm8")}
    T["N"] = T["qkf"]
    T["IT"] = T["sqf"]
    t_xc = [Tok(), Tok()]
    t_acc = [Tok(), Tok(), Tok()]
    t_ct = [Tok(), Tok()]
    for j in range(4):
        sc.dma("pool", self.cwsem, cw[:, j * 1536:(j + 1) * 1536], self.cw_d[:, l * 6144 + j * 1536:l * 6144 + (j + 1) * 1536],
               writes=[T["cw"]])
    T["cw"].w = (self.cwsem.key, self.cwsem.count)
    slabs = self.w_take(9)
    ident8 = bc_m(self.ident_f(), 8)
    Kn3, Qn3, Kb3, Kbe3, Kend3, Vb3, Qd3 = (a.rearrange("p (h d) -> p h d", h=8) for a in (Kn, Qn, Kb, Kbe, Kend, Vb, Qd))
    F4 = featT.rearrange("p (k i t) -> p k i t", k=5, i=4)
    N3, M3, X3, XT3, Y3, IT3 = (a.rearrange("p (h t) -> p h t", h=8) for a in (Nm, Mm, X, XT, Ym, IT))
    dec3 = decT.rearrange("p (h t) -> p h t", h=8)
    first = True
    self.bdbg = dict(cw=cw, acc=acc, qkf=qkf, Kn=Kn, Qn=Qn, Kb=Kb, Kbe=Kbe, Kend=Kend, Vb=Vb, Qd=Qd, featT=featT, decT=decT, N=Nm, M=Mm,
                     X=X, XT=XT, Ym=Ym, IT=IT, zz=zz, vnew=vnew, Sf=Sf, Sb=Sb, sm=sm, zs=zs, xc0=xc[0])
    for tt in range(16):
        tsl = slice(tt * 128, (tt + 1) * 128)
        par = tt % 2
        for j in range(9):
            ap, wtk = slabs[j]
            w = ap.rearrange("p (k n) -> p k n", k=KC)
            bank = (0, 0, 1, 1, 2, 2, 3, 3, 4)[j]
            dst = P[bank][:, (j % 2) * 256:(j % 2 + 1) * 256]
            for kc in range(KC):
                self.mm(dst, self.uT[:, kc, tsl], w[:, kc, :], kc == 0, kc == KC - 1, [wtk, self.utok[kc][tt // 4]], [Pt[bank]], full=True)
        self.act(zs, P[3][:], AF.Silu, [Pt[3]], [T["zs"]])
        self.tt(xa, P[4][:, 0:8], _vec(self, "dtb", 8), ALU.add, [Pt[4], tk["mc"]], [T["gate"]])
        self.act(ax, xa, AF.Abs, [T["gate"]], [T["gate"]])
        self.act(ee, ax, AF.Exp, [T["gate"]], [T["gate"]], scale=-1.0)
        self.act(lp, ee, AF.Ln, [T["gate"]], [T["gate"]], bias=1.0)
        self.ts(xa, xa, 0.0, None, ALU.max, None, [T["gate"]], [T["gate"]])
        self.tt(xa, xa, lp, ALU.add, [T["gate"]], [T["gate"]])
        self.tt(alog, xa, m["nA"], ALU.mult, [T["gate"], tk["der"]], [T["gate"]])
        self.act(beta, P[4][:, 8:16], AF.Sigmoid, [Pt[4]], [T["gate"]])
        for ct in range(3):
            self.copy(xc[par][:, ct * 512:(ct + 1) * 512], P[ct][:], [Pt[ct]], [t_xc[par]], eng="act")
        for ct in range(3):
            csl = slice(ct * 512, (ct + 1) * 512)
            for s_ in (1, 2, 3):
                bank = 4 + s_
                self.mm(P[bank][:], _mcb(self, "shc", 128, (s_ - 1) * 128), xc[par][:, csl], True, tt == 0,
                        [tk["mcb"], t_xc[par]], [Pt[bank]], full=True)
                if tt > 0:
                    self.mm(P[bank][:], _mcb(self, "shp", 128, (s_ - 1) * 128), xc[1 - par][:, csl], False, True,
                            [tk["mcb"], t_xc[1 - par]], [Pt[bank]], full=True)
            self.tt(acc[:, csl], P[ct][:], cw[:, 3 * 1536 + ct * 512:3 * 1536 + (ct + 1) * 512], ALU.mult,
                    [Pt[ct], T["cw"]], [t_acc[ct]])
            for s_ in (1, 2, 3):
                b_ = s_ % 2
                self.tt(ctmp[b_], P[4 + s_][:], cw[:, (3 - s_) * 1536 + ct * 512:(3 - s_) * 1536 + (ct + 1) * 512], ALU.mult,
                        [Pt[4 + s_], T["cw"]], [t_ct[b_]])
                self.tt(acc[:, csl], acc[:, csl], ctmp[b_], ALU.add, [t_acc[ct], t_ct[b_]], [t_acc[ct]], eng="pool")
        self.act(qkf, acc[:, 0:1024], AF.Silu, [t_acc[0], t_acc[1]], [T["qkf"]])
        self.act(ctmp[0], acc[:, 1024:1536], AF.Silu, [t_acc[2]], [t_ct[0]])
        self.tt(Vb3, ctmp[0].rearrange("p (h d) -> p h d", h=8), bc_h(beta, 64), ALU.mult, [t_ct[0], T["gate"]], [T["Vb"]])
        self.act(sqf, qkf, AF.Square, [T["qkf"]], [T["sqf"]])
        sc.op("dve", lambda e: e.tensor_reduce(out=ss16, in_=sqf.rearrange("p (h d) -> p h d", h=16), axis=AX.X, op=ALU.add),
              [T["sqf"]], [T["nrm"]])
        self.act(rst16, ss16, AF.Sqrt, [T["nrm"]], [T["nrm"]], bias=EPS)
        self.recip(rst16, rst16, [T["nrm"]], [T["nrm"]])
        self.ts(rst16[:, 0:8], rst16[:, 0:8], 0.125, None, ALU.mult, None, [T["nrm"]], [T["nrm"]])
        self.tt(Qn3, qkf[:, 0:512].rearrange("p (h d) -> p h d", h=8), bc_h(rst16[:, 0:8], 64), ALU.mult, [T["qkf"], T["nrm"]], [T["Qn"]])
        self.tt(Kn3, qkf[:, 512:1024].rearrange("p (h d) -> p h d", h=8), bc_h(rst16[:, 8:16], 64), ALU.mult,
                [T["qkf"], T["nrm"]], [T["Kn"]])
        self.mm(P[5][:, 0:8], _mcf(self, "ublk", 128), alog, True, True, [tk["mc"], T["gate"]], [Pt[5]])
        self.mm(P[5][:, 8:16], _mcf(self, "blk", 128), alog, True, True, [tk["mc"], T["gate"]], [Pt[5]])
        self.mm(P[5][:, 16:24], _mcf(self, "ind", 128), alog, True, True, [tk["mc"], T["gate"]], [Pt[5]])
        self.mm(P[5][:, 24:32], _mcf(self, "ind", 256)[:, 128:256], alog, True, True, [tk["mc"], T["gate"]], [Pt[5]])
        self.copy(gcum, P[5][:, 0:8], [Pt[5]], [T["g2"]])
        self.act(eg, P[5][:, 0:8], AF.Exp, [Pt[5]], [T["g2"]])
        self.tt(tmp8, P[5][:, 8:16], gcum, ALU.subtract, [Pt[5], T["g2"]], [T["g2"]])
        self.act(kdec, tmp8, AF.Exp, [T["g2"]], [T["g2"]])
        self.act(decS, P[5][:, 16:32], AF.Exp, [Pt[5]], [T["g2"]])
        self.tt(rhsU.rearrange("p (h t) -> p h t", h=8), bc_m(_mcf(self, "ublk", 128), 8), bc_h(alog, 128), ALU.mult,
                [tk["mc"], T["gate"], t_acc[0], t_acc[1]], [t_acc[0], t_acc[1]])
        for hh in range(2):
            self.mm(P[6 + hh][:], _mcf(self, "ones", 128), rhsU[:, hh * 512:(hh + 1) * 512], True, True,
                    [tk["mc"], t_acc[0], t_acc[1]], [Pt[6 + hh]], full=True)
        for h in range(8):
            self.stt(dec3[:, h, :], P[6 + h // 4][:, (h % 4) * 128:(h % 4 + 1) * 128], gcum[:, h:h + 1], _mcf(self, "mbias", 128),
                     ALU.subtract, ALU.add, [Pt[6 + h // 4], T["g2"], tk["mc"]], [T["decT"]])
        self.act(decT, decT, AF.Exp, [T["decT"]], [T["decT"]])
        self.tt(Kb3, Kn3, bc_h(beta, 64), ALU.mult, [T["Kn"], T["gate"]], [T["Kb"]], eng="pool")
        self.tt(Kbe3, Kb3, bc_h(eg, 64), ALU.mult, [T["Kb"], T["g2"]], [T["Kbe"]], eng="pool")
        self.tt(Kend3, Kn3, bc_h(kdec, 64), ALU.mult, [T["Kn"], T["g2"]], [T["Kend"]], eng="pool")
        self.tt(Qd3, Qn3, bc_h(eg, 64), ALU.mult, [T["Qn"], T["g2"]], [T["Qd"]], eng="pool")
        for ki, (arr, tkn) in enumerate(((Kn, "Kn"), (Kb, "Kb"), (Qn, "Qn"), (Qd, "Qd"), (Kbe, "Kbe"))):
            for pr in range(4):
                self.tr(Pb[ki][:, pr * 128:(pr + 1) * 128], arr[:, pr * 128:(pr + 1) * 128], self.ident_b(),
                        [T[tkn], self.t_cstb], [Pt[ki]])
        for ki in range(5):
            self.copy(featT[:, ki * 512:(ki + 1) * 512], Pb[ki][:, 0:512], [Pt[ki]], [T["featT"]], eng=("act" if ki % 2 == 0 else "dve"))
        for h in HORD:
            r0, pr = (h % 2) * 64, h // 2
            self.mm(P[2 + h // 4][:, (h % 4) * 128:(h % 4 + 1) * 128], F4[r0:r0 + 64, 0, pr, :], F4[r0:r0 + 64, 1, pr, :], True, True,
                    [T["featT"]], [Pt[2 + h // 4]], tkey=("T", r0, 0))
        for h in HORD:
            r0, pr = (h % 2) * 64, h // 2
            self.mm(P[4 + h // 4][:, (h % 4) * 128:(h % 4 + 1) * 128], F4[r0:r0 + 64, 0, pr, :], F4[r0:r0 + 64, 2, pr, :], True, True,
                    [T["featT"]], [Pt[4 + h // 4]], tkey=("T", r0, 0))
        for hh in range(2):
            hs = slice(hh * 512, (hh + 1) * 512)
            self.tt(Nm[:, hs], P[2 + hh][:], decT[:, hs], ALU.mult, [Pt[2 + hh], T["decT"]], [T["N"]])
            self.tt(IT[:, hs], P[4 + hh][:], decT[:, hs], ALU.mult, [Pt[4 + hh], T["decT"]], [T["IT"]])
        for h in range(8):
            self.tr(P[6 + h // 4][:, (h % 4) * 128:(h % 4 + 1) * 128], N3[:, h, :], self.ident_f(), [T["N"], self.t_cst], [Pt[6 + h // 4]])
        self.copy(Mm[:, 0:512], P[6][:], [Pt[6]], [T["M"]], eng="act")
        self.copy(Mm[:, 512:1024], P[7][:], [Pt[7]], [T["M"]], eng="act")
        self.tt(X3, N3, bc_m(_mcb(self, "nm", 128, 0), 8), ALU.mult, [T["N"], tk["mcb"]], [T["X"]], eng="pool")
        self.tt(X3, X3, ident8, ALU.add, [T["X"], self.t_cst], [T["X"]], eng="pool")
        self.tt(XT3, M3, bc_m(_mcb(self, "nmT", 128, 0), 8), ALU.mult, [T["M"], tk["mcb"]], [T["XT"]], eng="pool")
        self.tt(XT3, XT3, ident8, ALU.add, [T["XT"], self.t_cst], [T["XT"]], eng="pool")
        for li in range(1, 6):
            last = li == 5
            for h in range(8):
                self.mm(P[h // 4][:, (h % 4) * 128:(h % 4 + 1) * 128], M3[:, h, :], X3[:, h, :], True, True, [T["M"], T["X"]], [Pt[h // 4]], full=True)
            self.copy(Ym[:, 0:512], P[0][:], [Pt[0]], [T["Ym"]], eng="act")
            self.copy(Ym[:, 512:1024], P[1][:], [Pt[1]], [T["Ym"]], eng="act")
            for h in range(8):
                self.mm(P[2 + h // 4][:, (h % 4) * 128:(h % 4 + 1) * 128], XT3[:, h, :], Y3[:, h, :], True, True,
                        [T["XT"], T["Ym"]], [Pt[2 + h // 4]], full=True)
            if not last:
                for h in range(8):
                    self.mm(P[4 + h // 4][:, (h % 4) * 128:(h % 4 + 1) * 128], Y3[:, h, :], XT3[:, h, :], True, True,
                            [T["XT"], T["Ym"]], [Pt[4 + h // 4]], full=True)
            for hh in range(2):
                hs = slice(hh * 512, (hh + 1) * 512)
                self.tt(tmpW[:, hs].rearrange("p (h t) -> p h t", h=4), P[2 + hh][:].rearrange("p (h t) -> p h t", h=4),
                        bc_m(_mcb(self, "nm", 128, li * 128), 4), ALU.mult, [Pt[2 + hh], tk["mcb"], tW[hh]], [tW[hh]])
                self.tt(X[:, hs], X[:, hs], tmpW[:, hs], ALU.add, [T["X"], tW[hh]], [T["X"]], eng="pool")
            if not last:
                for hh in range(2):
                    hs = slice(hh * 512, (hh + 1) * 512)
                    self.tt(tmpWT[:, hs].rearrange("p (h t) -> p h t", h=4), P[4 + hh][:].rearrange("p (h t) -> p h t", h=4),
                            bc_m(_mcb(self, "nmT", 128, li * 128), 4), ALU.mult, [Pt[4 + hh], tk["mcb"], t_acc[hh]],
                            [t_acc[hh]])
                    self.tt(XT[:, hs], XT[:, hs], tmpWT[:, hs], ALU.add, [T["XT"], t_acc[hh]], [T["XT"]], eng="pool")
        for c in range(2):
            cs = slice(c * 64, (c + 1) * 64)
            lastc = (tt == 15 and c == 1)
            if not first:
                for h in HORD:
                    r0, pr = (h % 2) * 64, h // 2
                    self.mm(P[0][cs, h * 64:(h + 1) * 64], F4[r0:r0 + 64, 4, pr, cs], Sb[r0:r0 + 64, pr * 64:(pr + 1) * 64], True, True,
                            [T["featT"], T["Sb"]], [Pt[0]], tkey=("T", r0, c * 64))
                self.tt(zz[cs, :], Vb[cs, :], P[0][cs, :], ALU.subtract, [T["Vb"], Pt[0]], [T["zz"]])
            else:
                self.copy(zz[cs, :], Vb[cs, :], [T["Vb"]], [T["zz"]])
            for h in range(8):
                self.mm(P[6][cs, h * 64:(h + 1) * 64], X3[cs, h, cs], zz[cs, h * 64:(h + 1) * 64], True, True, [T["X"], T["zz"]], [Pt[6]],
                        tkey=("T", c * 64, c * 64))
            self.copy(vnew[cs, :], P[6][cs, :], [Pt[6]], [T["vnew"]], eng="act")
            for h in range(8):
                r0, pr = (h % 2) * 64, h // 2
                if not first:
                    self.mm(P[1][cs, h * 64:(h + 1) * 64], F4[r0:r0 + 64, 3, pr, cs], Sb[r0:r0 + 64, pr * 64:(pr + 1) * 64], True, False,
                            [T["featT"], T["Sb"]], [Pt[1]])
                self.mm(P[1][cs, h * 64:(h + 1) * 64], IT3[cs, h, cs], vnew[cs, h * 64:(h + 1) * 64], first, True,
                        [T["IT"], T["vnew"]], [Pt[1]])
            if not lastc:
                for h in HORD:
                    r0, pr = (h % 2) * 64, h // 2
                    self.mm(P[2][r0:r0 + 64, pr * 64:(pr + 1) * 64], Kend[cs, h * 64:(h + 1) * 64], vnew[cs, h * 64:(h + 1) * 64],
                            True, True, [T["Kend"], T["vnew"]], [Pt[2]], tkey=("T", c * 64, r0))
                if first:
                    self.copy(Sf, P[2][:, 0:256], [Pt[2]], [T["S"]])
                else:
                    S3 = Sf.rearrange("p (i v) -> p i v", i=4)
                    dS = decS[:, c * 8:(c + 1) * 8]
                    for half in range(2):
                        ps_ = slice(half * 64, (half + 1) * 64)
                        self.tt(S3[ps_], S3[ps_], dS[ps_, half::2].unsqueeze(2).to_broadcast([64, 4, 64]), ALU.mult,
                                [T["S"], T["g2"]], [T["S"]], eng="pool")
                    self.tt(Sf, Sf, P[2][:, 0:256], ALU.add, [T["S"], Pt[2]], [T["S"]])
                self.copy(Sb, Sf, [T["S"]], [T["Sb"]], eng="act")
            first = False
        self.act(osq, P[1][:], AF.Square, [Pt[1]], [t_ct[0]])
        sc.op("dve", lambda e: e.tensor_reduce(out=ss8, in_=osq.rearrange("p (h d) -> p h d", h=8), axis=AX.X, op=ALU.add),
              [t_ct[0]], [T["nrm8"]])
        self.act(rst8, ss8, AF.Sqrt, [T["nrm8"]], [T["nrm8"]], bias=EPS, scale=1.0 / 64)
        self.recip(rst8, rst8, [T["nrm8"]], [T["nrm8"]])
        ot3 = ot.rearrange("p (h d) -> p h d", h=8)
        self.tt(ot3, P[1][:].rearrange("p (h d) -> p h d", h=8), bc_h(rst8, 64), ALU.mult, [Pt[1], T["nrm8"]], [t_ct[1]])
        self.tt(ot3, ot3, bc_m(_vec(self, "gon", 64), 8), ALU.mult, [t_ct[1], tk["mc"]], [t_ct[1]])
        self.tt(ob, ot, zs, ALU.mult, [t_ct[1], T["zs"]], [T["ob"]], eng="pool")
        _emit_out(self, 1, tt, ob, T["ob"])
    self.w_release(9)
    A0.o, A2.o = mark0, mark2


Prog.branch_B = _branch_B


def _merge(self, l):
    sc = self.sc
    P, Pt = self.P, self.Pt
    sc.barrier()
    R0b = self.R0[:].bitcast(BF16)
    outT = R0b[:, 0:12 * S].rearrange("p (j t) -> p j t", j=12)
    otok = [Tok() for _ in range(12)]
    for j in range(12):
        sc.dma("sp", self.olsem, outT[:, j, :], self.osp_d[j], writes=[otok[j]])
    for t in otok:
        t.w = (self.olsem.key, self.olsem.count)
    macc = self.R0[:, 12 * S // 2:12 * S // 2 + 2048].rearrange("p (t n) -> p t n", t=4)
    sgm = self.R0[:, 12 * S // 2 + 2048:12 * S // 2 + 3072].rearrange("p (t n) -> p t n", t=2)
    tmpm = self.R0[:, 12 * S // 2 + 3072:12 * S // 2 + 4096].rearrange("p (t n) -> p t n", t=2)
    t_macc = [Tok() for _ in range(4)]
    t_sgm = [Tok(), Tok()]
    t_tmpm = [Tok(), Tok()]
    mergedT = self.actT
    mtok = self.atok
    n = 0
    for m in range(KC):
        for g in range(3):
            (ap, wtk), = self.w_take(1)
            wg = ap[:, 0:1024].rearrange("p (k n) -> p k n", k=KC)
            wbm = ap[:, 1024:1536].rearrange("p (k n) -> p k n", k=4)
            for t in range(4):
                tsl = slice(t * 512, (t + 1) * 512)
                pg, pb, b = n % 2, 2 + n % 2, n % 2
                n += 1
                for kc in range(KC):
                    self.mm(P[pg][:], wg[:, kc, :], self.uT[:, kc, tsl], kc == 0, kc == KC - 1, [wtk, self.utok[kc][t]], [Pt[pg]], full=True)
                for k in range(4):
                    self.mm(P[pb][:], wbm[:, k, :], outT[:, g * 4 + k, tsl], k == 0, k == 3, [wtk, otok[g * 4 + k]], [Pt[pb]], full=True)
                self.act(sgm[:, b, :], P[pg][:], AF.Sigmoid, [Pt[pg]], [t_sgm[b]])
                if g == 0:
                    self.tt(macc[:, t, :], sgm[:, b, :], P[pb][:], ALU.mult, [t_sgm[b], Pt[pb]], [t_macc[t]])
                else:
                    self.tt(tmpm[:, b, :], sgm[:, b, :], P[pb][:], ALU.mult, [t_sgm[b], Pt[pb]], [t_tmpm[b]])
                    if g == 1:
                        self.tt(macc[:, t, :], macc[:, t, :], tmpm[:, b, :], ALU.add, [t_macc[t], t_tmpm[b]], [t_macc[t]], eng="pool")
                    else:
                        self.tt(mergedT[:, m, tsl], macc[:, t, :], tmpm[:, b, :], ALU.add, [t_macc[t], t_tmpm[b]], [mtok[m][t]],
                                eng="pool")
            self.w_release(1)
    self.reload_x()
    for mp in range(0, KC, 2):
        (ap, wtk), = self.w_take(1)
        wo = ap.rearrange("p (i k n) -> p i k n", i=2, k=KC)
        for i in range(2):
            mo = mp + i
            for t in range(4):
                tsl = slice(t * 512, (t + 1) * 512)
                py = 4 + (mo * 4 + t) % 2
                for kc in range(KC):
                    self.mm(P[py][:], wo[:, i, kc, :], mergedT[:, kc, tsl], kc == 0, kc == KC - 1, [wtk, mtok[kc][t]], [Pt[py]], full=True)
                xt = [self.xtok[mo][4 * t + q] for q in range(4)]
                self.stt(self.xT[:, mo, tsl], P[py][:], self.cf(l, 1, 2, mo), self.xT[:, mo, tsl], ALU.mult, ALU.add,
                         [Pt[py], self.t_coef] + xt, xt)
        self.w_release(1)


Prog.merge = _merge
```

```python
import numpy as np
from contextlib import ExitStack
import concourse.bass as bass
import concourse.mybir as mybir
from concourse.bass_utils import run_bass_kernel_spmd

F32 = mybir.dt.float32
BF16 = mybir.dt.bfloat16
AF = mybir.ActivationFunctionType
ALU = mybir.AluOpType
AX = mybir.AxisListType

S = 2048
D = 1024
L = 2
DFF = 2816
NFC = 22
KC = 8
SLAB = 2048
NSLOT = 12
CUT = 9
TR_FULL = True
TKEY_RELAX = True
HORD = [0, 2, 4, 6, 1, 3, 5, 7]
EPS = 1e-6
FGROUPS = [(0, 8), (8, 16), (16, 22)]

C_AQ, C_AK, C_AV, C_DQKV, C_DA, C_DB, C_DZ, C_RQ, C_RK, C_RV, C_RG, C_GL = (
    0, 512, 640, 768, 2304, 2312, 2320, 2832, 3088, 3344, 3856, 4368)


class Tok:
    __slots__ = ("w", "r", "name", "pf")

    def __init__(self, name=""):
        self.w = None
        self.r = {}
        self.name = name
        self.pf = False


class DmaSem:
    def __init__(self, sem, key):
        self.sem = sem
        self.key = key
        self.count = 0


class Sched:
    ENG = ("pe", "act", "dve", "pool", "sp")

    def __init__(self, nc, es):
        self.nc = nc
        self.eng = {"pe": nc.tensor, "act": nc.scalar, "dve": nc.vector, "pool": nc.gpsimd, "sp": nc.sync}
        self.sem = {e: es.enter_context(nc.semaphore("sem_" + e)) for e in self.ENG}
        self.prog = {e: [] for e in self.ENG}
        self.cnt = {e: 0 for e in self.ENG}
        self.seen = {e: {} for e in self.ENG}
        self.needed = {e: set() for e in self.ENG}
        self.dsems = {}
        self.es = es
        self.pe_cls = None

    def dma_sem(self, name):
        s = DmaSem(self.es.enter_context(self.nc.semaphore("dsem_" + name)), "dma_" + name)
        self.dsems[s.key] = s
        return s

    def _collect(self, eng, reads, writes, pe_full=False):
        waits = {}

        def need(k, s):
            if waits.get(k, -1) < s:
                waits[k] = s
        for t in reads:
            if t.w is not None:
                need(*t.w)
        for t in writes:
            if t.w is not None and not (eng == "pe" and t.w[0] == "pe" and pe_full and t.pf == pe_full):
                need(*t.w)
            for k, s in t.r.items():
                need(k, s)
        final = []
        for k, s in waits.items():
            if self.seen[eng].get(k, -1) < s:
                self.seen[eng][k] = s
                final.append((k, s))
                if k in self.needed:
                    self.needed[k].add(s)
        return final

    def op(self, eng, fn, reads=(), writes=(), pe_full=False, pe_cls=None):
        final = self._collect(eng, reads, writes, pe_full)
        if eng == "pe":
            if self.pe_cls is not None and pe_cls != self.pe_cls and self.cnt["pe"] > 0:
                s_ = self.cnt["pe"] - 1
                if self.seen["pe"].get("pe", -1) < s_:
                    self.seen["pe"]["pe"] = s_
                    final.append(("pe", s_))
                    self.needed["pe"].add(s_)
            self.pe_cls = pe_cls
        seq = self.cnt[eng]
        self.cnt[eng] += 1
        self.prog[eng].append((final, fn, seq))
        for t in reads:
            t.r[eng] = seq
        for t in writes:
            t.w = (eng, seq)
            t.r = {}
            t.pf = pe_full
        return seq

    def dma(self, q, dsem, out, in_, reads=(), writes=(), **kw):
        final = self._collect(q, reads, writes)
        dsem.count += 1
        val = dsem.count

        def fn(e, out=out, in_=in_, dsem=dsem, kw=kw):
            return e.dma_start(out=out, in_=in_, **kw).then_inc(dsem.sem, 16)
        self.prog[q].append((final, fn, None))
        for t in reads:
            t.r[dsem.key] = val
        for t in writes:
            t.w = (dsem.key, val)
            t.r = {}

    def fence(self, eng, toks):
        self.prog[eng].append((self._fence_filter(eng, toks), None, None))

    def _fence_filter(self, eng, toks):
        waits = {}
        for t in toks:
            if t.w is not None:
                waits[t.w[0]] = max(waits.get(t.w[0], -1), t.w[1])
            for k, s_ in t.r.items():
                waits[k] = max(waits.get(k, -1), s_)
        final = []
        for k, s_ in waits.items():
            if self.seen[eng].get(k, -1) < s_:
                self.seen[eng][k] = s_
                final.append((k, s_))
                if k in self.needed:
                    self.needed[k].add(s_)
        return final

    def barrier(self):
        for e in self.ENG:
            final = []
            for k in self.ENG:
                if self.cnt[k] > 0:
                    s_ = self.cnt[k] - 1
                    if self.seen[e].get(k, -1) < s_:
                        self.seen[e][k] = s_
                        final.append((k, s_))
                        self.needed[k].add(s_)
            for k, d in self.dsems.items():
                if d.count > 0 and self.seen[e].get(k, -1) < d.count:
                    self.seen[e][k] = d.count
                    final.append((k, d.count))
            self.prog[e].append((final, None, None))

    def wait_all(self, eng, toks):
        final = self._collect(eng, toks, ())
        self.prog[eng].append((final, None, None))

    def replay(self, block):
        rank = {e: {s: i + 1 for i, s in enumerate(sorted(self.needed[e]))} for e in self.ENG}

        def semval(k, s):
            if k in rank:
                return self.sem[k], rank[k][s]
            return self.dsems[k].sem, 16 * s

        def make(e_name):
            def body(e):
                for waits, fn, seq in self.prog[e_name]:
                    for k, s in waits:
                        sm, v = semval(k, s)
                        e.wait_ge(sm, v)
                    if fn is None:
                        continue
                    ins = fn(e)
                    if seq is not None and seq in rank[e_name]:
                        ins.then_inc(self.sem[e_name], 1)
            return body
        block.tensor(make("pe"))
        block.scalar(make("act"))
        block.vector(make("dve"))
        block.gpsimd(make("pool"))
        block.sync(make("sp"))


def _kslab(Wcols):
    n = Wcols.shape[1]
    out = np.zeros((128, KC, 256), np.float32)
    out[:, :, :n] = Wcols.reshape(KC, 128, n).transpose(1, 0, 2)
    return out.reshape(128, SLAB)


def _ffn_slabs(w13, w2):
    slabs = []
    for (f0, f1) in FGROUPS:
        for c in range(f0, f1):
            cols = np.concatenate([w13[:, c * 128:(c + 1) * 128], w13[:, DFF + c * 128:DFF + (c + 1) * 128]], axis=1)
            slabs.append(_kslab(cols))
        for c in range(f0, f1, 2):
            slabs.append(w2[c * 128:(c + 2) * 128, :].reshape(2, 128, D).transpose(1, 0, 2).reshape(128, SLAB))
    return slabs


def pack_layer(inp, l, stages):
    slabs = []
    wm = inp["w_mod"][l]
    for j in range(36):
        slabs.append(_kslab(wm[:, j * 256:(j + 1) * 256]))
    slabs += _ffn_slabs(inp["ffn1_w13"][l], inp["ffn1_w2"][l])
    if stages >= 2:
        win = inp["w_in"][l]
        qperm = np.concatenate([np.arange(h * 64, (h + 1) * 64) for h in (0, 4, 1, 5, 2, 6, 3, 7)])
        winA = np.concatenate([win[:, qperm], win[:, 512:768]], axis=1)
        for j in range(3):
            slabs.append(_kslab(winA[:, j * 256:(j + 1) * 256]))
        for j in range(6):
            slabs.append(_kslab(win[:, C_RQ + j * 256:C_RQ + (j + 1) * 256]))
        for j in range(6):
            slabs.append(_kslab(win[:, C_DQKV + j * 256:C_DQKV + (j + 1) * 256]))
        for j in range(2):
            slabs.append(_kslab(win[:, C_DZ + j * 256:C_DZ + (j + 1) * 256]))
        slabs.append(_kslab(win[:, C_DA:C_DA + 16]))
        wb = inp["w_branch"][l]
        for m in range(8):
            for g in range(3):
                sl = np.zeros((128, SLAB), np.float32)
                gc = win[:, C_GL + g * D + m * 128:C_GL + g * D + (m + 1) * 128]
                sl[:, 0:1024] = gc.reshape(KC, 128, 128).transpose(1, 0, 2).reshape(128, 1024)
                bc = wb[g][:, m * 128:(m + 1) * 128]
                sl[:, 1024:1536] = bc.reshape(4, 128, 128).transpose(1, 0, 2).reshape(128, 512)
                slabs.append(sl)
        wo = inp["w_out"][l]
        for m in range(0, 8, 2):
            sl = np.zeros((128, SLAB), np.float32)
            for i in range(2):
                oc = wo[:, (m + i) * 128:(m + i + 1) * 128]
                sl[:, i * 1024:(i + 1) * 1024] = oc.reshape(KC, 128, 128).transpose(1, 0, 2).reshape(128, 1024)
            slabs.append(sl)
    if stages >= 3:
        slabs += _ffn_slabs(inp["ffn2_w13"][l], inp["ffn2_w2"][l])
    return slabs


class Prog:
    def __init__(self, nc, es, n_slabs, stages, nlayers, dbg=False):
        self.nc = nc
        self.es = es
        self.stages = stages
        self.nlayers = nlayers
        sc = self.sc = Sched(nc, es)
        dt = nc.dram_tensor
        self.x_d = dt("x", [S, D], F32, kind="ExternalInput").ap()
        self.w_d = dt("wst", [n_slabs, 128, SLAB], F32, kind="ExternalInput").ap()
        self.cT_d = dt("cT", [128, KC], F32, kind="ExternalInput").ap()
        self.bmod_d = dt("bmod", [128, L * 72], F32, kind="ExternalInput").ap()
        self.gains_d = dt("gains", [128, L * 3 * KC], F32, kind="ExternalInput").ap()
        self.cst_d = dt("cst", [128, CST_N], F32, kind="ExternalInput").ap()
        self.y_d = dt("y", [S, D], F32, kind="ExternalOutput").ap()
        self.mcf_d = dt("mcf", [128, MCF_N], F32, kind="ExternalInput").ap()
        self.mcb_d = dt("mcb", [128, MCB_N], F32, kind="ExternalInput").ap()
        self.vec_d = dt("vec", [128, L * VEC_N], F32, kind="ExternalInput").ap()
        self.cw_d = dt("cw", [128, L * 4 * 1536], F32, kind="ExternalInput").ap()
        self.xsp_d = dt("xsp", [128, KC * S], F32).ap()
        self.osp_d = dt("osp", [12, 128, S], BF16, kind=("ExternalOutput" if dbg else "Internal")).ap()
        self.dbg = dbg

        def sb(name, shape, dtype):
            return es.enter_context(nc.sbuf_tensor(name, shape, dtype))

        def ps(name):
            return es.enter_context(nc.psum_tensor(name, [128, 512], F32))
        self.R0 = sb("R0", [128, KC * S], F32)
        self.R1 = sb("R1", [128, KC * S], BF16)
        self.R2 = sb("R2", [128, KC * S], BF16)
        self.ring = sb("ring", [128, NSLOT * SLAB], BF16)
        self.R3 = sb("R3", [128, 2048], F32)
        self.cst = sb("cstsb", [128, CST_N], F32)
        self.cstb = sb("cstb", [128, CSTB_N], BF16)
        self.condT = sb("condT", [128, KC], BF16)
        self.condf = sb("condf", [128, KC], F32)
        self.modrow = sb("modrow", [1, 2 * 512], F32)
        self.bmodT = sb("bmodT", [128, L * 72], F32)
        self.modT = sb("modT", [128, L * 72], F32)
        self.gains = sb("gains_sb", [128, L * 3 * KC], F32)
        self.coef = sb("coef", [128, L * 3 * 3 * KC], F32)
        self.rstd = sb("rstd", [128, 2 * 512], F32)
        self.tmpA = sb("tmpA", [128, 4 * 512], F32)
        self.sqb = sb("sqb", [128, 4 * 512], BF16)
        self.P = [ps("ps%d" % i) for i in range(8)]
        self.Pt = [Tok("ps%d" % i) for i in range(8)]
        self.xT = self.R0[:].rearrange("p (c t) -> p c t", c=KC)
        self.uT = self.R1[:].rearrange("p (c t) -> p c t", c=KC)
        self.actT = self.R2[:].rearrange("p (c t) -> p c t", c=KC)
        self.xtok = [[Tok() for _ in range(16)] for _ in range(KC)]
        self.utok = [[Tok() for _ in range(4)] for _ in range(KC)]
        self.atok = [[Tok() for _ in range(4)] for _ in range(KC)]
        self.t_cst = Tok("cst")
        self.t_cstb = Tok("cstb")
        self.t_mod = Tok("mod")
        self.t_modrow = Tok("modrow")
        self.t_mr = [Tok(), Tok()]
        self.t_coef = Tok("coef")
        self.t_rstd = [Tok(), Tok()]
        self.t_tmpA = [Tok() for _ in range(4)]
        self.t_sqb = [Tok() for _ in range(4)]
        self.t_misc = Tok("misc")
        self.csem = sc.dma_sem("c")
        self.xsem = [sc.dma_sem("x0"), sc.dma_sem("x1")]
        self.osem = [sc.dma_sem("o%d" % i) for i in range(4)]
        self.spsem = sc.dma_sem("sp")
        self.mcsem = sc.dma_sem("mc")
        self.mcbsem = sc.dma_sem("mcb")
        self.cwsem = sc.dma_sem("cw")
        self.olsem = sc.dma_sem("ol")
        self.otsem = [sc.dma_sem("ot0"), sc.dma_sem("ot1")]
        self.rlsem = sc.dma_sem("rl")
        self.n_slabs = n_slabs
        self.w_issue = 0
        self.w_use = 0
        self.w_released = 0
        self.slot_tok = [Tok("slot%d" % i) for i in range(NSLOT)]
        self.slot_sem = [sc.dma_sem("w%d" % i) for i in range(NSLOT)]
        self.rr = 0

    def slot_ap(self, i):
        s = i % NSLOT
        return self.ring[:, s * SLAB:(s + 1) * SLAB]

    def w_pump(self):
        while self.w_issue < self.n_slabs and self.w_issue < self.w_released + NSLOT:
            i = self.w_issue
            s = i % NSLOT
            self.sc.dma("pool", self.slot_sem[s], self.slot_ap(i), self.w_d[i], writes=[self.slot_tok[s]])
            self.w_issue += 1

    def w_take(self, n):
        out = []
        for _ in range(n):
            i = self.w_use
            assert i < self.w_issue, "weight stream underflow (ring too small for resident set)"
            out.append((self.slot_ap(i), self.slot_tok[i % NSLOT]))
            self.w_use += 1
        return out

    def w_release(self, n):
        self.w_released += n
        self.w_pump()

    def mm(self, out, lhsT, rhs, start, stop, reads, writes, full=False, tkey=None):
        self.sc.op("pe", lambda e: e.matmul(out, lhsT=lhsT, rhs=rhs, start=start, stop=stop), reads, writes,
                   pe_full=(True if full else (tkey if (tkey is not None and TKEY_RELAX) else False)),
                   pe_cls=("f" if lhsT.dtype == F32 else "b"))

    def tr(self, out, in_, ident, reads, writes):
        self.sc.op("pe", lambda e: e.transpose(out, in_, ident), reads, writes, pe_full=TR_FULL,
                   pe_cls=("f" if in_.dtype == F32 else "b"))

    def act(self, out, in_, func, reads, writes, bias=None, scale=None, accum_out=None):
        kw = {}
        if bias is not None:
            kw["bias"] = bias
        if scale is not None:
            kw["scale"] = scale
        if accum_out is not None:
            kw["accum_out"] = accum_out
        self.sc.op("act", lambda e: e.activation(out=out, in_=in_, func=func, **kw), reads, writes)

    def tt(self, out, in0, in1, op, reads, writes, eng="dve"):
        self.sc.op(eng, lambda e: e.tensor_tensor(out=out, in0=in0, in1=in1, op=op), reads, writes)

    def ts(self, out, in0, s1, s2, op0, op1, reads, writes, eng="dve"):
        if op1 is None:
            self.sc.op(eng, lambda e: e.tensor_scalar(out=out, in0=in0, scalar1=s1, scalar2=None, op0=op0), reads, writes)
        else:
            self.sc.op(eng, lambda e: e.tensor_scalar(out=out, in0=in0, scalar1=s1, scalar2=s2, op0=op0, op1=op1), reads, writes)

    def stt(self, out, in0, scalar, in1, op0, op1, reads, writes):
        self.sc.op("dve", lambda e: e.scalar_tensor_tensor(out=out, in0=in0, scalar=scalar, in1=in1, op0=op0, op1=op1), reads, writes)

    def copy(self, out, in_, reads, writes, eng="dve"):
        if eng == "act":
            self.sc.op("act", lambda e: e.copy(out=out, in_=in_), reads, writes)
        else:
            self.sc.op(eng, lambda e: e.tensor_copy(out=out, in_=in_), reads, writes)

    def recip(self, out, in_, reads, writes):
        self.sc.op("dve", lambda e: e.reciprocal(out=out, in_=in_), reads, writes)

    def memset(self, ap, val, writes, eng="dve"):
        self.sc.op(eng, lambda e: e.memset(ap, val), (), writes)

    def load_consts(self):
        sc = self.sc
        sc.dma("sp", self.csem, self.cst[:], self.cst_d, writes=[self.t_cst])
        sc.dma("sp", self.csem, self.condf[:], self.cT_d, writes=[self.t_misc])
        sc.dma("sp", self.csem, self.bmodT[:], self.bmod_d, writes=[self.t_modrow])
        sc.dma("sp", self.csem, self.gains[:], self.gains_d, writes=[self.t_coef])
        for t in (self.t_cst, self.t_misc, self.t_modrow, self.t_coef):
            t.w = (self.csem.key, self.csem.count)
        self.copy(self.cstb[:], self.cst[:, 0:CSTB_N], [self.t_cst], [self.t_cstb])
        self.act(self.condT[:], self.condf[:], AF.Silu, [self.t_misc], [self.t_misc])

    def ident_f(self):
        return self.cst[:, CO_IDENT:CO_IDENT + 128]

    def ident_b(self):
        return self.cstb[:, CO_IDENT:CO_IDENT + 128]

    def onesdiv_b(self):
        return self.cstb[:, CO_ONESDIV:CO_ONESDIV + 128]

    def load_x(self):
        stg = self.R2[:].bitcast(F32)
        stok = [Tok(), Tok()]
        for tt in range(16):
            b = tt % 2
            st = stg[:, b * D:(b + 1) * D]
            self.sc.dma("sp", self.xsem[b], st, self.x_d[tt * 128:(tt + 1) * 128, :], writes=[stok[b]])
            for h in range(2):
                pi = 6 + h
                for q in range(4):
                    c = h * 4 + q
                    self.tr(self.P[pi][:, q * 128:(q + 1) * 128], st[:, c * 128:(c + 1) * 128], self.ident_f(),
                            [stok[b], self.t_cst], [self.Pt[pi]])
                dst = self.xT[:, h * 4:(h + 1) * 4, tt * 128:(tt + 1) * 128]
                src = self.P[pi][:].rearrange("p (q t) -> p q t", q=4)
                self.copy(dst, src, [self.Pt[pi]], [self.xtok[h * 4 + q][tt] for q in range(4)],
                          eng=("act" if h == 0 else "dve"))

    def store_x(self):
        stg = self.R2[:].bitcast(F32)
        stok = [Tok() for _ in range(4)]
        for tt in range(16):
            b = tt % 4
            st = stg[:, b * D:(b + 1) * D]
            for h in range(2):
                pi = 6 + h
                for q in range(4):
                    c = h * 4 + q
                    self.tr(self.P[pi][:, q * 128:(q + 1) * 128], self.xT[:, c, tt * 128:(tt + 1) * 128], self.ident_f(),
                            [self.xtok[c][tt], self.t_cst], [self.Pt[pi]])
                self.copy(st[:, h * 512:(h + 1) * 512], self.P[pi][:], [self.Pt[pi]], [stok[b]],
                          eng=("act" if h == 0 else "dve"))
            self.sc.dma("sp", self.osem[b], self.y_d[tt * 128:(tt + 1) * 128, :], st, reads=[stok[b]])
        self.sc.wait_all("sp", stok)

    def compute_mod(self, l):
        for j2 in range(18):
            slabs = self.w_take(2)
            pi = 6 + (j2 % 2)
            one = self.cst[0:1, CO_IDENT:CO_IDENT + 1]
            for h in range(2):
                ap, tk = slabs[h]
                w = ap.rearrange("p (k n) -> p k n", k=KC)
                for kc in range(KC):
                    self.mm(self.P[pi][0:1, h * 256:(h + 1) * 256], self.condT[:, kc:kc + 1], w[:, kc, :],
                            kc == 0, kc == KC - 1, [tk, self.t_misc], [self.Pt[pi]])
            self.w_release(2)
            mr = self.modrow[0:1, (j2 % 2) * 512:(j2 % 2 + 1) * 512]
            self.copy(mr, self.P[pi][0:1, :], [self.Pt[pi]], [self.t_mr[j2 % 2]])
            for q4 in range(4):
                q = j2 * 4 + q4
                self.mm(self.P[4][:, q:q + 1], mr[0:1, q4 * 128:(q4 + 1) * 128], one, True, True,
                        [self.t_mr[j2 % 2], self.t_cst], [self.Pt[4]])
        self.tt(self.modT[:, l * 72:(l + 1) * 72], self.P[4][:, 0:72], self.bmodT[:, l * 72:(l + 1) * 72], ALU.add,
                [self.Pt[4], self.t_modrow], [self.t_mod])
        for i in range(3):
            base = (l * 3 + i) * 3 * KC
            m0 = l * 72 + (3 * i) * KC
            self.stt(self.coef[:, base:base + KC], self.modT[:, m0 + KC:m0 + 2 * KC], 1.0,
                     self.gains[:, (l * 3 + i) * KC:(l * 3 + i + 1) * KC], ALU.add, ALU.mult,
                     [self.t_mod, self.t_coef], [self.t_coef])
            self.copy(self.coef[:, base + KC:base + 2 * KC], self.modT[:, m0:m0 + KC], [self.t_mod], [self.t_coef])
            self.ts(self.coef[:, base + 2 * KC:base + 3 * KC], self.modT[:, m0 + 2 * KC:m0 + 3 * KC],
                    0.5 if i != 1 else 1.0, None, ALU.mult, None, [self.t_mod], [self.t_coef])

    def cf(self, l, i, k, c):
        o = (l * 3 + i) * 3 * KC + k * KC + c
        return self.coef[:, o:o + 1]

    def modulate(self, l, i):
        for t in range(4):
            tsl = slice(t * 512, (t + 1) * 512)
            xt = [self.xtok[0][0]]
            for c in range(KC):
                b = c % 4
                rd = [self.xtok[c][4 * t + q] for q in range(4)]
                self.act(self.sqb[:, b * 512:(b + 1) * 512], self.xT[:, c, tsl], AF.Square, rd, [self.t_sqb[b]])
                self.mm(self.P[5][:], self.onesdiv_b(), self.sqb[:, b * 512:(b + 1) * 512], c == 0, c == KC - 1,
                        [self.t_sqb[b], self.t_cstb], [self.Pt[5]], full=True)
            r = t % 2
            rs = self.rstd[:, r * 512:(r + 1) * 512]
            self.act(rs, self.P[5][:], AF.Sqrt, [self.Pt[5]], [self.t_rstd[r]], bias=EPS)
            self.recip(rs, rs, [self.t_rstd[r]], [self.t_rstd[r]])
            for c in range(KC):
                b = c % 4
                rd = [self.xtok[c][4 * t + q] for q in range(4)]
                tmp = self.tmpA[:, b * 512:(b + 1) * 512]
                self.tt(tmp, self.xT[:, c, tsl], rs, ALU.mult, rd + [self.t_rstd[r]], [self.t_tmpA[b]])
                self.act(self.uT[:, c, tsl], tmp, AF.Identity, [self.t_tmpA[b], self.t_coef], [self.utok[c][t]],
                         bias=self.cf(l, i, 1, c), scale=self.cf(l, i, 0, c))

    def ffn(self, l, i):
        sg = self.tmpA
        for (f0, f1) in FGROUPS:
            nf = f1 - f0
            w13 = self.w_take(nf)
            for cl in range(nf):
                ap, tk = w13[cl]
                w = ap.rearrange("p (k n) -> p k n", k=KC)
                for t in range(4):
                    tsl = slice(t * 512, (t + 1) * 512)
                    pg = (cl * 4 + t) % 2
                    pu = 2 + pg
                    for kc in range(KC):
                        self.mm(self.P[pg][:], w[:, kc, 0:128], self.uT[:, kc, tsl], kc == 0, kc == KC - 1,
                                [tk, self.utok[kc][t]], [self.Pt[pg]], full=True)
                    for kc in range(KC):
                        self.mm(self.P[pu][:], w[:, kc, 128:256], self.uT[:, kc, tsl], kc == 0, kc == KC - 1,
                                [tk, self.utok[kc][t]], [self.Pt[pu]], full=True)
                    b = (cl * 4 + t) % 4
                    sgt = sg[:, b * 512:(b + 1) * 512]
                    self.act(sgt, self.P[pg][:], AF.Silu, [self.Pt[pg]], [self.t_tmpA[b]])
                    self.tt(self.actT[:, cl, tsl], sgt, self.P[pu][:], ALU.mult, [self.t_tmpA[b], self.Pt[pu]],
                            [self.atok[cl][t]])
            self.w_release(nf)
            w2 = self.w_take(nf // 2)
            for m in range(KC):
                for t in range(4):
                    tsl = slice(t * 512, (t + 1) * 512)
                    py = 4 + (m * 4 + t) % 2
                    for cl in range(nf):
                        ap, tk = w2[cl // 2]
                        w = ap.rearrange("p (f n) -> p f n", f=2)
                        self.mm(self.P[py][:], w[:, cl % 2, m * 128:(m + 1) * 128], self.actT[:, cl, tsl],
                                cl == 0, cl == nf - 1, [tk, self.atok[cl][t]], [self.Pt[py]], full=True)
                    xt = [self.xtok[m][4 * t + q] for q in range(4)]
                    self.stt(self.xT[:, m, tsl], self.P[py][:], self.cf(l, i, 2, m), self.xT[:, m, tsl],
                             ALU.mult, ALU.add, [self.Pt[py], self.t_coef] + xt, xt)
            self.w_release(nf // 2)


CO_IDENT = 0
CO_ONESDIV = 128
CSTB_N = 256
CST_N = 256


def make_consts():
    c = np.zeros((128, CST_N), np.float32)
    c[:, CO_IDENT:CO_IDENT + 128] = np.eye(128, dtype=np.float32)
    c[:, CO_ONESDIV:CO_ONESDIV + 128] = 1.0 / D
    return c


def build_program(n_slabs, stages=3, nlayers=L, dbg=None):
    nc = bass.Bass("TRN2", target_bir_lowering=False)
    with ExitStack() as es:
        pg = Prog(nc, es, n_slabs, stages, nlayers, dbg=dbg)
        blk = es.enter_context(nc.Block())
        pg.w_pump()
        pg.load_consts()
        pg.load_x()
        for l in range(nlayers):
            pg.compute_mod(l)
            pg.modulate(l, 0)
            pg.ffn(l, 0)
            if stages >= 2:
                pg.mixer(l)
            if stages >= 3:
                pg.modulate(l, 2)
                pg.ffn(l, 2)
        pg.store_x()
        assert pg.w_use == n_slabs, (pg.w_use, n_slabs)
        pg.sc.replay(blk)
    return nc


def prepare_inputs(inp, stages=3, nlayers=L):
    slabs = []
    for l in range(nlayers):
        slabs += pack_layer(inp, l, stages)
    wst = np.ascontiguousarray(np.stack(slabs, axis=0))
    gains = np.stack([inp["ffn1_norm"], inp["mix_norm"], inp["ffn2_norm"]], axis=1)
    gains = np.ascontiguousarray(gains.reshape(L, 3, KC, 128).transpose(3, 0, 1, 2).reshape(128, L * 3 * KC))
    bmod = np.ascontiguousarray(inp["b_mod"].reshape(L, 72, 128).transpose(2, 0, 1).reshape(128, L * 72))
    cst = make_consts()
    mcf, mcb = make_mixer_consts()
    vec, cw = make_vecs(inp)
    maps = []
    for b in range(8):
        maps.append({
            "x": np.ascontiguousarray(inp["x"][b]),
            "wst": wst,
            "cT": np.ascontiguousarray(inp["c"][b].reshape(KC, 128).T),
            "bmod": bmod,
            "gains": gains,
            "cst": cst,
            "mcf": mcf, "mcb": mcb, "vec": vec, "cw": cw,
        })
    return maps, wst.shape[0]


def kernel(**inputs):
    inp = {k: np.asarray(v, dtype=np.float32) for k, v in inputs.items()}
    maps, n_slabs = prepare_inputs(inp)
    nc = build_program(n_slabs)
    res = run_bass_kernel_spmd(nc, maps, core_ids=list(range(8)))
    return np.stack([res.results[b]["y"] for b in range(8)], axis=0).astype(np.float32)


def _layout(items):
    off, o = {}, 0
    for k, n in items:
        off[k] = o
        o += n
    return off, o


MCF, MCF_N = _layout([("cosA", 128), ("sinA", 128), ("cosC", 512), ("sinC", 512), ("decC", 512), ("gqC", 256),
                      ("kendC", 4), ("gSC", 2), ("ublk", 128), ("blk", 128), ("ind", 256), ("mbias", 128), ("ones", 128)])
MCB, MCB_N = _layout([("mcur", 128), ("mprev", 128), ("shc", 384), ("shp", 384), ("nm", 768), ("nmT", 768)])
VEC, VEC_N = _layout([("gq", 64), ("gk", 64), ("sink", 8), ("alog", 8), ("dtb", 8), ("gon", 64)])
LEVELS = [1, 2, 4, 8, 16, 32]


def make_mixer_consts():
    f = np.zeros((128, MCF_N), np.float32)
    b = np.zeros((128, MCB_N), np.float32)
    pos = np.arange(S, dtype=np.float32)
    inv = (np.float32(500000.0) ** (-np.arange(0, 16, 2, dtype=np.float32) / np.float32(16))).astype(np.float32)
    ph = (pos[:, None] * inv[None, :]).astype(np.float32)
    f[:, MCF["cosA"]:MCF["cosA"] + 128] = np.cos(ph).astype(np.float32).reshape(16, 128, 8).transpose(1, 0, 2).reshape(128, 128)
    f[:, MCF["sinA"]:MCF["sinA"] + 128] = np.sin(ph).astype(np.float32).reshape(16, 128, 8).transpose(1, 0, 2).reshape(128, 128)
    ang = (1.0 / (np.float32(10000.0) ** np.linspace(0.0, 1.0, 32, dtype=np.float32))).astype(np.float32)
    phc = (pos[:, None] * ang[None, :]).astype(np.float32)
    f[:, MCF["cosC"]:MCF["cosC"] + 512] = np.cos(phc).astype(np.float32).reshape(16, 128, 32).transpose(1, 0, 2).reshape(128, 512)
    f[:, MCF["sinC"]:MCF["sinC"] + 512] = np.sin(phc).astype(np.float32).reshape(16, 128, 32).transpose(1, 0, 2).reshape(128, 512)
    lg = np.log1p(-np.exp2(-5.0 - np.arange(4, dtype=np.float64)))
    j = np.arange(128)[:, None].astype(np.float64)
    i = np.arange(128)[None, :].astype(np.float64)
    dec = np.zeros((128, 4, 128), np.float64)
    for h in range(4):
        dec[:, h, :] = np.where(i >= j, np.exp(lg[h] * (i - j)), 0.0) * 0.125
    f[:, MCF["decC"]:MCF["decC"] + 512] = dec.reshape(128, 512)
    gq = np.zeros((128, 2, 128), np.float64)
    for p in range(128):
        for pr in range(2):
            h = 2 * pr + p // 64
            gq[p, pr, :] = np.exp(lg[h] * (np.arange(128) + 1.0))
    f[:, MCF["gqC"]:MCF["gqC"] + 256] = gq.reshape(128, 256)
    for h in range(4):
        f[:, MCF["kendC"] + h] = 0.125 * np.exp(lg[h] * (127.0 - np.arange(128)))
    for pr in range(2):
        for p in range(128):
            f[p, MCF["gSC"] + pr] = np.exp(lg[2 * pr + p // 64] * 128.0)
    t = np.arange(128)[:, None]
    m = np.arange(128)[None, :]
    same = (t // 64) == (m // 64)
    f[:, MCF["ublk"]:MCF["ublk"] + 128] = ((t <= m) & same)
    f[:, MCF["blk"]:MCF["blk"] + 128] = same
    f[:, MCF["ind"]:MCF["ind"] + 128] = (t < 64) * np.ones((1, 128))
    f[:, MCF["ind"] + 128:MCF["ind"] + 256] = (t >= 64) * np.ones((1, 128))
    f[:, MCF["mbias"]:MCF["mbias"] + 128] = np.where((m >= t) & same, 0.0, -30000.0)
    f[:, MCF["ones"]:MCF["ones"] + 128] = 1.0
    k = np.arange(128)[:, None]
    q = np.arange(128)[None, :]
    b[:, MCB["mcur"]:MCB["mcur"] + 128] = (k <= q)
    b[:, MCB["mprev"]:MCB["mprev"] + 128] = (k > q)
    for s_ in (1, 2, 3):
        b[:, MCB["shc"] + (s_ - 1) * 128:MCB["shc"] + s_ * 128] = (t == m - s_)
        b[:, MCB["shp"] + (s_ - 1) * 128:MCB["shp"] + s_ * 128] = (t == 128 + m - s_)
    for li, s_ in enumerate(LEVELS):
        msk = ((t // (2 * s_)) == (m // (2 * s_))) & ((t % (2 * s_)) < s_) & ((m % (2 * s_)) >= s_)
        b[:, MCB["nm"] + li * 128:MCB["nm"] + (li + 1) * 128] = -1.0 * msk
        b[:, MCB["nmT"] + li * 128:MCB["nmT"] + (li + 1) * 128] = -1.0 * msk.T
    return f, b


def make_vecs(inp):
    v = np.zeros((L, 128, VEC_N), np.float32)
    for l in range(L):
        for name, key in (("gq", "attn_q_norm"), ("gk", "attn_k_norm"), ("sink", "attn_sinks"), ("alog", "dn_a_log"),
                          ("dtb", "dn_dt_bias"), ("gon", "dn_out_norm")):
            a = inp[key][l]
            v[l, :, VEC[name]:VEC[name] + a.shape[0]] = a[None, :]
    cw = np.ascontiguousarray(np.broadcast_to(inp["dn_conv"].reshape(L, 1, 4 * 1536), (L, 128, 4 * 1536)))
    return np.ascontiguousarray(v.transpose(1, 0, 2).reshape(128, L * VEC_N)), np.ascontiguousarray(
        cw.transpose(1, 0, 2).reshape(128, L * 4 * 1536))


class Arena:
    def __init__(self, f32view, b16view, nbytes):
        self.f = f32view
        self.b = b16view
        self.n = nbytes
        self.o = 0

    def f32(self, n):
        self.o = (self.o + 3) // 4 * 4
        o = self.o
        self.o += 4 * n
        assert self.o <= self.n, ("arena overflow", self.o, self.n)
        return self.f[:, o // 4:o // 4 + n]

    def b16(self, n):
        self.o = (self.o + 3) // 4 * 4
        o = self.o
        self.o += 2 * n
        assert self.o <= self.n, ("arena overflow", self.o, self.n)
        return self.b[:, o // 2:o // 2 + n]


def _mixer_setup(self, l):
    sc = self.sc
    for c in range(KC):
        sc.dma("sp", self.spsem, self.xsp_d[:, c * S:(c + 1) * S], self.xT[:, c, :],
               reads=[self.xtok[c][tt] for tt in range(16)])
    sc.barrier()
    A0 = Arena(self.R0[:], self.R0[:].bitcast(BF16), 4 * KC * S)
    A2 = Arena(self.R2[:].bitcast(F32), self.R2[:], 2 * KC * S)
    self.A0, self.A2 = A0, A2
    m = self.m = {}
    tk = self.mt = {}
    m["mcf"] = A0.f32(MCF_N)
    m["mcb"] = A0.b16(MCB_N)
    m["vec"] = A0.f32(VEC_N)
    m["esink"] = A0.f32(8)
    m["nA"] = A0.f32(8)
    tk["mc"] = Tok("mc")
    tk["mcb"] = Tok("mcb")
    sc.dma("sp", self.mcsem, m["mcf"], self.mcf_d, writes=[tk["mc"]])
    sc.dma("sp", self.mcsem, m["vec"], self.vec_d[:, l * VEC_N:(l + 1) * VEC_N], writes=[tk["mc"]])
    for h_ in range(2):
        hs = slice(h_ * (MCB_N // 2), (h_ + 1) * (MCB_N // 2))
        sc.dma("pool", self.mcbsem, m["mcb"][:, hs], self.mcb_d[:, hs], writes=[tk["mcb"]])
    m["oT"] = [A0.b16(512), A0.b16(512)]
    tk["oT"] = [Tok(), Tok()]
    tk["mc"].w = (self.mcsem.key, self.mcsem.count)
    tk["der"] = Tok("der")
    self.act(m["esink"], m["vec"][:, VEC["sink"]:VEC["sink"] + 8], AF.Exp, [tk["mc"]], [tk["der"]])
    self.act(m["nA"], m["vec"][:, VEC["alog"]:VEC["alog"] + 8], AF.Exp, [tk["mc"]], [tk["der"]])
    self.ts(m["nA"], m["nA"], -1.0, None, ALU.mult, None, [tk["der"]], [tk["der"]])
    self.Pb = [p[:].bitcast(BF16) for p in self.P]


def _mcf(self, name, n):
    return self.m["mcf"][:, MCF[name]:MCF[name] + n]


def _mcb(self, name, n, off=0):
    return self.m["mcb"][:, MCB[name] + off:MCB[name] + off + n]


def _vec(self, name, n):
    return self.m["vec"][:, VEC[name]:VEC[name] + n]


def _emit_out(self, g, tt, ob, t_ob, bank=2):
    par = tt % 2
    for k in range(4):
        self.tr(self.Pb[bank][:, k * 128:(k + 1) * 128], ob[:, k * 128:(k + 1) * 128], self.ident_b(),
                [t_ob, self.t_cstb], [self.Pt[bank]])
    oT = self.m["oT"][par]
    self.copy(oT, self.Pb[bank][:, 0:512], [self.Pt[bank]], [self.mt["oT"][par]], eng="act")
    dst = self.osp_d[g * 4:(g + 1) * 4, :, tt * 128:(tt + 1) * 128].rearrange("k p t -> p k t")
    self.sc.dma("sp", self.otsem[par], dst, oT.rearrange("p (k t) -> p k t", k=4), reads=[self.mt["oT"][par]])


def _branch_A(self, l):
    m, tk, A2 = self.m, self.mt, self.A2
    P, Pt, Pb = self.P, self.Pt, self.Pb
    mark = A2.o
    sq = A2.f32(640)
    qn = A2.f32(640)
    rt = A2.f32(4 * 80)
    qb = A2.b16(640)
    ss = A2.f32(10)
    rst = A2.f32(10)
    qT = A2.b16(512)
    kT = [A2.b16(128), A2.b16(128)]
    vaug = [A2.b16(130), A2.b16(130)]
    E = [[A2.b16(512) for _ in range(2)] for _ in range(2)]
    den = A2.f32(4)
    oa = A2.b16(512)
    t_sq, t_qn, t_rt, t_qb, t_ss, t_qT, t_den, t_oa = (Tok() for _ in range(8))
    t_kT = [Tok(), Tok()]
    t_v = [Tok(), Tok()]
    t_E = [[Tok(), Tok()], [Tok(), Tok()]]
    for par in range(2):
        self.memset(vaug[par], 1.0, [t_v[par]])
    slabs = self.w_take(3)
    qn3 = qn.rearrange("p (h d) -> p h d", d=64)
    qb3 = qb.rearrange("p (h d) -> p h d", d=64)
    rt4 = rt.rearrange("p (a h d) -> p a h d", a=4, h=10)
    for tt in range(16):
        tsl = slice(tt * 128, (tt + 1) * 128)
        par = tt % 2
        for j in range(3):
            ap, wtk = slabs[j]
            w = ap.rearrange("p (k n) -> p k n", k=KC)
            dst = P[0][:, j * 256:(j + 1) * 256] if j < 2 else P[1][:, 0:256]
            for kc in range(KC):
                self.mm(dst, self.uT[:, kc, tsl], w[:, kc, :], kc == 0, kc == KC - 1,
                        [wtk, self.utok[kc][tt // 4]], [Pt[0] if j < 2 else Pt[1]], full=True)
        self.act(sq[:, 0:512], P[0][:], AF.Square, [Pt[0]], [t_sq])
        self.act(sq[:, 512:640], P[1][:, 0:128], AF.Square, [Pt[1]], [t_sq])
        self.sc.op("dve", lambda e: e.tensor_reduce(out=ss, in_=sq.rearrange("p (h d) -> p h d", d=64), axis=AX.X, op=ALU.add),
                   [t_sq], [t_ss])
        self.act(rst, ss, AF.Sqrt, [t_ss], [t_ss], bias=EPS, scale=1.0 / 64)
        self.recip(rst, rst, [t_ss], [t_ss])
        self.tt(qn3[:, 0:8, :], P[0][:].rearrange("p (h d) -> p h d", d=64), rst[:, 0:8].unsqueeze(2).to_broadcast([128, 8, 64]),
                ALU.mult, [Pt[0], t_ss], [t_qn])
        self.tt(qn3[:, 8:10, :], P[1][:, 0:128].rearrange("p (h d) -> p h d", d=64),
                rst[:, 8:10].unsqueeze(2).to_broadcast([128, 2, 64]), ALU.mult, [Pt[1], t_ss], [t_qn])
        self.tt(qn3[:, 0:8, :], qn3[:, 0:8, :], _vec(self, "gq", 64).unsqueeze(1).to_broadcast([128, 8, 64]), ALU.mult,
                [t_qn, tk["mc"]], [t_qn])
        self.tt(qn3[:, 8:10, :], qn3[:, 8:10, :], _vec(self, "gk", 64).unsqueeze(1).to_broadcast([128, 2, 64]), ALU.mult,
                [t_qn, tk["mc"]], [t_qn])
        self.copy(vaug[par].rearrange("p (g d) -> p g d", g=2)[:, :, 0:64], P[1][:, 128:256].rearrange("p (g d) -> p g d", g=2),
                  [Pt[1]], [t_v[par]], eng="act")
        cos = _mcf(self, "cosA", 128)[:, tt * 8:(tt + 1) * 8].unsqueeze(1).to_broadcast([128, 10, 8])
        sin = _mcf(self, "sinA", 128)[:, tt * 8:(tt + 1) * 8].unsqueeze(1).to_broadcast([128, 10, 8])
        x1, x2 = qn3[:, :, 0:8], qn3[:, :, 8:16]
        self.tt(rt4[:, 0], x1, cos, ALU.mult, [t_qn, tk["mc"]], [t_rt], eng="pool")
        self.tt(rt4[:, 1], x2, sin, ALU.mult, [t_qn, tk["mc"]], [t_rt], eng="pool")
        self.tt(rt4[:, 2], x2, cos, ALU.mult, [t_qn, tk["mc"]], [t_rt], eng="pool")
        self.tt(rt4[:, 3], x1, sin, ALU.mult, [t_qn, tk["mc"]], [t_rt], eng="pool")
        self.copy(qb, qn, [t_qn], [t_qb], eng="act")
        self.tt(qb3[:, :, 0:8], rt4[:, 0], rt4[:, 1], ALU.subtract, [t_rt, t_qb], [t_qb], eng="pool")
        self.tt(qb3[:, :, 8:16], rt4[:, 2], rt4[:, 3], ALU.add, [t_rt, t_qb], [t_qb], eng="pool")
        for i in range(4):
            self.tr(Pb[2][:, i * 128:(i + 1) * 128], qb[:, i * 128:(i + 1) * 128], self.ident_b(), [t_qb, self.t_cstb], [Pt[2]])
        self.tr(Pb[1][:, 0:128], qb[:, 512:640], self.ident_b(), [t_qb, self.t_cstb], [Pt[1]])
        self.copy(qT, Pb[2][:, 0:512], [Pt[2]], [t_qT], eng="act")
        self.copy(kT[par], Pb[1][:, 0:128], [Pt[1]], [t_kT[par]], eng="dve")
        blocks = [(1, par)] if tt == 0 else [(0, 1 - par), (1, par)]
        for g in range(2):
            for (jj, kp) in blocks:
                bank = 3 + g * 2 + jj
                self.mm(P[bank][:], kT[kp][g * 64:(g + 1) * 64, :], qT[g * 64:(g + 1) * 64, :], True, True,
                        [t_kT[kp], t_qT], [Pt[bank]])
                self.act(E[g][jj], P[bank][:], AF.Exp, [Pt[bank]], [t_E[g][jj]], scale=0.125)
                msk = _mcb(self, "mcur" if jj == 1 else "mprev", 128).unsqueeze(1).to_broadcast([128, 4, 128])
                e3 = E[g][jj].rearrange("p (i q) -> p i q", i=4)
                self.tt(e3, e3, msk, ALU.mult, [t_E[g][jj], tk["mcb"]], [t_E[g][jj]], eng="pool")
            for i in range(4):
                for n_, (jj, kp) in enumerate(blocks):
                    self.mm(P[7][:, i * 65:(i + 1) * 65], E[g][jj][:, i * 128:(i + 1) * 128],
                            vaug[kp][:, g * 65:(g + 1) * 65], n_ == 0, n_ == len(blocks) - 1,
                            [t_E[g][jj], t_v[kp]], [Pt[7]], full=True)
            p3 = P[7][:, 0:260].rearrange("p (i d) -> p i d", i=4)
            self.tt(den, p3[:, :, 64], m["esink"][:, g * 4:(g + 1) * 4], ALU.add, [Pt[7], tk["der"]], [t_den])
            self.recip(den, den, [t_den], [t_den])
            self.tt(oa[:, g * 256:(g + 1) * 256].rearrange("p (i d) -> p i d", i=4), p3[:, :, 0:64],
                    den.unsqueeze(2).to_broadcast([128, 4, 64]), ALU.mult, [Pt[7], t_den], [t_oa])
        _emit_out(self, 0, tt, oa, t_oa, bank=7)
        yield
    self.w_release(3)
    A2.o = mark


Prog.mixer_setup = _mixer_setup
Prog.branch_A = _branch_A


def _run_AC(self, l, doA=True, doC=True):
    mark = self.A2.o
    mark0 = self.A0.o
    ga = self.branch_A(l) if doA else None
    if not doA:
        self.w_take(3)
    gc = self.branch_C(l) if doC else None
    if not doC:
        self.w_take(6)
    for tt in range(16):
        if ga is not None:
            next(ga)
        if gc is not None:
            next(gc)
    for g_ in (ga, gc):
        if g_ is not None:
            for _ in g_:
                pass
    if not doA:
        self.w_release(3)
    if not doC:
        self.w_release(6)
    self.A2.o = mark
    self.A0.o = mark0


def _mixer(self, l):
    sc = self.sc
    self.modulate(l, 1)
    self.mixer_setup(l)
    dbg = self.dbg
    _run_AC(self, l, (not dbg or "A" in dbg), (not dbg or "C" in dbg))
    if not dbg or "B" in dbg:
        self.branch_B(l)
    else:
        self.w_take(9); self.w_release(9)
    if dbg:
        for _ in range(28):
            self.w_take(1); self.w_release(1)
        self.reload_x()
        return
    self.merge(l)


def _reload_x(self):
    sc = self.sc
    sc.barrier()
    for c in range(KC):
        sc.dma("sp", self.rlsem, self.xT[:, c, :], self.xsp_d[:, c * S:(c + 1) * S],
               writes=[self.xtok[c][tt] for tt in range(16)])
    for row in self.xtok:
        for t in row:
            t.w = (self.rlsem.key, self.rlsem.count)


Prog.mixer = _mixer
Prog.reload_x = _reload_x


def _branch_C(self, l):
    m, tk, A2 = self.m, self.mt, self.A2
    P, Pt, Pb = self.P, self.Pt, self.Pb
    mark = A2.o
    rt = self.A0.f32(4 * 256)
    qkr = A2.b16(512)
    qkT = A2.b16(512)
    vt = A2.b16(512)
    ST = A2.b16(512)
    qdT = A2.b16(256)
    kend = A2.b16(256)
    Sf = A2.f32(256)
    Stmp = A2.f32(256)
    Sb = A2.b16(256)
    osq = A2.f32(512)
    sg = A2.f32(512)
    ot = A2.f32(512)
    oc = A2.b16(512)
    ssC = A2.f32(4)
    rstC = A2.f32(4)
    t_rt, t_qkr, t_qkT, t_vt, t_ST, t_qdT, t_kend, t_S, t_Stmp, t_Sb, t_osq, t_sg, t_ot, t_oc, t_ss = (Tok() for _ in range(15))
    slabs = self.w_take(6)
    rt4 = rt.rearrange("p (a h m) -> p a h m", a=4, h=8)
    qkr4 = qkr.rearrange("p (h m two) -> p h m two", h=8, two=2)
    for tt in range(16):
        tsl = slice(tt * 128, (tt + 1) * 128)
        for j in range(6):
            ap, wtk = slabs[j]
            w = ap.rearrange("p (k n) -> p k n", k=KC)
            bank = (0, 0, 1, 1, 6, 6)[j]
            dst = P[bank][:, (j % 2) * 256:(j % 2 + 1) * 256]
            for kc in range(KC):
                self.mm(dst, self.uT[:, kc, tsl], w[:, kc, :], kc == 0, kc == KC - 1, [wtk, self.utok[kc][tt // 4]], [Pt[bank]], full=True)
        v4 = P[0][:].rearrange("p (h m two) -> p h m two", h=8, two=2)
        xe, xo = v4[:, :, :, 0], v4[:, :, :, 1]
        cos = _mcf(self, "cosC", 512)[:, tt * 32:(tt + 1) * 32].unsqueeze(1).to_broadcast([128, 8, 32])
        sin = _mcf(self, "sinC", 512)[:, tt * 32:(tt + 1) * 32].unsqueeze(1).to_broadcast([128, 8, 32])
        self.tt(rt4[:, 0], xe, cos, ALU.mult, [Pt[0], tk["mc"]], [t_rt])
        self.tt(rt4[:, 1], xo, sin, ALU.mult, [Pt[0], tk["mc"]], [t_rt])
        self.tt(rt4[:, 2], xo, cos, ALU.mult, [Pt[0], tk["mc"]], [t_rt])
        self.tt(rt4[:, 3], xe, sin, ALU.mult, [Pt[0], tk["mc"]], [t_rt])
        self.tt(qkr4[:, :, :, 0], rt4[:, 0], rt4[:, 1], ALU.subtract, [t_rt], [t_qkr], eng="pool")
        self.tt(qkr4[:, :, :, 1], rt4[:, 2], rt4[:, 3], ALU.add, [t_rt], [t_qkr], eng="pool")
        self.copy(vt, P[1][:], [Pt[1]], [t_vt], eng="act")
        for i in range(4):
            self.tr(Pb[2][:, i * 128:(i + 1) * 128], qkr[:, i * 128:(i + 1) * 128], self.ident_b(), [t_qkr, self.t_cstb], [Pt[2]])
        self.copy(qkT, Pb[2][:, 0:512], [Pt[2]], [t_qkT], eng="act")
        qkT3 = qkT.rearrange("p (i t) -> p i t", i=4)
        for h in (0, 2, 1, 3):
            r0 = (h % 2) * 64
            self.mm(P[3][:, h * 128:(h + 1) * 128], qkT3[r0:r0 + 64, 2 + h // 2, :], qkT3[r0:r0 + 64, h // 2, :], True, True,
                    [t_qkT], [Pt[3]], tkey=("T", r0, 0))
        self.tt(ST, P[3][:], _mcf(self, "decC", 512), ALU.mult, [Pt[3], tk["mc"]], [t_ST])
        if tt > 0:
            self.tt(qdT, qkT[:, 0:256], _mcf(self, "gqC", 256), ALU.mult, [t_qkT, tk["mc"]], [t_qdT], eng="pool")
        qdT3 = qdT.rearrange("p (i t) -> p i t", i=2)
        Sb3 = Sb.rearrange("p (i v) -> p i v", i=2)
        for h in range(4):
            r0 = (h % 2) * 64
            self.mm(P[4][:, h * 128:(h + 1) * 128], ST[:, h * 128:(h + 1) * 128], vt[:, h * 128:(h + 1) * 128], True, tt == 0,
                    [t_ST, t_vt], [Pt[4]])
            if tt > 0:
                self.mm(P[4][:, h * 128:(h + 1) * 128], qdT3[r0:r0 + 64, h // 2, :], Sb3[r0:r0 + 64, h // 2, :], False, True,
                        [t_qdT, t_Sb], [Pt[4]])
        if tt < 15:
            self.tt(kend.rearrange("p (h d) -> p h d", h=4), qkr[:, 256:512].rearrange("p (h d) -> p h d", h=4),
                    _mcf(self, "kendC", 4).unsqueeze(2).to_broadcast([128, 4, 64]), ALU.mult, [t_qkr, tk["mc"]], [t_kend], eng="pool")
            for h in (0, 2, 1, 3):
                r0 = (h % 2) * 64
                self.mm(P[5][r0:r0 + 64, (h // 2) * 128:(h // 2 + 1) * 128], kend[:, h * 64:(h + 1) * 64],
                        vt[:, h * 128:(h + 1) * 128], True, True, [t_kend, t_vt], [Pt[5]], tkey=("T", 0, r0))
            if tt == 0:
                self.copy(Sf, P[5][:, 0:256], [Pt[5]], [t_S], eng="dve")
            else:
                self.tt(Stmp.rearrange("p (i v) -> p i v", i=2), Sf.rearrange("p (i v) -> p i v", i=2),
                        _mcf(self, "gSC", 2).unsqueeze(2).to_broadcast([128, 2, 128]), ALU.mult, [t_S, tk["mc"]], [t_Stmp], eng="pool")
                self.tt(Sf, Stmp, P[5][:, 0:256], ALU.add, [t_Stmp, Pt[5]], [t_S])
            self.copy(Sb, Sf, [t_S], [t_Sb], eng="act")
        self.act(osq, P[4][:], AF.Square, [Pt[4]], [t_osq])
        self.sc.op("dve", lambda e: e.tensor_reduce(out=ssC, in_=osq.rearrange("p (h d) -> p h d", h=4), axis=AX.X, op=ALU.add),
                   [t_osq], [t_ss])
        self.act(rstC, ssC, AF.Sqrt, [t_ss], [t_ss], bias=EPS, scale=1.0 / 128)
        self.recip(rstC, rstC, [t_ss], [t_ss])
        self.act(sg, P[6][:], AF.Silu, [Pt[6]], [t_sg])
        self.tt(ot.rearrange("p (h d) -> p h d", h=4), P[4][:].rearrange("p (h d) -> p h d", h=4),
                rstC.unsqueeze(2).to_broadcast([128, 4, 128]), ALU.mult, [Pt[4], t_ss], [t_ot])
        self.tt(oc, ot, sg, ALU.mult, [t_ot, t_sg], [t_oc], eng="pool")
        _emit_out(self, 2, tt, oc, t_oc, bank=2)
        yield
    self.w_release(6)


Prog.branch_C = _branch_C


def _branch_B(self, l):
    m, tk = self.m, self.mt
    P, Pt, Pb = self.P, self.Pt, self.Pb
    A0, A2 = self.A0, self.A2
    mark0, mark2 = A0.o, A2.o
    sc = self.sc

    def bc_h(ap8, n):
        return ap8.unsqueeze(2).to_broadcast([128, 8, n])

    def bc_m(ap, h):
        return ap.unsqueeze(1).to_broadcast([128, h, ap.shape[1]])

    cw = A2.b16(4 * 1536)
    xc = [A2.b16(1536), A2.b16(1536)]
    acc = A2.f32(1536)
    ctmp = [A2.f32(512), A2.f32(512)]
    zs = A2.f32(512)
    Vb = A2.f32(512)
    A3 = Arena(self.R3[:], self.R3[:].bitcast(BF16), 8192)
    qkf = A0.f32(1024)
    Nm = qkf
    sqf = A0.b16(1024)
    IT = sqf
    Kn, Qn, Kb, Kbe, Kend, Qd = (A0.b16(512) for _ in range(6))
    featT = A0.b16(2560)
    decT = A0.f32(1024)
    tmpW = A0.f32(1024)
    tW = [Tok(), Tok()]
    tY = [Tok(), Tok()]
    X, XT = A0.f32(1024), A0.f32(1024)
    Mm, Ym = A3.f32(1024), A3.f32(1024)
    zz = A0.f32(512)
    vnew = A0.b16(512)
    Sf = A0.f32(256)
    Sb = A0.b16(256)
    ob = A0.b16(512)
    sm = A0.f32(16 * 12)
    xa, ax, ee, lp, alog, beta, gcum, eg, kdec, tmp8 = (sm[:, i * 8:(i + 1) * 8] for i in range(10))
    decS = sm[:, 80:96]
    ss16 = sm[:, 96:112]
    rst16 = sm[:, 112:128]
    ss8 = sm[:, 128:136]
    rst8 = sm[:, 136:144]
    rhsU = acc[:, 0:1024]
    tmpWT = acc[:, 0:1024]
    osq, ot = ctmp[0], ctmp[1]
    T = {k: Tok(k) for k in ("cw", "zs", "qkf", "sqf", "Kn", "Qn", "Kb", "Kbe", "Kend", "Vb", "Qd", "featT", "decT", "M",
                             "X", "XT", "Ym", "zz", "vnew", "S", "Sb", "ob", "gate", "g2", "nrm", "nrm8")}
    T["N"] = T["qkf"]
    T["IT"] = T["sqf"]
    t_xc = [Tok(), Tok()]
    t_acc = [Tok(), Tok(), Tok()]
    t_ct = [Tok(), Tok()]
    for j in range(4):
        sc.dma("pool", self.cwsem, cw[:, j * 1536:(j + 1) * 1536], self.cw_d[:, l * 6144 + j * 1536:l * 6144 + (j + 1) * 1536],
               writes=[T["cw"]])
    T["cw"].w = (self.cwsem.key, self.cwsem.count)
    slabs = self.w_take(9)
    ident8 = bc_m(self.ident_f(), 8)
    Kn3, Qn3, Kb3, Kbe3, Kend3, Vb3, Qd3 = (a.rearrange("p (h d) -> p h d", h=8) for a in (Kn, Qn, Kb, Kbe, Kend, Vb, Qd))
    F4 = featT.rearrange("p (k i t) -> p k i t", k=5, i=4)
    N3, M3, X3, XT3, Y3, IT3 = (a.rearrange("p (h t) -> p h t", h=8) for a in (Nm, Mm, X, XT, Ym, IT))
    dec3 = decT.rearrange("p (h t) -> p h t", h=8)
    first = True
    self.bdbg = dict(cw=cw, acc=acc, qkf=qkf, Kn=Kn, Qn=Qn, Kb=Kb, Kbe=Kbe, Kend=Kend, Vb=Vb, Qd=Qd, featT=featT, decT=decT, N=Nm, M=Mm,
                     X=X, XT=XT, Ym=Ym, IT=IT, zz=zz, vnew=vnew, Sf=Sf, Sb=Sb, sm=sm, zs=zs, xc0=xc[0])
    for tt in range(16):
        tsl = slice(tt * 128, (tt + 1) * 128)
        par = tt % 2
        for j in range(9):
            ap, wtk = slabs[j]
            w = ap.rearrange("p (k n) -> p k n", k=KC)
            bank = (0, 0, 1, 1, 2, 2, 3, 3, 4)[j]
            dst = P[bank][:, (j % 2) * 256:(j % 2 + 1) * 256]
            for kc in range(KC):
                self.mm(dst, self.uT[:, kc, tsl], w[:, kc, :], kc == 0, kc == KC - 1, [wtk, self.utok[kc][tt // 4]], [Pt[bank]], full=True)
        self.act(zs, P[3][:], AF.Silu, [Pt[3]], [T["zs"]])
        self.tt(xa, P[4][:, 0:8], _vec(self, "dtb", 8), ALU.add, [Pt[4], tk["mc"]], [T["gate"]])
        self.act(ax, xa, AF.Abs, [T["gate"]], [T["gate"]])
        self.act(ee, ax, AF.Exp, [T["gate"]], [T["gate"]], scale=-1.0)
        self.act(lp, ee, AF.Ln, [T["gate"]], [T["gate"]], bias=1.0)
        self.ts(xa, xa, 0.0, None, ALU.max, None, [T["gate"]], [T["gate"]])
        self.tt(xa, xa, lp, ALU.add, [T["gate"]], [T["gate"]])
        self.tt(alog, xa, m["nA"], ALU.mult, [T["gate"], tk["der"]], [T["gate"]])
        self.act(beta, P[4][:, 8:16], AF.Sigmoid, [Pt[4]], [T["gate"]])
        for ct in range(3):
            self.copy(xc[par][:, ct * 512:(ct + 1) * 512], P[ct][:], [Pt[ct]], [t_xc[par]], eng="act")
        for ct in range(3):
            csl = slice(ct * 512, (ct + 1) * 512)
            for s_ in (1, 2, 3):
                bank = 4 + s_
                self.mm(P[bank][:], _mcb(self, "shc", 128, (s_ - 1) * 128), xc[par][:, csl], True, tt == 0,
                        [tk["mcb"], t_xc[par]], [Pt[bank]], full=True)
                if tt > 0:
                    self.mm(P[bank][:], _mcb(self, "shp", 128, (s_ - 1) * 128), xc[1 - par][:, csl], False, True,
                            [tk["mcb"], t_xc[1 - par]], [Pt[bank]], full=True)
            self.tt(acc[:, csl], P[ct][:], cw[:, 3 * 1536 + ct * 512:3 * 1536 + (ct + 1) * 512], ALU.mult,
                    [Pt[ct], T["cw"]], [t_acc[ct]])
            for s_ in (1, 2, 3):
                b_ = s_ % 2
                self.tt(ctmp[b_], P[4 + s_][:], cw[:, (3 - s_) * 1536 + ct * 512:(3 - s_) * 1536 + (ct + 1) * 512], ALU.mult,
                        [Pt[4 + s_], T["cw"]], [t_ct[b_]])
                self.tt(acc[:, csl], acc[:, csl], ctmp[b_], ALU.add, [t_acc[ct], t_ct[b_]], [t_acc[ct]], eng="pool")
        self.act(qkf, acc[:, 0:1024], AF.Silu, [t_acc[0], t_acc[1]], [T["qkf"]])
        self.act(ctmp[0], acc[:, 1024:1536], AF.Silu, [t_acc[2]], [t_ct[0]])
        self.tt(Vb3, ctmp[0].rearrange("p (h d) -> p h d", h=8), bc_h(beta, 64), ALU.mult, [t_ct[0], T["gate"]], [T["Vb"]])
        self.act(sqf, qkf, AF.Square, [T["qkf"]], [T["sqf"]])
        sc.op("dve", lambda e: e.tensor_reduce(out=ss16, in_=sqf.rearrange("p (h d) -> p h d", h=16), axis=AX.X, op=ALU.add),
              [T["sqf"]], [T["nrm"]])
        self.act(rst16, ss16, AF.Sqrt, [T["nrm"]], [T["nrm"]], bias=EPS)
        self.recip(rst16, rst16, [T["nrm"]], [T["nrm"]])
        self.ts(rst16[:, 0:8], rst16[:, 0:8], 0.125, None, ALU.mult, None, [T["nrm"]], [T["nrm"]])
        self.tt(Qn3, qkf[:, 0:512].rearrange("p (h d) -> p h d", h=8), bc_h(rst16[:, 0:8], 64), ALU.mult, [T["qkf"], T["nrm"]], [T["Qn"]])
        self.tt(Kn3, qkf[:, 512:1024].rearrange("p (h d) -> p h d", h=8), bc_h(rst16[:, 8:16], 64), ALU.mult,
                [T["qkf"], T["nrm"]], [T["Kn"]])
        self.mm(P[5][:, 0:8], _mcf(self, "ublk", 128), alog, True, True, [tk["mc"], T["gate"]], [Pt[5]])
        self.mm(P[5][:, 8:16], _mcf(self, "blk", 128), alog, True, True, [tk["mc"], T["gate"]], [Pt[5]])
        self.mm(P[5][:, 16:24], _mcf(self, "ind", 128), alog, True, True, [tk["mc"], T["gate"]], [Pt[5]])
        self.mm(P[5][:, 24:32], _mcf(self, "ind", 256)[:, 128:256], alog, True, True, [tk["mc"], T["gate"]], [Pt[5]])
        self.copy(gcum, P[5][:, 0:8], [Pt[5]], [T["g2"]])
        self.act(eg, P[5][:, 0:8], AF.Exp, [Pt[5]], [T["g2"]])
        self.tt(tmp8, P[5][:, 8:16], gcum, ALU.subtract, [Pt[5], T["g2"]], [T["g2"]])
        self.act(kdec, tmp8, AF.Exp, [T["g2"]], [T["g2"]])
        self.act(decS, P[5][:, 16:32], AF.Exp, [Pt[5]], [T["g2"]])
        self.tt(rhsU.rearrange("p (h t) -> p h t", h=8), bc_m(_mcf(self, "ublk", 128), 8), bc_h(alog, 128), ALU.mult,
                [tk["mc"], T["gate"], t_acc[0], t_acc[1]], [t_acc[0], t_acc[1]])
        for hh in range(2):
            self.mm(P[6 + hh][:], _mcf(self, "ones", 128), rhsU[:, hh * 512:(hh + 1) * 512], True, True,
                    [tk["mc"], t_acc[0], t_acc[1]], [Pt[6 + hh]], full=True)
        for h in range(8):
            self.stt(dec3[:, h, :], P[6 + h // 4][:, (h % 4) * 128:(h % 4 + 1) * 128], gcum[:, h:h + 1], _mcf(self, "mbias", 128),
                     ALU.subtract, ALU.add, [Pt[6 + h // 4], T["g2"], tk["mc"]], [T["decT"]])
        self.act(decT, decT, AF.Exp, [T["decT"]], [T["decT"]])
        self.tt(Kb3, Kn3, bc_h(beta, 64), ALU.mult, [T["Kn"], T["gate"]], [T["Kb"]], eng="pool")
        self.tt(Kbe3, Kb3, bc_h(eg, 64), ALU.mult, [T["Kb"], T["g2"]], [T["Kbe"]], eng="pool")
        self.tt(Kend3, Kn3, bc_h(kdec, 64), ALU.mult, [T["Kn"], T["g2"]], [T["Kend"]], eng="pool")
        self.tt(Qd3, Qn3, bc_h(eg, 64), ALU.mult, [T["Qn"], T["g2"]], [T["Qd"]], eng="pool")
        for ki, (arr, tkn) in enumerate(((Kn, "Kn"), (Kb, "Kb"), (Qn, "Qn"), (Qd, "Qd"), (Kbe, "Kbe"))):
            for pr in range(4):
                self.tr(Pb[ki][:, pr * 128:(pr + 1) * 128], arr[:, pr * 128:(pr + 1) * 128], self.ident_b(),
                        [T[tkn], self.t_cstb], [Pt[ki]])
        for ki in range(5):
            self.copy(featT[:, ki * 512:(ki + 1) * 512], Pb[ki][:, 0:512], [Pt[ki]], [T["featT"]], eng=("act" if ki % 2 == 0 else "dve"))
        for h in HORD:
            r0, pr = (h % 2) * 64, h // 2
            self.mm(P[2 + h // 4][:, (h % 4) * 128:(h % 4 + 1) * 128], F4[r0:r0 + 64, 0, pr, :], F4[r0:r0 + 64, 1, pr, :], True, True,
                    [T["featT"]], [Pt[2 + h // 4]], tkey=("T", r0, 0))
        for h in HORD:
            r0, pr = (h % 2) * 64, h // 2
            self.mm(P[4 + h // 4][:, (h % 4) * 128:(h % 4 + 1) * 128], F4[r0:r0 + 64, 0, pr, :], F4[r0:r0 + 64, 2, pr, :], True, True,
                    [T["featT"]], [Pt[4 + h // 4]], tkey=("T", r0, 0))
        for hh in range(2):
            hs = slice(hh * 512, (hh + 1) * 512)
            self.tt(Nm[:, hs], P[2 + hh][:], decT[:, hs], ALU.mult, [Pt[2 + hh], T["decT"]], [T["N"]])
            self.tt(IT[:, hs], P[4 + hh][:], decT[:, hs], ALU.mult, [Pt[4 + hh], T["decT"]], [T["IT"]])
        for h in range(8):
            self.tr(P[6 + h // 4][:, (h % 4) * 128:(h % 4 + 1) * 128], N3[:, h, :], self.ident_f(), [T["N"], self.t_cst], [Pt[6 + h // 4]])
        self.copy(Mm[:, 0:512], P[6][:], [Pt[6]], [T["M"]], eng="act")
        self.copy(Mm[:, 512:1024], P[7][:], [Pt[7]], [T["M"]], eng="act")
        self.tt(X3, N3, bc_m(_mcb(self, "nm", 128, 0), 8), ALU.mult, [T["N"], tk["mcb"]], [T["X"]], eng="pool")
        self.tt(X3, X3, ident8, ALU.add, [T["X"], self.t_cst], [T["X"]], eng="pool")
        self.tt(XT3, M3, bc_m(_mcb(self, "nmT", 128, 0), 8), ALU.mult, [T["M"], tk["mcb"]], [T["XT"]], eng="pool")
        self.tt(XT3, XT3, ident8, ALU.add, [T["XT"], self.t_cst], [T["XT"]], eng="pool")
        for li in range(1, 6):
            last = li == 5
            for h in range(8):
                self.mm(P[h // 4][:, (h % 4) * 128:(h % 4 + 1) * 128], M3[:, h, :], X3[:, h, :], True, True, [T["M"], T["X"]], [Pt[h // 4]], full=True)
            self.copy(Ym[:, 0:512], P[0][:], [Pt[0]], [tY[0]], eng="act")
            self.copy(Ym[:, 512:1024], P[1][:], [Pt[1]], [tY[1]], eng="act")
            for h in range(8):
                self.mm(P[2 + h // 4][:, (h % 4) * 128:(h % 4 + 1) * 128], XT3[:, h, :], Y3[:, h, :], True, True,
                        [T["XT"], tY[h // 4]], [Pt[2 + h // 4]], full=True)
            if not last:
                for h in range(8):
                    self.mm(P[4 + h // 4][:, (h % 4) * 128:(h % 4 + 1) * 128], Y3[:, h, :], XT3[:, h, :], True, True,
                            [T["XT"], tY[h // 4]], [Pt[4 + h // 4]], full=True)
            for hh in range(2):
                hs = slice(hh * 512, (hh + 1) * 512)
                self.tt(tmpW[:, hs].rearrange("p (h t) -> p h t", h=4), P[2 + hh][:].rearrange("p (h t) -> p h t", h=4),
                        bc_m(_mcb(self, "nm", 128, li * 128), 4), ALU.mult, [Pt[2 + hh], tk["mcb"], tW[hh]], [tW[hh]])
                self.tt(X[:, hs], X[:, hs], tmpW[:, hs], ALU.add, [T["X"], tW[hh]], [T["X"]], eng="pool")
            if not last:
                for hh in range(2):
                    hs = slice(hh * 512, (hh + 1) * 512)
                    self.tt(tmpWT[:, hs].rearrange("p (h t) -> p h t", h=4), P[4 + hh][:].rearrange("p (h t) -> p h t", h=4),
                            bc_m(_mcb(self, "nmT", 128, li * 128), 4), ALU.mult, [Pt[4 + hh], tk["mcb"], t_acc[hh]],
                            [t_acc[hh]])
                    self.tt(XT[:, hs], XT[:, hs], tmpWT[:, hs], ALU.add, [T["XT"], t_acc[hh]], [T["XT"]], eng="pool")
        for c in range(2):
            cs = slice(c * 64, (c + 1) * 64)
            lastc = (tt == 15 and c == 1)
            if not first:
                for h in HORD:
                    r0, pr = (h % 2) * 64, h // 2
                    self.mm(P[0][cs, h * 64:(h + 1) * 64], F4[r0:r0 + 64, 4, pr, cs], Sb[r0:r0 + 64, pr * 64:(pr + 1) * 64], True, True,
                            [T["featT"], T["Sb"]], [Pt[0]], tkey=("T", r0, c * 64))
                self.tt(zz[cs, :], Vb[cs, :], P[0][cs, :], ALU.subtract, [T["Vb"], Pt[0]], [T["zz"]])
            else:
                self.copy(zz[cs, :], Vb[cs, :], [T["Vb"]], [T["zz"]])
            for h in range(8):
                self.mm(P[6][cs, h * 64:(h + 1) * 64], X3[cs, h, cs], zz[cs, h * 64:(h + 1) * 64], True, True, [T["X"], T["zz"]], [Pt[6]],
                        tkey=("T", c * 64, c * 64))
            self.copy(vnew[cs, :], P[6][cs, :], [Pt[6]], [T["vnew"]], eng="act")
            for h in range(8):
                r0, pr = (h % 2) * 64, h // 2
                if not first:
                    self.mm(P[1][cs, h * 64:(h + 1) * 64], F4[r0:r0 + 64, 3, pr, cs], Sb[r0:r0 + 64, pr * 64:(pr + 1) * 64], True, False,
                            [T["featT"], T["Sb"]], [Pt[1]])
                self.mm(P[1][cs, h * 64:(h + 1) * 64], IT3[cs, h, cs], vnew[cs, h * 64:(h + 1) * 64], first, True,
                        [T["IT"], T["vnew"]], [Pt[1]])
            if not lastc:
                for h in HORD:
                    r0, pr = (h % 2) * 64, h // 2
                    self.mm(P[2][r0:r0 + 64, pr * 64:(pr + 1) * 64], Kend[cs, h * 64:(h + 1) * 64], vnew[cs, h * 64:(h + 1) * 64],
                            True, True, [T["Kend"], T["vnew"]], [Pt[2]], tkey=("T", c * 64, r0))
                if first:
                    self.copy(Sf, P[2][:, 0:256], [Pt[2]], [T["S"]])
                else:
                    S3 = Sf.rearrange("p (i v) -> p i v", i=4)
                    dS = decS[:, c * 8:(c + 1) * 8]
                    for half in range(2):
                        ps_ = slice(half * 64, (half + 1) * 64)
                        self.tt(S3[ps_], S3[ps_], dS[ps_, half::2].unsqueeze(2).to_broadcast([64, 4, 64]), ALU.mult,
                                [T["S"], T["g2"]], [T["S"]], eng="pool")
                    self.tt(Sf, Sf, P[2][:, 0:256], ALU.add, [T["S"], Pt[2]], [T["S"]])
                self.copy(Sb, Sf, [T["S"]], [T["Sb"]], eng="act")
            first = False
        self.act(osq, P[1][:], AF.Square, [Pt[1]], [t_ct[0]])
        sc.op("dve", lambda e: e.tensor_reduce(out=ss8, in_=osq.rearrange("p (h d) -> p h d", h=8), axis=AX.X, op=ALU.add),
              [t_ct[0]], [T["nrm8"]])
        self.act(rst8, ss8, AF.Sqrt, [T["nrm8"]], [T["nrm8"]], bias=EPS, scale=1.0 / 64)
        self.recip(rst8, rst8, [T["nrm8"]], [T["nrm8"]])
        ot3 = ot.rearrange("p (h d) -> p h d", h=8)
        self.tt(ot3, P[1][:].rearrange("p (h d) -> p h d", h=8), bc_h(rst8, 64), ALU.mult, [Pt[1], T["nrm8"]], [t_ct[1]])
        self.tt(ot3, ot3, bc_m(_vec(self, "gon", 64), 8), ALU.mult, [t_ct[1], tk["mc"]], [t_ct[1]])
        self.tt(ob, ot, zs, ALU.mult, [t_ct[1], T["zs"]], [T["ob"]], eng="pool")
        _emit_out(self, 1, tt, ob, T["ob"])
    self.w_release(9)
    A0.o, A2.o = mark0, mark2


Prog.branch_B = _branch_B


def _merge(self, l):
    sc = self.sc
    P, Pt = self.P, self.Pt
    sc.barrier()
    R0b = self.R0[:].bitcast(BF16)
    outT = R0b[:, 0:12 * S].rearrange("p (j t) -> p j t", j=12)
    otok = [Tok() for _ in range(12)]
    for j in range(12):
        sc.dma("sp", self.olsem, outT[:, j, :], self.osp_d[j], writes=[otok[j]])
    for t in otok:
        t.w = (self.olsem.key, self.olsem.count)
    macc = self.R0[:, 12 * S // 2:12 * S // 2 + 2048].rearrange("p (t n) -> p t n", t=4)
    sgm = self.R0[:, 12 * S // 2 + 2048:12 * S // 2 + 3072].rearrange("p (t n) -> p t n", t=2)
    tmpm = self.R0[:, 12 * S // 2 + 3072:12 * S // 2 + 4096].rearrange("p (t n) -> p t n", t=2)
    t_macc = [Tok() for _ in range(4)]
    t_sgm = [Tok(), Tok()]
    t_tmpm = [Tok(), Tok()]
    mergedT = self.actT
    mtok = self.atok
    n = 0
    for m in range(KC):
        for g in range(3):
            (ap, wtk), = self.w_take(1)
            wg = ap[:, 0:1024].rearrange("p (k n) -> p k n", k=KC)
            wbm = ap[:, 1024:1536].rearrange("p (k n) -> p k n", k=4)
            for t in range(4):
                tsl = slice(t * 512, (t + 1) * 512)
                pg, pb, b = n % 2, 2 + n % 2, n % 2
                n += 1
                for kc in range(KC):
                    self.mm(P[pg][:], wg[:, kc, :], self.uT[:, kc, tsl], kc == 0, kc == KC - 1, [wtk, self.utok[kc][t]], [Pt[pg]], full=True)
                for k in range(4):
                    self.mm(P[pb][:], wbm[:, k, :], outT[:, g * 4 + k, tsl], k == 0, k == 3, [wtk, otok[g * 4 + k]], [Pt[pb]], full=True)
                self.act(sgm[:, b, :], P[pg][:], AF.Sigmoid, [Pt[pg]], [t_sgm[b]])
                if g == 0:
                    self.tt(macc[:, t, :], sgm[:, b, :], P[pb][:], ALU.mult, [t_sgm[b], Pt[pb]], [t_macc[t]])
                else:
                    self.tt(tmpm[:, b, :], sgm[:, b, :], P[pb][:], ALU.mult, [t_sgm[b], Pt[pb]], [t_tmpm[b]])
                    if g == 1:
                        self.tt(macc[:, t, :], macc[:, t, :], tmpm[:, b, :], ALU.add, [t_macc[t], t_tmpm[b]], [t_macc[t]], eng="pool")
                    else:
                        self.tt(mergedT[:, m, tsl], macc[:, t, :], tmpm[:, b, :], ALU.add, [t_macc[t], t_tmpm[b]], [mtok[m][t]],
                                eng="pool")
            self.w_release(1)
    self.reload_x()
    for mp in range(0, KC, 2):
        (ap, wtk), = self.w_take(1)
        wo = ap.rearrange("p (i k n) -> p i k n", i=2, k=KC)
        for i in range(2):
            mo = mp + i
            for t in range(4):
                tsl = slice(t * 512, (t + 1) * 512)
                py = 4 + (mo * 4 + t) % 2
                for kc in range(KC):
                    self.mm(P[py][:], wo[:, i, kc, :], mergedT[:, kc, tsl], kc == 0, kc == KC - 1, [wtk, mtok[kc][t]], [Pt[py]], full=True)
                xt = [self.xtok[mo][4 * t + q] for q in range(4)]
                self.stt(self.xT[:, mo, tsl], P[py][:], self.cf(l, 1, 2, mo), self.xT[:, mo, tsl], ALU.mult, ALU.add,
                         [Pt[py], self.t_coef] + xt, xt)
        self.w_release(1)


Prog.merge = _merge
```
